# Optimizing a Trainium2 kernel written in Bass

```python
import jax, jax.numpy as jnp
from jax import lax
import numpy as np

D_MODEL = 1024
BATCH = 1
SEQ = 16384
DEPTH = 4

GRID_W = 64
CTX_LEN = 256
N_MIXERS = 3
N_MOD = 9
D_FF = 2816
FFN_RES = 0.5
NORM_EPS = 1e-6

RW_HEAD = 64
RW_HEADS = D_MODEL // RW_HEAD
RW_DECAY_LORA = 64
RW_ICLR_LORA = 64
RW_VALUE_LORA = 32
RW_GATE_LORA = 160
RW_GN_EPS = 64e-5

MLA_HEADS = 16
MLA_NOPE = 64
MLA_ROPE = 32
MLA_V = 64
MLA_Q_RANK = 384
MLA_KV_RANK = 256
ROPE_BASE = 10000.0
Q_BLOCK = 128

POOL_WINDOWS = (2, 4, 8, 16)
POOL_GROUP = D_MODEL // len(POOL_WINDOWS)

N_RWKV = (DEPTH + 2) // 3
N_RWKV_VRES = N_RWKV - 1
N_MLA = (DEPTH + 1) // 3
N_POOL = DEPTH // 3

kernel_name = 'hybrid_rwkv7_mla_pool_dit_trunk'


def _rms(x, g, eps=NORM_EPS):
    xf = x.astype(jnp.float32)
    y = xf * lax.rsqrt(jnp.mean(xf * xf, axis=-1, keepdims=True) + eps)
    return (y * g.astype(jnp.float32)).astype(x.dtype)


def _swiglu(h, wg, wu, wd):
    return (jax.nn.silu(h @ wg) * (h @ wu)) @ wd


def _pre(s, mod, slot, g_pre):
    return _rms(s, g_pre) * (1 + mod[:, 3 * slot + 1]) + mod[:, 3 * slot]


def _post(s, y, mod, slot, g_post, weight):
    return s + weight * mod[:, 3 * slot + 2] * _rms(y, g_post)


def _ffn_sublayer(s, mod, slot, g_pre, g_post, wg, wu, wd):
    return _post(s, _swiglu(_pre(s, mod, slot, g_pre), wg, wu, wd), mod, slot, g_post, FFN_RES)


def _axial_angles(rows):
    row = jnp.repeat(jnp.arange(rows, dtype=jnp.float32), GRID_W)
    col = jnp.tile(jnp.arange(GRID_W, dtype=jnp.float32), rows)
    axis_dim = MLA_ROPE // 2
    inv = ROPE_BASE ** (-jnp.arange(0, axis_dim, 2, dtype=jnp.float32) / axis_dim)
    return row[:, None] * inv, col[:, None] * inv


def _rotate_half(x, ang):
    shape = (1, ang.shape[0]) + (1,) * (x.ndim - 3) + (ang.shape[1],)
    cos = jnp.cos(ang).reshape(shape)
    sin = jnp.sin(ang).reshape(shape)
    x1, x2 = jnp.split(x.astype(jnp.float32), 2, axis=-1)
    return jnp.concatenate([x1 * cos - x2 * sin, x1 * sin + x2 * cos], axis=-1).astype(x.dtype)


def _axial_rope(x, ang_row, ang_col):
    half = x.shape[-1] // 2
    return jnp.concatenate([_rotate_half(x[..., :half], ang_row),
                            _rotate_half(x[..., half:], ang_col)], axis=-1)


def _heads(t):
    return t.reshape(t.shape[:-1] + (RW_HEADS, RW_HEAD))


def _rwkv_stream(h, p, v_first, readout):
    B, T, D = h.shape
    zero = jnp.zeros((B, 1, D), h.dtype)
    prev = jnp.concatenate([zero, h[:, :-1]], axis=1)
    nxt = jnp.concatenate([h[:, 1:], zero], axis=1)
    xx = 0.5 * (prev + nxt) - h
    mu = p['mu']
    xw, xk, xv, xa = (h + xx * mu[n] for n in (1, 2, 3, 4))
    k = xk @ p['w_k']
    v = xv @ p['w_v']
    if v_first is None:
        v_first = v
    else:
        v = v + (v_first - v) * jax.nn.sigmoid(p['v0'] + (xv @ p['v1']) @ p['v2'])
    kkf = _heads(k * p['k_k']).astype(jnp.float32)
    kk = kkf * lax.rsqrt(jnp.maximum(jnp.sum(kkf * kkf, -1, keepdims=True), 1e-24))
    dirs = []
    for d in range(2):
        lw = (p['w0'][d] + jnp.tanh(xw @ p['w1'][d]) @ p['w2'][d]).astype(jnp.float32)
        decay = jnp.exp(-jnp.exp(-jax.nn.softplus(-lw) - 0.5))
        a = jax.nn.sigmoid(p['a0'][d] + (xa @ p['a1'][d]) @ p['a2'][d])
        k_d = k * (1 + (a - 1) * p['k_a'])
        dirs.append((_heads(decay), _heads(a).astype(jnp.float32), _heads(k_d).astype(jnp.float32)))
    st = {'v': _heads(v).astype(jnp.float32), 'kk': kk, 'dirs': dirs}
    if readout:
        xr = h + xx * mu[0]
        xg = h + xx * mu[5]
        st['r'] = _heads(xr @ p['w_r']).astype(jnp.float32)
        st['g'] = jax.nn.sigmoid(xg @ p['g1']) @ p['g2']
    return st, v_first


def _wkv_scan(s0, decay, k, v, kk, a, r, reverse):
    emit = r is not None
    seq = (decay, k, v, kk, a) + ((r,) if emit else ())
    xs = tuple(jnp.moveaxis(t, 1, 0) for t in seq)

    def step(s, inp):
        w_t, k_t, v_t, kk_t, a_t = inp[:5]
        s_kk = jnp.einsum('bhvk,bhk->bhv', s, kk_t)
        s = (s * w_t[:, :, None, :]
             - s_kk[..., None] * (kk_t * a_t)[:, :, None, :]
             + v_t[..., None] * k_t[:, :, None, :])
        return s, (jnp.einsum('bhvk,bhk->bhv', s, inp[5]) if emit else None)

    s_fin, ys = lax.scan(step, s0, xs, reverse=reverse)
    return s_fin, (jnp.moveaxis(ys, 0, 1) if emit else None)


def _rwkv_readout(st, ys, p, out_dtype):
    r, v = st['r'], st['v']
    B, T = r.shape[:2]
    ln_w = p['ln_w'].astype(jnp.float32)
    ln_b = p['ln_b'].astype(jnp.float32)
    r_k = p['r_k'].astype(jnp.float32)
    o = jnp.zeros((B, T, D_MODEL), jnp.float32)
    for (_, _, k_d), y in zip(st['dirs'], ys):
        mu = jnp.mean(y, -1, keepdims=True)
        var = jnp.mean(jnp.square(y - mu), -1, keepdims=True)
        yn = ((y - mu) * lax.rsqrt(var + RW_GN_EPS)).reshape(B, T, D_MODEL) * ln_w + ln_b
        bonus = (jnp.sum(r * k_d * r_k, -1, keepdims=True) * v).reshape(B, T, D_MODEL)
        o = o + yn + bonus
    return (o.astype(out_dtype) * st['g']) @ p['w_o']


def _rwkv_mixer(hc, hx, vf_c, vf_x, p, need_ctx):
    st_c, vf_c = _rwkv_stream(hc, p, vf_c, need_ctx)
    st_x, vf_x = _rwkv_stream(hx, p, vf_x, True)
    B = hx.shape[0]
    ys_c, ys_x = [], []
    for d, reverse in enumerate((False, True)):
        s0 = jnp.zeros((B, RW_HEADS, RW_HEAD, RW_HEAD), jnp.float32)
        dc, ac, kc = st_c['dirs'][d]
        s_ctx, y_c = _wkv_scan(s0, dc, kc, st_c['v'], st_c['kk'], ac, st_c.get('r'), reverse)
        dx, ax, kx = st_x['dirs'][d]
        _, y_x = _wkv_scan(s_ctx, dx, kx, st_x['v'], st_x['kk'], ax, st_x['r'], reverse)
        ys_c.append(y_c)
        ys_x.append(y_x)
    yx = _rwkv_readout(st_x, ys_x, p, hx.dtype)
    yc = _rwkv_readout(st_c, ys_c, p, hc.dtype) if need_ctx else None
    return yc, yx, vf_c, vf_x


def _mla_project(h, w_dq, q_norm, w_uq, w_dkv, kv_norm, w_ukv):
    B, T, _ = h.shape
    q = (_rms(h @ w_dq, q_norm) @ w_uq).reshape(B, T, MLA_HEADS, MLA_NOPE + MLA_ROPE)
    ckv = h @ w_dkv
    kv = (_rms(ckv[..., :MLA_KV_RANK], kv_norm) @ w_ukv).reshape(B, T, MLA_HEADS, MLA_NOPE + MLA_V)
    return (q[..., :MLA_NOPE], q[..., MLA_NOPE:], kv[..., :MLA_NOPE],
            ckv[..., MLA_KV_RANK:], kv[..., MLA_NOPE:])


def _attend(q_nope, q_rope, k_nope, k_rope, v):
    scale = (MLA_NOPE + MLA_ROPE) ** -0.5
    s = (jnp.einsum('bqhd,bkhd->bhqk', q_nope, k_nope)
         + jnp.einsum('bqhr,bkr->bhqk', q_rope, k_rope))
    p = jax.nn.softmax(s.astype(jnp.float32) * scale, axis=-1).astype(v.dtype)
    return jnp.einsum('bhqk,bkhd->bqhd', p, v)


def _blocked_attend(q_nope, q_rope, k_nope, k_rope, v):
    B, T = q_nope.shape[:2]
    nblk = T // Q_BLOCK

    def to_blocks(t):
        return jnp.moveaxis(t.reshape((B, nblk, Q_BLOCK) + t.shape[2:]), 1, 0)

    o = lax.map(lambda qs: _attend(qs[0], qs[1], k_nope, k_rope, v),
                (to_blocks(q_nope), to_blocks(q_rope)))
    return jnp.moveaxis(o, 0, 1).reshape((B, T) + o.shape[3:])


def _mla_mixer(hc, hx, ang_row, ang_col, w_dq, q_norm, w_uq, w_dkv, kv_norm, w_ukv, w_o, need_ctx):
    qn_c, qr_c, kn_c, kr_c, v_c = _mla_project(hc, w_dq, q_norm, w_uq, w_dkv, kv_norm, w_ukv)
    qn_x, qr_x, kn_x, kr_x, v_x = _mla_project(hx, w_dq, q_norm, w_uq, w_dkv, kv_norm, w_ukv)
    qr_x = _axial_rope(qr_x, ang_row, ang_col)
    kr_x = _axial_rope(kr_x, ang_row, ang_col)
    kn = jnp.concatenate([kn_c, kn_x], axis=1)
    kr = jnp.concatenate([kr_c, kr_x], axis=1)
    vv = jnp.concatenate([v_c, v_x], axis=1)
    B, T, _ = hx.shape
    yx = _blocked_attend(qn_x, qr_x, kn, kr, vv).reshape(B, T, MLA_HEADS * MLA_V) @ w_o
    yc = None
    if need_ctx:
        yc = _attend(qn_c, qr_c, kn_c, kr_c, v_c).reshape(B, hc.shape[1], MLA_HEADS * MLA_V) @ w_o
    return yc, yx


def _pool_mixer(h, w, b, scale):
    B, T, D = h.shape
    hf = h.astype(jnp.float32)
    csum = jnp.concatenate([jnp.zeros((B, 1, D), jnp.float32), jnp.cumsum(hf, axis=1)], axis=1)
    t = jnp.arange(T)
    outs = []
    for gi, win in enumerate(POOL_WINDOWS):
        lo = jnp.clip(t - win // 2, 0, T)
        hi = jnp.clip(t + win // 2, 0, T)
        sl = slice(gi * POOL_GROUP, (gi + 1) * POOL_GROUP)
        cg = csum[..., sl]
        mean = (jnp.take(cg, hi, axis=1) - jnp.take(cg, lo, axis=1)) / (hi - lo).astype(jnp.float32)[None, :, None]
        diff = (mean - hf[..., sl]).astype(h.dtype)
        outs.append(diff @ w[gi] + b[gi])
    return jnp.concatenate(outs, axis=-1) * scale


def setup_inputs(seed: int = 0) -> dict:
    key = jax.random.key(seed)
    ks = iter(jax.random.split(key, 64))
    D, F, H, N = D_MODEL, D_FF, RW_HEADS, RW_HEAD

    def nrm(shape, scale=1.0):
        return scale * jax.random.normal(next(ks), shape, jnp.float32)

    def gain(shape):
        return 1.0 + nrm(shape, 0.05)

    return {
        'x': nrm((BATCH, SEQ, D)),
        'c': nrm((BATCH, D)),
        'ctx': nrm((BATCH, CTX_LEN, D)),
        'c_ctx': nrm((D,)),
        'mod_w': nrm((DEPTH, D, N_MOD * D), 0.5 * D ** -0.5),
        'mod_b': nrm((DEPTH, N_MOD * D), 0.02),
        'norm_pre': gain((DEPTH, 3, D)),
        'norm_post': gain((DEPTH, 3, D)),
        'ffn_w_gate': nrm((DEPTH, 2, D, F), D ** -0.5),
        'ffn_w_up': nrm((DEPTH, 2, D, F), D ** -0.5),
        'ffn_w_down': nrm((DEPTH, 2, F, D), F ** -0.5),
        'rw_mu': jax.random.uniform(next(ks), (N_RWKV, 6, D), jnp.float32),
        'rw_w_r': nrm((N_RWKV, D, D), D ** -0.5),
        'rw_w_k': nrm((N_RWKV, D, D), D ** -0.5),
        'rw_w_v': nrm((N_RWKV, D, D), D ** -0.5),
        'rw_w_o': nrm((N_RWKV, D, D), D ** -0.5),
        'rw_w0': nrm((N_RWKV, 2, D)) - 3.0,
        'rw_w1': nrm((N_RWKV, 2, D, RW_DECAY_LORA), D ** -0.5),
        'rw_w2': nrm((N_RWKV, 2, RW_DECAY_LORA, D), 0.5 * RW_DECAY_LORA ** -0.5),
        'rw_a0': nrm((N_RWKV, 2, D), 0.5),
        'rw_a1': nrm((N_RWKV, 2, D, RW_ICLR_LORA), D ** -0.5),
        'rw_a2': nrm((N_RWKV, 2, RW_ICLR_LORA, D), 0.5 * RW_ICLR_LORA ** -0.5),
        'rw_v0': nrm((N_RWKV_VRES, D), 0.5),
        'rw_v1': nrm((N_RWKV_VRES, D, RW_VALUE_LORA), D ** -0.5),
        'rw_v2': nrm((N_RWKV_VRES, RW_VALUE_LORA, D), 0.5 * RW_VALUE_LORA ** -0.5),
        'rw_g1': nrm((N_RWKV, D, RW_GATE_LORA), D ** -0.5),
        'rw_g2': nrm((N_RWKV, RW_GATE_LORA, D), RW_GATE_LORA ** -0.5),
        'rw_k_k': 0.85 + nrm((N_RWKV, D), 0.05),
        'rw_k_a': 1.0 + nrm((N_RWKV, D), 0.05),
        'rw_r_k': nrm((N_RWKV, H, N), 0.1),
        'rw_ln_w': gain((N_RWKV, D)),
        'rw_ln_b': nrm((N_RWKV, D), 0.01),
        'mla_w_dq': nrm((N_MLA, D, MLA_Q_RANK), D ** -0.5),
        'mla_q_norm': gain((N_MLA, MLA_Q_RANK)),
        'mla_w_uq': nrm((N_MLA, MLA_Q_RANK, MLA_HEADS * (MLA_NOPE + MLA_ROPE)), MLA_Q_RANK ** -0.5),
        'mla_w_dkv': nrm((N_MLA, D, MLA_KV_RANK + MLA_ROPE), D ** -0.5),
        'mla_kv_norm': gain((N_MLA, MLA_KV_RANK)),
        'mla_w_ukv': nrm((N_MLA, MLA_KV_RANK, MLA_HEADS * (MLA_NOPE + MLA_V)), MLA_KV_RANK ** -0.5),
        'mla_w_o': nrm((N_MLA, MLA_HEADS * MLA_V, D), (MLA_HEADS * MLA_V) ** -0.5),
        'pool_w': nrm((N_POOL, len(POOL_WINDOWS), POOL_GROUP, POOL_GROUP), POOL_GROUP ** -0.5),
        'pool_b': nrm((N_POOL, len(POOL_WINDOWS), POOL_GROUP), 0.01),
        'pool_scale': 1.0 + nrm((N_POOL, D), 0.1),
    }


def reference(x, c, ctx, c_ctx, mod_w, mod_b, norm_pre, norm_post, ffn_w_gate, ffn_w_up, ffn_w_down,
              rw_mu, rw_w_r, rw_w_k, rw_w_v, rw_w_o, rw_w0, rw_w1, rw_w2, rw_a0, rw_a1, rw_a2,
              rw_v0, rw_v1, rw_v2, rw_g1, rw_g2, rw_k_k, rw_k_a, rw_r_k, rw_ln_w, rw_ln_b,
              mla_w_dq, mla_q_norm, mla_w_uq, mla_w_dkv, mla_kv_norm, mla_w_ukv, mla_w_o,
              pool_w, pool_b, pool_scale):
    B, T, D = x.shape
    rows = T // GRID_W
    ang_row, ang_col = _axial_angles(rows)
    cs = ctx
    sc = jax.nn.silu(c)
    scc = jax.nn.silu(c_ctx)[None]
    vf_c = None
    vf_x = None
    for i in range(DEPTH):
        kind, j = i % N_MIXERS, i // N_MIXERS
        last = i == DEPTH - 1
        ctx_live = (not last) or kind != 2
        mod_x = (sc @ mod_w[i] + mod_b[i]).reshape(B, N_MOD, 1, D)
        mod_c = (scc @ mod_w[i] + mod_b[i]).reshape(1, N_MOD, 1, D)

        x = _ffn_sublayer(x, mod_x, 0, norm_pre[i, 0], norm_post[i, 0],
                          ffn_w_gate[i, 0], ffn_w_up[i, 0], ffn_w_down[i, 0])
        if ctx_live:
            cs = _ffn_sublayer(cs, mod_c, 0, norm_pre[i, 0], norm_post[i, 0],
                               ffn_w_gate[i, 0], ffn_w_up[i, 0], ffn_w_down[i, 0])

        hx = _pre(x, mod_x, 1, norm_pre[i, 1])
        hc = _pre(cs, mod_c, 1, norm_pre[i, 1]) if ctx_live else None
        if kind == 0:
            p = {'mu': rw_mu[j], 'w_r': rw_w_r[j], 'w_k': rw_w_k[j], 'w_v': rw_w_v[j], 'w_o': rw_w_o[j],
                 'w0': rw_w0[j], 'w1': rw_w1[j], 'w2': rw_w2[j],
                 'a0': rw_a0[j], 'a1': rw_a1[j], 'a2': rw_a2[j],
                 'g1': rw_g1[j], 'g2': rw_g2[j], 'k_k': rw_k_k[j], 'k_a': rw_k_a[j], 'r_k': rw_r_k[j],
                 'ln_w': rw_ln_w[j], 'ln_b': rw_ln_b[j]}
            if j > 0:
                p['v0'] = rw_v0[j - 1]
                p['v1'] = rw_v1[j - 1]
                p['v2'] = rw_v2[j - 1]
            yc, yx, vf_c, vf_x = _rwkv_mixer(hc, hx, vf_c, vf_x, p, not last)
        elif kind == 1:
            yc, yx = _mla_mixer(hc, hx, ang_row, ang_col, mla_w_dq[j], mla_q_norm[j], mla_w_uq[j],
                                mla_w_dkv[j], mla_kv_norm[j], mla_w_ukv[j], mla_w_o[j], not last)
        else:
            yx = _pool_mixer(hx, pool_w[j], pool_b[j], pool_scale[j])
            yc = _pool_mixer(hc, pool_w[j], pool_b[j], pool_scale[j]) if not last else None
        x = _post(x, yx, mod_x, 1, norm_post[i, 1], 1.0)

        x = _ffn_sublayer(x, mod_x, 2, norm_pre[i, 2], norm_post[i, 2],
                          ffn_w_gate[i, 1], ffn_w_up[i, 1], ffn_w_down[i, 1])
        if not last:
            cs = _post(cs, yc, mod_c, 1, norm_post[i, 1], 1.0)
            cs = _ffn_sublayer(cs, mod_c, 2, norm_pre[i, 2], norm_post[i, 2],
                               ffn_w_gate[i, 1], ffn_w_up[i, 1], ffn_w_down[i, 1])
    return x
```

```python
import numpy as np
from contextlib import ExitStack
import concourse.bass as bass
import concourse.mybir as mybir
from concourse.bass_utils import run_bass_kernel_spmd

F32 = mybir.dt.float32
BF16 = mybir.dt.bfloat16
AF = mybir.ActivationFunctionType
ALU = mybir.AluOpType
AX = mybir.AxisListType

NCORES = 8
D = 1024
DC = 8
SEQ = 16384
CTX = 256
DFF = 2816
FC = 22
TX = SEQ // NCORES
TCX = CTX // NCORES
NTOK = TX + TCX
EPS = 1e-6
SEM_LIMIT = 30000
NDSLOT = 6


class Prog:
    ENG = ('pe', 'dve', 'act', 'pool', 'sp')

    def __init__(self, nc):
        self.nc = nc
        self.ops = {e: [] for e in self.ENG}
        self.cnt = {}
        self.epoch = {}
        self.seen = {e: {} for e in self.ENG}
        self.last_w = {}
        self.readers = {}
        self.semnames = []
        self.n_ops = 0
        self.dslot = {}
        self.dlast = {}

    def _tok(self, kind, eng, inc):
        base = kind + eng
        ep = self.epoch.get(base, 0)
        name = f"{base}{ep}"
        c = self.cnt.get(name, 0) + inc
        if c > SEM_LIMIT:
            ep += 1
            self.epoch[base] = ep
            name = f"{base}{ep}"
            c = inc
        self.cnt[name] = c
        if name not in self.semnames:
            self.semnames.append(name)
        return (name, c)

    def op(self, eng, fn, reads=(), writes=(), dma=False):
        need = {}

        def add(tok):
            if tok is None:
                return
            s, v = tok
            if eng == 'pe' and s.startswith('cpe') and not dma:
                return
            if need.get(s, 0) < v:
                need[s] = v
        for k in reads:
            add(self.last_w.get(k))
        for k in writes:
            add(self.last_w.get(k))
            for s, v in self.readers.get(k, {}).items():
                add((s, v))
        waits = []
        for s, v in need.items():
            if self.seen[eng].get(s, 0) < v:
                waits.append((s, v))
                self.seen[eng][s] = v
        if dma:
            slot = self.dslot.get(eng, 0)
            self.dslot[eng] = (slot + 1) % NDSLOT
            kind = f'd{slot}'
            prev = self.dlast.get((eng, slot))
            if prev is not None and self.seen[eng].get(prev[0], 0) < prev[1]:
                waits.append(prev)
                self.seen[eng][prev[0]] = prev[1]
            tok = self._tok(kind, eng, 16)
            self.dlast[(eng, slot)] = tok
        else:
            tok = self._tok('c', eng, 1)
        for k in writes:
            self.last_w[k] = tok
            self.readers[k] = {}
        for k in reads:
            r = self.readers.setdefault(k, {})
            if r.get(tok[0], 0) < tok[1]:
                r[tok[0]] = tok[1]
        self.ops[eng].append((waits, fn, tok, dma))
        self.n_ops += 1
        return tok

    def emit(self):
        nc = self.nc
        with ExitStack() as st:
            sems = {n: st.enter_context(nc.semaphore(n)) for n in self.semnames}
            block = st.enter_context(nc.Block())
            finals = [(n, self.cnt[n]) for n in self.semnames]

            def mk(engname):
                def body(e):
                    for waits, fn, tok, dma in self.ops[engname]:
                        for s, v in waits:
                            e.wait_ge(sems[s], v)
                        ins = fn(e)
                        ins.then_inc(sems[tok[0]], 16 if dma else 1)
                    if engname == 'sp':
                        for n, v in finals:
                            e.wait_ge(sems[n], v)
                return body
            block.tensor(mk('pe'))
            block.vector(mk('dve'))
            block.scalar(mk('act'))
            block.gpsimd(mk('pool'))
            block.sync(mk('sp'))


class Ctx:
    def __init__(self, name):
        self.nc = bass.Bass("TRN2", target_bir_lowering=False)
        self.st = ExitStack()
        self.p = Prog(self.nc)
        self.uid = 0
        self.psum_banks = []
        self.ps_rr = 0
        self.dq = 0

    def din(self, name, shape, dt=F32):
        return self.nc.dram_tensor(name, list(shape), dt, kind="ExternalInput").ap()

    def dout(self, name, shape, dt=F32):
        return self.nc.dram_tensor(name, list(shape), dt, kind="ExternalOutput").ap()

    def sb(self, name, shape, dt=F32):
        return self.st.enter_context(self.nc.sbuf_tensor('s_' + name, list(shape), dt))

    def init_psum(self, split=2):
        self.ps_bufs = []
        for b in range(8):
            t = self.st.enter_context(self.nc.psum_tensor(f"psb{b}", [128, 512], F32))
            w = 512 // split
            for h in range(split):
                self.ps_bufs.append((t, h * w, w, f"ps{b}_{h}"))

    def ps(self):
        t, off, w, key = self.ps_bufs[self.ps_rr % len(self.ps_bufs)]
        self.ps_rr += 1
        return t, off, w, key

    def dma(self, out, in_, reads=(), writes=(), q=None):
        if q is None:
            q = 'sp'
        self.p.op(q, lambda e: e.dma_start(out=out, in_=in_), reads=reads, writes=writes, dma=True)

    def finish(self):
        self.p.emit()
        self.st.close()
        return self.nc


def _run(nc, in_maps):
    res = run_bass_kernel_spmd(nc, in_maps, core_ids=list(range(NCORES)))
    return res.results


_MOD_NC = None


def build_mod():
    c = Ctx("mod")
    p = c.p
    cin = c.din("cin", [128, 8, 2])
    mw = c.din("mw", [4, 1024, 1152])
    mb = c.din("mb", [128, 4, 9])
    out = c.dout("mo", [128, 4, 9, 2])
    c.init_psum(split=1)
    cs = c.sb("cs", [128, 8, 2])
    ss = c.sb("ss", [128, 8, 2])
    sg = c.sb("sg", [128, 8, 2])
    bs = c.sb("bs", [128, 4, 9])
    os_ = c.sb("os", [128, 4, 9, 2])
    ws = [c.sb(f"w{l}", [128, 8, 1152]) for l in range(4)]
    c.dma(cs[:], cin[:, :, :], writes=['cs'])
    c.dma(bs[:], mb[:, :, :], writes=['bs'])
    for l in range(4):
        for kc in range(8):
            q = ('sp', 'pool')[kc % 2]
            c.dma(ws[l][:, kc, :], mw[l, kc * 128:(kc + 1) * 128, :], writes=[f'w{l}_{kc}'], q=q)
    p.op('act', lambda e: e.activation(out=sg[:], in_=cs[:], func=AF.Sigmoid), reads=['cs'], writes=['sg'])
    p.op('dve', lambda e: e.tensor_tensor(out=ss[:], in0=cs[:], in1=sg[:], op=ALU.mult), reads=['cs', 'sg'], writes=['ss'])
    for l in range(4):
        for oc in range(9):
            t, off, w, key = c.ps()
            for kc in range(8):
                p.op('pe', lambda e, l=l, oc=oc, kc=kc, t=t, off=off: e.matmul(
                    t[:, off:off + 2], lhsT=ws[l][:, kc, oc * 128:(oc + 1) * 128], rhs=ss[:, kc, :],
                    start=(kc == 0), stop=(kc == 7)),
                    reads=['ss', f'w{l}_{kc}'], writes=[key])
            p.op('dve', lambda e, l=l, oc=oc, t=t, off=off: e.tensor_scalar(
                out=os_[:, l, oc, :], in0=t[:, off:off + 2], scalar1=bs[:, l, oc:oc + 1], scalar2=None, op0=ALU.add),
                reads=[key, 'bs'], writes=['os'])
    c.dma(out[:, :, :, :], os_[:], reads=['os'])
    return c.finish()


def run_mod(inp):
    global _MOD_NC
    if _MOD_NC is None:
        _MOD_NC = build_mod()
    cin = np.stack([inp['c'].reshape(D), inp['c_ctx'].reshape(D)], axis=-1)
    cin = np.ascontiguousarray(cin.reshape(8, 128, 2).transpose(1, 0, 2))
    maps = []
    for core in range(NCORES):
        mw = np.ascontiguousarray(inp['mod_w'][:, :, core * 1152:(core + 1) * 1152])
        mb = inp['mod_b'][:, core * 1152:(core + 1) * 1152].reshape(4, 9, 128)
        mb = np.ascontiguousarray(mb.transpose(2, 0, 1))
        maps.append({"cin": cin, "mw": mw, "mb": mb})
    res = _run(_MOD_NC, maps)
    mo = np.concatenate([r["mo"] for r in res], axis=2)
    return [np.ascontiguousarray(mo[:, l].reshape(128, 9, 8, 2)) for l in range(4)]


def load_consts(c):
    ones = c.sb("ones", [128, 128])
    epsc = c.sb("epsc", [128, 1])
    c.p.op('pool', lambda e: e.memset(ones[:], 1.0), writes=['ones'])
    c.p.op('pool', lambda e: e.memset(epsc[:], EPS), writes=['epsc'])
    c.ones, c.epsc = ones, epsc


def rms_rstd(c, src, skey, nch, n, dn, sq, sqkey, rstd, rkey, tmp, tkey, eps_ap=None, sq_eng='act'):
    p = c.p
    if sq_eng == 'act':
        p.op('act', lambda e: e.activation(out=sq[:, 0:nch, 0:n], in_=src, func=AF.Square), reads=[skey], writes=[sqkey])
    else:
        p.op('pool', lambda e: e.tensor_tensor(out=sq[:, 0:nch, 0:n], in0=src, in1=src, op=ALU.mult), reads=[skey], writes=[sqkey])
    t, off, w, key = c.ps()
    for kc in range(nch):
        p.op('pe', lambda e, kc=kc: e.matmul(t[:, off:off + n], lhsT=c.ones[:], rhs=sq[:, kc, 0:n],
                                             start=(kc == 0), stop=(kc == nch - 1)),
             reads=[sqkey, 'ones'], writes=[key])
    ea = c.epsc if eps_ap is None else eps_ap
    p.op('act', lambda e: e.activation(out=tmp[:, 0:n], in_=t[:, off:off + n], func=AF.Sqrt, scale=1.0 / dn, bias=ea[:, 0:1]),
         reads=[key, 'epsc'], writes=[tkey])
    p.op('dve', lambda e: e.reciprocal(out=rstd[:, 0:n], in_=tmp[:, 0:n]), reads=[tkey], writes=[rkey])


NT_F = 256
_FFN_NC = None


def build_ffn():
    c = Ctx("ffn")
    p = c.p
    xin = c.din("x", [128, DC, NTOK])
    xout = c.dout("xo", [128, DC, NTOK])
    wg = c.din("wg", [D, DFF])
    wu = c.din("wu", [D, DFF])
    wd = c.din("wd", [DFF, D])
    modp = c.din("modp", [128, 3, DC, 2])
    gn = c.din("gn", [128, 2, DC])
    c.init_psum(split=1)
    load_consts(c)
    wgs = c.sb("wgs", [128, DC, DFF], BF16)
    wus = c.sb("wus", [128, DC, DFF], BF16)
    wds = c.sb("wds", [128, FC, D], BF16)
    mps = c.sb("mps", [128, 3, DC, 2])
    gns = c.sb("gns", [128, 2, DC])
    gs = c.sb("gs", [128, DC, 2])
    gp = c.sb("gp", [128, DC, 2])
    xt = [c.sb(f"xt{b}", [128, DC, NT_F]) for b in range(2)]
    sqa = c.sb("sqa", [128, DC, NT_F])
    sqb = c.sb("sqb", [128, DC, NT_F])
    hb = [c.sb(f"h{b}", [128, DC, NT_F], BF16) for b in range(2)]
    hact = c.sb("hact", [128, FC, NT_F], BF16)
    sg = [c.sb(f"sg{b}", [128, NT_F], BF16) for b in range(3)]
    ysb = c.sb("ysb", [128, DC, NT_F])
    rs = [c.sb(f"rs{b}", [128, NT_F]) for b in range(2)]
    rs2 = c.sb("rs2", [128, NT_F])
    tmpa = c.sb("tmpa", [128, NT_F])
    tmpb = c.sb("tmpb", [128, NT_F])

    c.dma(mps[:], modp[:, :, :, :], writes=['mps'])
    c.dma(gns[:], gn[:, :, :], writes=['gns'])
    for kc in range(DC):
        c.dma(wgs[:, kc, :], wg[kc * 128:(kc + 1) * 128, :], writes=[f'wg{kc}'], q='pool')
        c.dma(wus[:, kc, :], wu[kc * 128:(kc + 1) * 128, :], writes=[f'wu{kc}'], q='pool')
    for fc in range(FC):
        c.dma(wds[:, fc, :], wd[fc * 128:(fc + 1) * 128, :], writes=[f'wd{fc}'], q='pool')
    p.op('dve', lambda e: e.tensor_scalar(out=gs[:], in0=mps[:, 1, :, :], scalar1=1.0, scalar2=None, op0=ALU.add),
         reads=['mps'], writes=['gs'])
    p.op('dve', lambda e: e.tensor_tensor(out=gs[:], in0=gs[:], in1=gns[:, 0, :].unsqueeze(2).to_broadcast([128, DC, 2]), op=ALU.mult),
         reads=['gs', 'gns'], writes=['gs'])
    p.op('dve', lambda e: e.tensor_scalar(out=gp[:], in0=mps[:, 2, :, :], scalar1=0.5, scalar2=None, op0=ALU.mult),
         reads=['mps'], writes=['gp'])
    p.op('dve', lambda e: e.tensor_tensor(out=gp[:], in0=gp[:], in1=gns[:, 1, :].unsqueeze(2).to_broadcast([128, DC, 2]), op=ALU.mult),
         reads=['gp', 'gns'], writes=['gp'])

    tiles = [(i * NT_F, NT_F, 0) for i in range(TX // NT_F)] + [(TX, TCX, 1)]

    def pre(i):
        col0, n, var = tiles[i]
        b = i % 2
        c.dma(xt[b][:, :, 0:n], xin[:, :, col0:col0 + n], writes=[f'xt{b}'])
        rms_rstd(c, xt[b][:, :, 0:n], f'xt{b}', DC, n, D, sqa, 'sqa', rs[b], f'rs{b}', tmpa, 'tmpa')
        p.op('dve', lambda e: e.tensor_tensor(out=sqa[:, :, 0:n], in0=xt[b][:, :, 0:n],
                                              in1=rs[b][:, 0:n].unsqueeze(1).to_broadcast([128, DC, n]), op=ALU.mult),
             reads=[f'xt{b}', f'rs{b}'], writes=['sqa'])
        for kc in range(DC):
            p.op('act', lambda e, kc=kc: e.activation(out=hb[b][:, kc, 0:n], in_=sqa[:, kc, 0:n], func=AF.Identity,
                                                      scale=gs[:, kc, var:var + 1], bias=mps[:, 0, kc, var:var + 1]),
                 reads=['sqa', 'gs', 'mps'], writes=[f'h{b}'])

    def phase1(i):
        col0, n, var = tiles[i]
        b = i % 2
        for fc in range(FC):
            tg, og, _, kg = c.ps()
            tu, ou, _, ku = c.ps()
            for kc in range(DC):
                p.op('pe', lambda e, kc=kc, fc=fc, tg=tg, og=og: e.matmul(
                    tg[:, og:og + n], lhsT=wgs[:, kc, fc * 128:(fc + 1) * 128], rhs=hb[b][:, kc, 0:n],
                    start=(kc == 0), stop=(kc == DC - 1)), reads=[f'h{b}', f'wg{kc}'], writes=[kg])
            for kc in range(DC):
                p.op('pe', lambda e, kc=kc, fc=fc, tu=tu, ou=ou: e.matmul(
                    tu[:, ou:ou + n], lhsT=wus[:, kc, fc * 128:(fc + 1) * 128], rhs=hb[b][:, kc, 0:n],
                    start=(kc == 0), stop=(kc == DC - 1)), reads=[f'h{b}', f'wu{kc}'], writes=[ku])
            s = sg[fc % 3]
            sk = f'sg{fc % 3}'
            p.op('act', lambda e, tg=tg, og=og, s=s: e.activation(out=s[:, 0:n], in_=tg[:, og:og + n], func=AF.Silu),
                 reads=[kg], writes=[sk])
            p.op('dve', lambda e, tu=tu, ou=ou, s=s, fc=fc: e.tensor_tensor(out=hact[:, fc, 0:n], in0=tu[:, ou:ou + n], in1=s[:, 0:n], op=ALU.mult),
                 reads=[sk, ku], writes=[f'hact{fc}'])

    def phase2(i):
        col0, n, var = tiles[i]
        for dc in range(DC):
            t, o, _, k = c.ps()
            for fc in range(FC):
                p.op('pe', lambda e, fc=fc, dc=dc, t=t, o=o: e.matmul(
                    t[:, o:o + n], lhsT=wds[:, fc, dc * 128:(dc + 1) * 128], rhs=hact[:, fc, 0:n],
                    start=(fc == 0), stop=(fc == FC - 1)), reads=[f'hact{fc}', f'wd{fc}'], writes=[k])
            if dc % 2 == 0:
                p.op('act', lambda e, dc=dc, t=t, o=o: e.copy(out=ysb[:, dc, 0:n], in_=t[:, o:o + n]), reads=[k], writes=[f'y{dc}'])
            else:
                p.op('dve', lambda e, dc=dc, t=t, o=o: e.tensor_copy(out=ysb[:, dc, 0:n], in_=t[:, o:o + n]), reads=[k], writes=[f'y{dc}'])

    def post(i):
        col0, n, var = tiles[i]
        b = i % 2
        ykeys = [f'y{dc}' for dc in range(DC)]
        p.op('pool', lambda e: e.memset(tmpb[:, 0:1], 0.0), reads=ykeys, writes=['yall'])
        rms_rstd(c, ysb[:, :, 0:n], 'yall', DC, n, D, sqb, 'sqb', rs2, 'rs2', tmpb, 'tmpb2')
        p.op('dve', lambda e: e.tensor_tensor(out=sqb[:, :, 0:n], in0=ysb[:, :, 0:n],
                                              in1=rs2[:, 0:n].unsqueeze(1).to_broadcast([128, DC, n]), op=ALU.mult),
             reads=['yall', 'rs2'], writes=['sqb', 'yall'])
        for kc in range(DC):
            p.op('dve', lambda e, kc=kc: e.scalar_tensor_tensor(
                out=xt[b][:, kc, 0:n], in0=sqb[:, kc, 0:n], scalar=gp[:, kc, var:var + 1], in1=xt[b][:, kc, 0:n],
                op0=ALU.mult, op1=ALU.add), reads=['sqb', 'gp', f'xt{b}'], writes=[f'xt{b}'])
        c.dma(xout[:, :, col0:col0 + n], xt[b][:, :, 0:n], reads=[f'xt{b}'], writes=['xout'])

    nt = len(tiles)
    pre(0)
    for i in range(nt):
        phase1(i)
        if i + 1 < nt:
            pre(i + 1)
        phase2(i)
        post(i)
    return c.finish()


def to_fm(a):
    T = a.shape[0]
    return np.ascontiguousarray(a.reshape(T, DC, 128).transpose(2, 1, 0))


def from_fm(a):
    T = a.shape[2]
    return np.ascontiguousarray(a.transpose(2, 1, 0).reshape(T, D))


def run_ffn(xs, inp, mods, layer, which):
    global _FFN_NC
    if _FFN_NC is None:
        _FFN_NC = build_ffn()
    slot = 0 if which == 0 else 2
    modp = np.ascontiguousarray(mods[layer][:, 3 * slot:3 * slot + 3])
    gn = np.stack([inp['norm_pre'][layer, slot].reshape(DC, 128).T,
                   inp['norm_post'][layer, slot].reshape(DC, 128).T], axis=1)
    gn = np.ascontiguousarray(gn)
    wg = inp['ffn_w_gate'][layer, which]
    wu = inp['ffn_w_up'][layer, which]
    wd = inp['ffn_w_down'][layer, which]
    maps = [{"x": xs[k], "wg": wg, "wu": wu, "wd": wd, "modp": modp, "gn": gn} for k in range(NCORES)]
    res = _run(_FFN_NC, maps)
    return [r["xo"] for r in res]


CH = 64
NBLK = 10
LSCAN = CTX + SEQ


def scan_consts():
    su = np.triu(np.ones((CH, CH), np.float32), 1)
    ui = np.triu(np.ones((CH, CH), np.float32), 0)
    z = np.zeros((CH, CH), np.float32)
    bd = lambda m: np.block([[m, z], [z, m]])
    m12 = np.concatenate([bd(su), bd(ui)], axis=1)
    ml = bd(su.T.copy())
    idn = np.eye(128, dtype=np.float32)
    sm = np.ones((128, NBLK, CH), np.float32)
    sm[:, :, 0] = 0.0
    return {"m12": m12, "ml": ml, "idn": idn, "smask": sm.reshape(128, NBLK * CH)}


def build_scan(L=LSCAN, dbg=9):
    nch = L // CH
    assert nch % NBLK == 0
    nblk = nch // NBLK
    W = NBLK * CH
    c = Ctx("scan")
    p = c.p
    ch5 = [c.din(f"ch5_{d}", [128, 5, L]) for d in range(2)]
    vst = [c.din(f"vst_{d}", [nch, 128, CH]) for d in range(2)]
    yo = [c.dout(f"yo_{d}", [CH, nch, 128]) for d in range(2)]
    m12d = c.din("m12", [128, 256])
    mld = c.din("ml", [128, 128])
    idnd = c.din("idn", [128, 128])
    smd = c.din("smask", [128, W])
    c.init_psum(split=1)
    m12 = c.sb("m12s", [128, 256])
    ml = c.sb("mls", [128, 128])
    idn = c.sb("idns", [128, 128])
    sm = c.sb("sms", [128, W])
    c.dma(m12[:], m12d[:, :], writes=['m12'])
    c.dma(ml[:], mld[:, :], writes=['ml'])
    c.dma(idn[:], idnd[:, :], writes=['idn'])
    c.dma(sm[:], smd[:, :], writes=['sm'])
    NB2 = 2
    T = {}
    for d in range(2):
        for b in range(NB2):
            T[f'ch5{d}{b}'] = c.sb(f"ch5s{d}{b}", [128, 5, W])
            T[f'v{d}{b}'] = c.sb(f"vs{d}{b}", [128, NBLK, CH])
            T[f'g{d}{b}'] = c.sb(f"g{d}{b}", [128, W])
            T[f'kr{d}{b}'] = c.sb(f"kr{d}{b}", [128, NBLK, 2, 128])
            T[f'kig{d}{b}'] = c.sb(f"kig{d}{b}", [128, NBLK, 128])
            T[f'big{d}{b}'] = c.sb(f"big{d}{b}", [128, NBLK, 128])
            for nm in ('kr', 'kig', 'big'):
                t = T[f'{nm}{d}{b}']
                p.op('pool', lambda e, t=t: e.memset(t[:], 0.0), writes=[f'{nm}{d}{b}'])
        T[f'cl{d}'] = c.sb(f"cl{d}", [128, W])
        T[f'ig{d}'] = c.sb(f"ig{d}", [128, W])
        T[f'gm{d}'] = c.sb(f"gm{d}", [128, W])
        T[f'st{d}'] = c.sb(f"st{d}", [128, CH])
        p.op('pool', lambda e, d=d: e.memset(T[f'st{d}'][:], 0.0), writes=[f'st{d}'])
        for b in range(2):
            T[f'kigB{d}{b}'] = c.sb(f"kigB{d}{b}", [128, 128])
            T[f'bigB{d}{b}'] = c.sb(f"bigB{d}{b}", [128, 128])
            T[f'a1{d}{b}'] = c.sb(f"a1_{d}{b}", [128, 256])
            T[f'a2{d}{b}'] = c.sb(f"a2_{d}{b}", [128, 256])
            T[f'P{d}{b}'] = c.sb(f"P{d}{b}", [128, 128])
            T[f'X{d}{b}'] = c.sb(f"X{d}{b}", [128, 128])
            T[f'XT{d}{b}'] = c.sb(f"XT{d}{b}", [128, 128])
            T[f'yo{d}{b}'] = c.sb(f"yos{d}{b}", [CH, 128])
        T[f'rhs{d}'] = c.sb(f"rhs{d}", [128, CH])
        T[f'nu{d}'] = c.sb(f"nu{d}", [128, CH])
    cnt = {'ev': 0}

    def evac(out, in_, reads, writes):
        cnt['ev'] += 1
        if cnt['ev'] % 2:
            p.op('act', lambda e: e.copy(out=out, in_=in_), reads=reads, writes=writes)
        else:
            p.op('dve', lambda e: e.tensor_copy(out=out, in_=in_), reads=reads, writes=writes)

    def prep(d, blk):
        b = blk % NB2
        c0 = blk * W
        t5 = T[f'ch5{d}{b}']
        k5 = f'ch5{d}{b}'
        c.dma(t5[:], ch5[d][:, :, c0:c0 + W], writes=[k5])
        c.dma(T[f'v{d}{b}'][:], vst[d][blk * NBLK:(blk + 1) * NBLK].rearrange("c p v -> p c v"), writes=[f'v{d}{b}'])
        cl, g, ig, gm = T[f'cl{d}'], T[f'g{d}{b}'], T[f'ig{d}'], T[f'gm{d}']
        p.op('dve', lambda e: e.tensor_tensor_scan(out=cl[:], data0=sm[:], data1=t5[:, 0, :], initial=0.0, op0=ALU.mult, op1=ALU.add),
             reads=[k5, 'sm'], writes=[f'cl{d}'])
        p.op('act', lambda e: e.activation(out=g[:], in_=cl[:], func=AF.Exp), reads=[f'cl{d}'], writes=[f'g{d}{b}'])
        p.op('act', lambda e: e.activation(out=ig[:], in_=cl[:], func=AF.Exp, scale=-1.0), reads=[f'cl{d}'], writes=[f'ig{d}'])
        p.op('dve', lambda e: e.tensor_tensor(out=gm[:], in0=cl[:], in1=t5[:, 0, :], op=ALU.subtract), reads=[f'cl{d}', k5], writes=[f'gm{d}'])
        p.op('act', lambda e: e.activation(out=gm[:], in_=gm[:], func=AF.Exp), reads=[f'gm{d}'], writes=[f'gm{d}'])
        kr, kig, big = T[f'kr{d}{b}'], T[f'kig{d}{b}'], T[f'big{d}{b}']
        i = 0
        for h in range(2):
            ps_ = slice(64 * h, 64 * h + 64)
            v3 = lambda ap: ap.rearrange("p (c t) -> p c t", t=CH)
            jobs = [
                (kr[ps_, :, 0, 64 * h:64 * h + 64], v3(t5[ps_, 3, :]), v3(gm[ps_, :]), [k5, f'gm{d}'], f'kr{d}{b}'),
                (kr[ps_, :, 1, 64 * h:64 * h + 64], v3(t5[ps_, 4, :]), v3(g[ps_, :]), [k5, f'g{d}{b}'], f'kr{d}{b}'),
                (kig[ps_, :, 64 * h:64 * h + 64], v3(t5[ps_, 1, :]), v3(ig[ps_, :]), [k5, f'ig{d}'], f'kig{d}{b}'),
                (big[ps_, :, 64 * h:64 * h + 64], v3(t5[ps_, 2, :]), v3(ig[ps_, :]), [k5, f'ig{d}'], f'big{d}{b}'),
            ]
            for (o, a, bb, rd, wk) in jobs:
                eng = 'pool' if i % 2 == 0 else 'dve'
                i += 1
                p.op(eng, lambda e, o=o, a=a, bb=bb: e.tensor_tensor(out=o, in0=a, in1=bb, op=ALU.mult), reads=rd, writes=[wk])

    def chunk_pre_stages(d, blk, ci):
        b = blk % NB2
        q = ci % 2
        kr, kig, big = T[f'kr{d}{b}'], T[f'kig{d}{b}'], T[f'big{d}{b}']
        kkr, kkig, kbig = f'kr{d}{b}', f'kig{d}{b}', f'big{d}{b}'
        kigB, bigB, a1, a2 = T[f'kigB{d}{q}'], T[f'bigB{d}{q}'], T[f'a1{d}{q}'], T[f'a2{d}{q}']
        nk = lambda s: f'{s}{d}{q}'
        P = [T[f'P{d}0'], T[f'P{d}1']]
        X = [T[f'X{d}0'], T[f'X{d}1']]
        XT = [T[f'XT{d}0'], T[f'XT{d}1']]
        stages = []

        def s0():
            for (src, ksrc, dst, kd) in ((kig, kkig, kigB, nk('kigB')), (big, kbig, bigB, nk('bigB'))):
                t, _, _, key = c.ps()
                p.op('pe', lambda e, t=t, src=src: e.transpose(t[:, 0:128], src[:, ci, :], idn[:]), reads=[ksrc, 'idn'], writes=[key])
                evac(dst[:], t[:, 0:128], [key], [kd])
            for (src, ksrc, dst, kd) in ((kig, kkig, a1, nk('a1')), (big, kbig, a2, nk('a2'))):
                t, _, _, key = c.ps()
                p.op('pe', lambda e, t=t, src=src: e.matmul(t[:, 0:256], lhsT=src[:, ci, :], rhs=kr[:, ci, :, :].rearrange("p a b -> p (a b)"),
                                                            start=True, stop=True), reads=[ksrc, kkr], writes=[key])
                p.op('dve', lambda e, t=t, dst=dst: e.tensor_tensor(out=dst[:], in0=t[:, 0:256], in1=m12[:], op=ALU.mult),
                     reads=[key, 'm12'], writes=[kd])
            t, _, _, key = c.ps()
            p.op('pe', lambda e, t=t: e.matmul(t[:, 0:128], lhsT=kr[:, ci, 0, :], rhs=big[:, ci, :], start=True, stop=True),
                 reads=[kkr, kbig], writes=[key])
            p.op('dve', lambda e, t=t: e.tensor_tensor(out=XT[0][:], in0=t[:, 0:128], in1=ml[:], op=ALU.mult),
                 reads=[key, 'ml'], writes=[f'XT{d}0'])
            p.op('pool', lambda e: e.tensor_copy(out=X[0][:], in_=a2[:, 0:128]), reads=[nk('a2')], writes=[f'X{d}0'])
            p.op('pool', lambda e: e.tensor_tensor(out=P[0][:], in0=idn[:], in1=a2[:, 0:128], op=ALU.subtract),
                 reads=[nk('a2'), 'idn'], writes=[f'P{d}0'])
        stages.append(s0)

        def mkstep(k):
            def s():
                a, bn = k % 2, (k + 1) % 2
                if k < 4:
                    t, _, _, key = c.ps()
                    p.op('pe', lambda e, t=t: e.matmul(t[:, 0:128], lhsT=XT[a][:], rhs=X[a][:], start=True, stop=True),
                         reads=[f'XT{d}{a}', f'X{d}{a}'], writes=[key])
                    evac(X[bn][:], t[:, 0:128], [key], [f'X{d}{bn}'])
                t2, _, _, key2 = c.ps()
                p.op('pe', lambda e, t2=t2: e.matmul(t2[:, 0:128], lhsT=X[a][:], rhs=XT[a][:], start=True, stop=True),
                     reads=[f'XT{d}{a}', f'X{d}{a}'], writes=[key2])
                evac(XT[bn][:], t2[:, 0:128], [key2], [f'XT{d}{bn}'])
                t3, _, _, key3 = c.ps()
                p.op('pe', lambda e, t3=t3: e.matmul(t3[:, 0:128], lhsT=XT[bn][:], rhs=P[a][:], start=True, stop=True),
                     reads=[f'XT{d}{bn}', f'P{d}{a}'], writes=[key3])
                p.op('dve', lambda e, t3=t3: e.tensor_tensor(out=P[bn][:], in0=t3[:, 0:128], in1=P[a][:], op=ALU.add),
                     reads=[key3, f'P{d}{a}'], writes=[f'P{d}{bn}'])
            return s
        for k in range(5):
            stages.append(mkstep(k))
        return stages

    def chunk_seq_stages(d, blk, ci):
        b = blk % NB2
        q = ci % 2
        gci = blk * NBLK + ci
        kr, g = T[f'kr{d}{b}'], T[f'g{d}{b}']
        kkr = f'kr{d}{b}'
        v = T[f'v{d}{b}']
        kv = f'v{d}{b}'
        kigB, bigB, a1, a2 = T[f'kigB{d}{q}'], T[f'bigB{d}{q}'], T[f'a1{d}{q}'], T[f'a2{d}{q}']
        nk = lambda s: f'{s}{d}{q}'
        Pf = T[f'P{d}1']
        st, rhs, nu, yos = T[f'st{d}'], T[f'rhs{d}'], T[f'nu{d}'], T[f'yo{d}{q}']
        kst, krhs, knu, kyo = f'st{d}', f'rhs{d}', f'nu{d}', f'yo{d}{q}'

        def s_rhs():
            t, _, _, key = c.ps()
            p.op('pe', lambda e: e.matmul(t[:, 0:CH], lhsT=a1[:, 0:128], rhs=v[:, ci, :], start=True, stop=False),
                 reads=[nk('a1'), kv], writes=[key])
            p.op('pe', lambda e: e.matmul(t[:, 0:CH], lhsT=kr[:, ci, 0, :], rhs=st[:], start=False, stop=True),
                 reads=[kkr, kst], writes=[key])
            p.op('dve', lambda e: e.tensor_copy(out=rhs[:], in_=t[:, 0:CH]), reads=[key], writes=[krhs])

        def s_u():
            t, _, _, key = c.ps()
            p.op('pe', lambda e: e.matmul(t[:, 0:CH], lhsT=Pf[:], rhs=rhs[:], start=True, stop=True),
                 reads=[f'P{d}1', krhs], writes=[key])
            p.op('act', lambda e: e.mul(out=nu[:], in_=t[:, 0:CH], mul=-1.0), reads=[key], writes=[knu])

        def s_y():
            t, _, _, key = c.ps()
            p.op('pe', lambda e: e.matmul(t[0:CH, 0:128], lhsT=st[:], rhs=kr[:, ci, 1, :], start=True, stop=False),
                 reads=[kst, kkr], writes=[key])
            p.op('pe', lambda e: e.matmul(t[0:CH, 0:128], lhsT=v[:, ci, :], rhs=a1[:, 128:256], start=False, stop=False),
                 reads=[kv, nk('a1')], writes=[key])
            p.op('pe', lambda e: e.matmul(t[0:CH, 0:128], lhsT=nu[:], rhs=a2[:, 128:256], start=False, stop=True),
                 reads=[knu, nk('a2')], writes=[key])
            p.op('dve', lambda e: e.tensor_copy(out=yos[:], in_=t[0:CH, 0:128]), reads=[key], writes=[kyo])
            c.dma(yo[d][:, gci, :], yos[:], reads=[kyo], writes=[f'yout{d}'])

        def s_s():
            t, _, _, key = c.ps()
            p.op('pe', lambda e: e.matmul(t[:, 0:CH], lhsT=idn[:], rhs=st[:], start=True, stop=False),
                 reads=['idn', kst], writes=[key])
            p.op('pe', lambda e: e.matmul(t[:, 0:CH], lhsT=kigB[:], rhs=v[:, ci, :], start=False, stop=False),
                 reads=[nk('kigB'), kv], writes=[key])
            p.op('pe', lambda e: e.matmul(t[:, 0:CH], lhsT=bigB[:], rhs=nu[:], start=False, stop=True),
                 reads=[nk('bigB'), knu], writes=[key])
            col = ci * CH + CH - 1
            p.op('act', lambda e: e.activation(out=st[:], in_=t[:, 0:CH], func=AF.Identity, scale=g[:, col:col + 1]),
                 reads=[key, f'g{d}{b}'], writes=[kst])
        return [s_rhs, s_u, s_y, s_s]

    for blk in range(nblk):
        for d in range(2):
            prep(d, blk)
        if dbg < 2:
            continue
        for ci in range(NBLK):
            pre = [chunk_pre_stages(d, blk, ci) for d in range(2)]
            for k in range(len(pre[0]) if dbg >= 3 else 1):
                for d in range(2):
                    pre[d][k]()
            if dbg < 4:
                continue
            seq = [chunk_seq_stages(d, blk, ci) for d in range(2)]
            for k in range(min(len(seq[0]), dbg - 3)):
                for d in range(2):
                    seq[d][k]()
    return c.finish()


def mod_prologue(c, modp_d, gn_d, res_w):
    p = c.p
    mps = c.sb("mps", [128, 3, DC, 2])
    gns = c.sb("gns", [128, 2, DC])
    gs = c.sb("gs", [128, DC, 2])
    gp = c.sb("gp", [128, DC, 2])
    c.dma(mps[:], modp_d[:, :, :, :], writes=['mps'])
    c.dma(gns[:], gn_d[:, :, :], writes=['gns'])
    p.op('dve', lambda e: e.tensor_scalar(out=gs[:], in0=mps[:, 1, :, :], scalar1=1.0, scalar2=None, op0=ALU.add),
         reads=['mps'], writes=['gs'])
    p.op('dve', lambda e: e.tensor_tensor(out=gs[:], in0=gs[:], in1=gns[:, 0, :].unsqueeze(2).to_broadcast([128, DC, 2]), op=ALU.mult),
         reads=['gs', 'gns'], writes=['gs'])
    p.op('dve', lambda e: e.tensor_scalar(out=gp[:], in0=mps[:, 2, :, :], scalar1=float(res_w), scalar2=None, op0=ALU.mult),
         reads=['mps'], writes=['gp'])
    p.op('dve', lambda e: e.tensor_tensor(out=gp[:], in0=gp[:], in1=gns[:, 1, :].unsqueeze(2).to_broadcast([128, DC, 2]), op=ALU.mult),
         reads=['gp', 'gns'], writes=['gp'])
    c.mps, c.gs, c.gp = mps, gs, gp


def emit_pre(c, xt, xkey, n, var, sq, sqkey, rstd, rkey, tmp, tkey, hout, hkey):
    p = c.p
    rms_rstd(c, xt[:, :, 0:n], xkey, DC, n, D, sq, sqkey, rstd, rkey, tmp, tkey)
    p.op('dve', lambda e: e.tensor_tensor(out=sq[:, :, 0:n], in0=xt[:, :, 0:n],
                                          in1=rstd[:, 0:n].unsqueeze(1).to_broadcast([128, DC, n]), op=ALU.mult),
         reads=[xkey, rkey], writes=[sqkey])
    for kc in range(DC):
        p.op('act', lambda e, kc=kc: e.activation(out=hout[:, kc, 0:n], in_=sq[:, kc, 0:n], func=AF.Identity,
                                                  scale=c.gs[:, kc, var:var + 1], bias=c.mps[:, 0, kc, var:var + 1]),
             reads=[sqkey, 'gs', 'mps'], writes=[hkey])


def emit_post(c, ysb, ykey, xt, xkey, n, var, sq, sqkey, rstd, rkey, tmp, tkey, xoff=0):
    p = c.p
    rms_rstd(c, ysb[:, :, 0:n], ykey, DC, n, D, sq, sqkey, rstd, rkey, tmp, tkey)
    p.op('dve', lambda e: e.tensor_tensor(out=sq[:, :, 0:n], in0=ysb[:, :, 0:n],
                                          in1=rstd[:, 0:n].unsqueeze(1).to_broadcast([128, DC, n]), op=ALU.mult),
         reads=[ykey, rkey], writes=[sqkey])
    for kc in range(DC):
        p.op('dve', lambda e, kc=kc: e.scalar_tensor_tensor(
            out=xt[:, kc, xoff:xoff + n], in0=sq[:, kc, 0:n], scalar=c.gp[:, kc, var:var + 1], in1=xt[:, kc, xoff:xoff + n],
            op0=ALU.mult, op1=ALU.add), reads=[sqkey, 'gp', xkey], writes=[xkey])


def load_w_bf16(c, name, dram2d, rows, cols):
    kc_n = (rows + 127) // 128
    pr = min(rows, 128)
    t = c.sb(name, [pr, kc_n, cols], BF16)
    keys = []
    for kc in range(kc_n):
        r0 = kc * 128
        r1 = min(rows, r0 + 128)
        c.dma(t[0:r1 - r0, kc, :], dram2d[r0:r1, :], writes=[f'{name}{kc}'], q='pool')
        keys.append(f'{name}{kc}')
    return t, keys


class Rot:
    def __init__(self, c, name, shape, dt, n):
        self.t = [c.sb(f"{name}{i}", shape, dt) for i in range(n)]
        self.k = [f"{name}{i}" for i in range(n)]
        self.i = 0

    def get(self):
        j = self.i % len(self.t)
        self.i += 1
        return self.t[j], self.k[j]


def block_ones():
    z = np.zeros((128, 128), np.float32)
    z[:64, :64] = 1.0
    z[64:, 64:] = 1.0
    return z


NT_A = 256
SCAN_NAMES = ['ld0', 'ld1', 'kd0', 'kd1', 'b0', 'b1', 'kk', 'v', 'r', 'g', 'bon']


def build_rwa(vres):
    c = Ctx("rwa")
    p = c.p
    xh = c.din("xh", [128, DC, TX + 2])
    chh = c.din("chh", [128, DC, TCX + 2])
    hmask = c.din("hmask", [128, 4])
    modp = c.din("modp", [128, 3, DC, 2])
    gn = c.din("gn", [128, 2, DC])
    vecs_d = c.din("vecs", [128, 14, DC])
    bones_d = c.din("bones", [128, 128])
    wr_d, wk_d, wv_d = c.din("w_r", [D, D]), c.din("w_k", [D, D]), c.din("w_v", [D, D])
    w1_d, a1_d = c.din("w1", [2, D, 64]), c.din("a1", [2, D, 64])
    w2_d, a2_d = c.din("w2", [2, 64, D]), c.din("a2", [2, 64, D])
    g1_d, g2_d = c.din("g1", [D, 160]), c.din("g2", [160, D])
    if vres:
        v1_d, v2_d = c.din("v1", [D, 32]), c.din("v2", [32, D])
        vf_d = c.din("vf", [128, DC, NTOK])
    outs = {nm: c.dout(f"o_{nm}", [128, DC, NTOK]) for nm in SCAN_NAMES}
    c.init_psum(split=1)
    load_consts(c)
    mod_prologue(c, modp, gn, 1.0)
    vecs = c.sb("vecs", [128, 14, DC])
    c.dma(vecs[:], vecs_d[:, :, :], writes=['vecs'])
    bones = c.sb("bones", [128, 128])
    c.dma(bones[:], bones_d[:, :], writes=['bones'])
    hm = c.sb("hm", [128, 4])
    c.dma(hm[:], hmask[:, :], writes=['hm'])
    omka = c.sb("omka", [128, DC])
    p.op('dve', lambda e: e.tensor_scalar(out=omka[:], in0=vecs[:, 12, :], scalar1=-1.0, scalar2=1.0, op0=ALU.mult, op1=ALU.add),
         reads=['vecs'], writes=['omka'])
    wr, kwr = load_w_bf16(c, "wr", wr_d, D, D)
    wk, kwk = load_w_bf16(c, "wk", wk_d, D, D)
    wv, kwv = load_w_bf16(c, "wv", wv_d, D, D)
    w1 = [load_w_bf16(c, f"w1_{d}", w1_d[d], D, 64) for d in range(2)]
    a1 = [load_w_bf16(c, f"a1_{d}", a1_d[d], D, 64) for d in range(2)]
    w2 = [load_w_bf16(c, f"w2_{d}", w2_d[d], 64, D) for d in range(2)]
    a2 = [load_w_bf16(c, f"a2_{d}", a2_d[d], 64, D) for d in range(2)]
    g1 = load_w_bf16(c, "g1", g1_d, D, 160)
    g2a = load_w_bf16(c, "g2a", g2_d[0:128, :], 128, D)
    g2b = load_w_bf16(c, "g2b", g2_d[128:160, :], 32, D)
    if vres:
        v1 = load_w_bf16(c, "v1", v1_d, D, 32)
        v2 = load_w_bf16(c, "v2", v2_d, 32, D)
    NW = NT_A + 2
    xt = [c.sb(f"xt{b}", [128, DC, NW]) for b in range(2)]
    sq = c.sb("sq", [128, DC, NW])
    hh = c.sb("hh", [128, DC, NW])
    xx = c.sb("xx", [128, DC, NT_A])
    mix = [c.sb(f"mix{m}", [128, DC, NT_A], BF16) for m in range(6)]
    k32 = c.sb("k32", [128, DC, NT_A])
    v32 = c.sb("v32", [128, DC, NT_A])
    r32 = c.sb("r32", [128, DC, NT_A])
    rs = c.sb("rs", [128, NW])
    tmpa = c.sb("tmpa", [128, NW])
    tw = [c.sb(f"tw{d}", [64, NT_A], BF16) for d in range(2)]
    ta = [c.sb(f"ta{d}", [64, NT_A], BF16) for d in range(2)]
    tga = c.sb("tga", [128, NT_A], BF16)
    tgb = c.sb("tgb", [32, NT_A], BF16)
    tv = c.sb("tv", [32, NT_A], BF16)
    if vres:
        vft = c.sb("vft", [128, DC, NT_A])
    R = Rot(c, "sc", [128, NT_A], F32, 24)
    cnt = {'i': 0}

    def evac(out, in_, reads, writes):
        cnt['i'] += 1
        if cnt['i'] % 2:
            p.op('act', lambda e: e.copy(out=out, in_=in_), reads=reads, writes=writes)
        else:
            p.op('dve', lambda e: e.tensor_copy(out=out, in_=in_), reads=reads, writes=writes)

    tiles = [(xh, i * NT_A, NT_A, 0, i * NT_A, (0 if i == 0 else None, 1 if i == TX // NT_A - 1 else None)) for i in range(TX // NT_A)]
    tiles.append((chh, 0, TCX, 1, TX, (2, 3)))
    def do_tile(ti, src, col0, n, var, ocol, ml_, mr_):
        b = ti % 2
        xk = f'xt{b}'
        c.dma(xt[b][:, :, 0:n + 2], src[:, :, col0:col0 + n + 2], writes=[xk])
        emit_pre(c, xt[b], xk, n + 2, var, sq, 'sq', rs, 'rs', tmpa, 'tmpa', hh, 'hh')
        if ml_ is not None:
            p.op('dve', lambda e, ml_=ml_: e.tensor_scalar(out=hh[:, :, 0:1], in0=hh[:, :, 0:1], scalar1=hm[:, ml_:ml_ + 1], scalar2=None, op0=ALU.mult),
                 reads=['hh', 'hm'], writes=['hh'])
        if mr_ is not None:
            p.op('dve', lambda e, mr_=mr_: e.tensor_scalar(out=hh[:, :, n + 1:n + 2], in0=hh[:, :, n + 1:n + 2], scalar1=hm[:, mr_:mr_ + 1], scalar2=None, op0=ALU.mult),
                 reads=['hh', 'hm'], writes=['hh'])
        p.op('pool', lambda e: e.tensor_tensor(out=xx[:, :, 0:n], in0=hh[:, :, 0:n], in1=hh[:, :, 2:n + 2], op=ALU.add),
             reads=['hh'], writes=['xx'])
        p.op('dve', lambda e: e.scalar_tensor_tensor(out=xx[:, :, 0:n], in0=xx[:, :, 0:n], scalar=0.5, in1=hh[:, :, 1:n + 1],
                                                     op0=ALU.mult, op1=ALU.subtract), reads=['xx', 'hh'], writes=['xx'])
        for m in range(6):
            for kc in range(DC):
                p.op('dve', lambda e, m=m, kc=kc: e.scalar_tensor_tensor(
                    out=mix[m][:, kc, 0:n], in0=xx[:, kc, 0:n], scalar=vecs[:, m, kc:kc + 1], in1=hh[:, kc, 1:n + 1],
                    op0=ALU.mult, op1=ALU.add), reads=['xx', 'hh', 'vecs'], writes=[f'mix{m}'])
        if vres:
            c.dma(vft[:, :, 0:n], vf_d[:, :, ocol:ocol + n], writes=['vft'])
        def down(wt, mi, mcols, c0, dst, dkey, func):
            t, _, _, key = c.ps()
            for kc in range(DC):
                p.op('pe', lambda e, kc=kc: e.matmul(t[0:mcols, 0:n], lhsT=wt[0][:, kc, c0:c0 + mcols], rhs=mix[mi][:, kc, 0:n],
                                                     start=(kc == 0), stop=(kc == DC - 1)), reads=[wt[1][kc], f'mix{mi}'], writes=[key])
            p.op('act', lambda e: e.activation(out=dst[0:mcols, 0:n], in_=t[0:mcols, 0:n], func=func), reads=[key], writes=[dkey])
        for d in range(2):
            down(w1[d], 1, 64, 0, tw[d], f'tw{d}', AF.Tanh)
            down(a1[d], 4, 64, 0, ta[d], f'ta{d}', AF.Identity)
        down(g1, 5, 128, 0, tga, 'tga', AF.Sigmoid)
        down(g1, 5, 32, 128, tgb, 'tgb', AF.Sigmoid)
        if vres:
            down(v1, 3, 32, 0, tv, 'tv', AF.Identity)
        for (wt, kw, mi, dst, dk) in ((wr, kwr, 0, r32, 'r32'), (wk, kwk, 2, k32, 'k32'), (wv, kwv, 3, v32, 'v32')):
            for oc in range(DC):
                t, _, _, key = c.ps()
                for kc in range(DC):
                    p.op('pe', lambda e, kc=kc, oc=oc, wt=wt, mi=mi, t=t: e.matmul(
                        t[:, 0:n], lhsT=wt[:, kc, oc * 128:(oc + 1) * 128], rhs=mix[mi][:, kc, 0:n],
                        start=(kc == 0), stop=(kc == DC - 1)), reads=[kw[kc], f'mix{mi}'], writes=[key])
                evac(dst[:, oc, 0:n], t[:, 0:n], [key], [f'{dk}{oc}'])
        def do_oc(oc):
            ocs = slice(oc * 128, (oc + 1) * 128)

            def emit_out(nm, tile, key):
                c.dma(outs[nm][:, oc, ocol:ocol + n], tile[:, 0:n], reads=[key], writes=[f'out_{nm}'])
            kkr, kkr_k = R.get()
            p.op('act', lambda e: e.activation(out=kkr[:, 0:n], in_=k32[:, oc, 0:n], func=AF.Identity, scale=vecs[:, 11, oc:oc + 1]),
                 reads=[f'k32{oc}', 'vecs'], writes=[kkr_k])
            sqk, sqk_k = R.get()
            p.op('pool', lambda e: e.tensor_tensor(out=sqk[:, 0:n], in0=kkr[:, 0:n], in1=kkr[:, 0:n], op=ALU.mult), reads=[kkr_k], writes=[sqk_k])
            t, _, _, key = c.ps()
            p.op('pe', lambda e, t=t: e.matmul(t[:, 0:n], lhsT=bones[:], rhs=sqk[:, 0:n], start=True, stop=True), reads=['bones', sqk_k], writes=[key])
            rn, rn_k = R.get()
            p.op('dve', lambda e, t=t: e.tensor_scalar(out=rn[:, 0:n], in0=t[:, 0:n], scalar1=1e-24, scalar2=None, op0=ALU.max), reads=[key], writes=[rn_k])
            p.op('act', lambda e: e.activation(out=rn[:, 0:n], in_=rn[:, 0:n], func=AF.Sqrt), reads=[rn_k], writes=[rn_k])
            p.op('dve', lambda e: e.reciprocal(out=rn[:, 0:n], in_=rn[:, 0:n]), reads=[rn_k], writes=[rn_k])
            kk, kk_k = R.get()
            p.op('dve', lambda e: e.tensor_tensor(out=kk[:, 0:n], in0=kkr[:, 0:n], in1=rn[:, 0:n], op=ALU.mult), reads=[kkr_k, rn_k], writes=[kk_k])
            emit_out('kk', kk, kk_k)
            if vres:
                t, _, _, key = c.ps()
                p.op('pe', lambda e, t=t: e.matmul(t[:, 0:n], lhsT=v2[0][:, 0, ocs], rhs=tv[:, 0:n], start=True, stop=True),
                     reads=[v2[1][0], 'tv'], writes=[key])
                sgv, sgv_k = R.get()
                p.op('act', lambda e, t=t: e.activation(out=sgv[:, 0:n], in_=t[:, 0:n], func=AF.Sigmoid, bias=vecs[:, 10, oc:oc + 1]),
                     reads=[key, 'vecs'], writes=[sgv_k])
                dv, dv_k = R.get()
                p.op('pool', lambda e: e.tensor_tensor(out=dv[:, 0:n], in0=vft[:, oc, 0:n], in1=v32[:, oc, 0:n], op=ALU.subtract),
                     reads=['vft', f'v32{oc}'], writes=[dv_k])
                p.op('pool', lambda e: e.tensor_tensor(out=dv[:, 0:n], in0=dv[:, 0:n], in1=sgv[:, 0:n], op=ALU.mult), reads=[dv_k, sgv_k], writes=[dv_k])
                p.op('pool', lambda e: e.tensor_tensor(out=v32[:, oc, 0:n], in0=v32[:, oc, 0:n], in1=dv[:, 0:n], op=ALU.add),
                     reads=[dv_k, f'v32{oc}'], writes=[f'v32{oc}'])
            c.dma(outs['v'][:, oc, ocol:ocol + n], v32[:, oc, 0:n], reads=[f'v32{oc}'], writes=['out_v'])
            c.dma(outs['r'][:, oc, ocol:ocol + n], r32[:, oc, 0:n], reads=[f'r32{oc}'], writes=['out_r'])
            t, _, _, key = c.ps()
            p.op('pe', lambda e, t=t: e.matmul(t[:, 0:n], lhsT=g2a[0][:, 0, ocs], rhs=tga[:, 0:n], start=True, stop=False),
                 reads=[g2a[1][0], 'tga'], writes=[key])
            p.op('pe', lambda e, t=t: e.matmul(t[:, 0:n], lhsT=g2b[0][:, 0, ocs], rhs=tgb[:, 0:n], start=False, stop=True),
                 reads=[g2b[1][0], 'tgb'], writes=[key])
            gt, gt_k = R.get()
            evac(gt[:, 0:n], t[:, 0:n], [key], [gt_k])
            emit_out('g', gt, gt_k)
            rrk, rrk_k = R.get()
            p.op('pool', lambda e: e.tensor_scalar(out=rrk[:, 0:n], in0=r32[:, oc, 0:n], scalar1=vecs[:, 13, oc:oc + 1], scalar2=None, op0=ALU.mult),
                 reads=[f'r32{oc}', 'vecs'], writes=[rrk_k])
            tb, _, _, keyb = c.ps()
            def do_dir(d):
                t, _, _, key = c.ps()
                p.op('pe', lambda e, t=t, d=d: e.matmul(t[:, 0:n], lhsT=w2[d][0][:, 0, ocs], rhs=tw[d][:, 0:n], start=True, stop=True),
                     reads=[w2[d][1][0], f'tw{d}'], writes=[key])
                ld, ld_k = R.get()
                p.op('act', lambda e, t=t, d=d, ld=ld: e.activation(out=ld[:, 0:n], in_=t[:, 0:n], func=AF.Sigmoid, bias=vecs[:, 6 + d, oc:oc + 1]),
                     reads=[key, 'vecs'], writes=[ld_k])
                p.op('pool', lambda e, ld=ld: e.tensor_scalar(out=ld[:, 0:n], in0=ld[:, 0:n], scalar1=-0.6065306597126334, scalar2=None, op0=ALU.mult),
                     reads=[ld_k], writes=[ld_k])
                emit_out(f'ld{d}', ld, ld_k)
                t2, _, _, key2 = c.ps()
                p.op('pe', lambda e, t2=t2, d=d: e.matmul(t2[:, 0:n], lhsT=a2[d][0][:, 0, ocs], rhs=ta[d][:, 0:n], start=True, stop=True),
                     reads=[a2[d][1][0], f'ta{d}'], writes=[key2])
                ad, ad_k = R.get()
                p.op('act', lambda e, t2=t2, d=d, ad=ad: e.activation(out=ad[:, 0:n], in_=t2[:, 0:n], func=AF.Sigmoid, bias=vecs[:, 8 + d, oc:oc + 1]),
                     reads=[key2, 'vecs'], writes=[ad_k])
                bd, bd_k = R.get()
                p.op('pool', lambda e, bd=bd, ad=ad: e.tensor_tensor(out=bd[:, 0:n], in0=kk[:, 0:n], in1=ad[:, 0:n], op=ALU.mult),
                     reads=[kk_k, ad_k], writes=[bd_k])
                emit_out(f'b{d}', bd, bd_k)
                kd, kd_k = R.get()
                p.op('dve', lambda e, kd=kd, ad=ad: e.tensor_scalar(out=kd[:, 0:n], in0=ad[:, 0:n], scalar1=vecs[:, 12, oc:oc + 1],
                                                                    scalar2=omka[:, oc:oc + 1], op0=ALU.mult, op1=ALU.add),
                     reads=[ad_k, 'vecs', 'omka'], writes=[kd_k])
                p.op('dve', lambda e, kd=kd: e.tensor_tensor(out=kd[:, 0:n], in0=kd[:, 0:n], in1=k32[:, oc, 0:n], op=ALU.mult),
                     reads=[kd_k, f'k32{oc}'], writes=[kd_k])
                emit_out(f'kd{d}', kd, kd_k)
                rk, rk_k = R.get()
                p.op('dve', lambda e, rk=rk, kd=kd: e.tensor_tensor(out=rk[:, 0:n], in0=rrk[:, 0:n], in1=kd[:, 0:n], op=ALU.mult),
                     reads=[rrk_k, kd_k], writes=[rk_k])
                p.op('pe', lambda e, rk=rk, d=d: e.matmul(tb[:, 0:n], lhsT=bones[:], rhs=rk[:, 0:n], start=(d == 0), stop=(d == 1)),
                     reads=['bones', rk_k], writes=[keyb])
            for d in range(2):
                do_dir(d)
            bon, bon_k = R.get()
            p.op('dve', lambda e: e.tensor_tensor(out=bon[:, 0:n], in0=tb[:, 0:n], in1=v32[:, oc, 0:n], op=ALU.mult),
                 reads=[keyb, f'v32{oc}'], writes=[bon_k])
            emit_out('bon', bon, bon_k)
        for oc in range(DC):
            do_oc(oc)

    for ti, (src, col0, n, var, ocol, (ml_, mr_)) in enumerate(tiles):
        do_tile(ti, src, col0, n, var, ocol, ml_, mr_)
    return c.finish()


def fm_vec(v):
    return np.ascontiguousarray(np.asarray(v).reshape(DC, 128).T)


def halo_cols(xs, width):
    xh, chh, hmask = [], [], []
    for k in range(NCORES):
        z = np.zeros((128, DC, width), np.float32)
        xl = xs[k - 1][:, :, TX - width:TX] if k > 0 else z
        xr = xs[k + 1][:, :, 0:width] if k < NCORES - 1 else z
        cl = xs[k - 1][:, :, NTOK - width:NTOK] if k > 0 else z
        cr = xs[k + 1][:, :, TX:TX + width] if k < NCORES - 1 else z
        xh.append(np.ascontiguousarray(np.concatenate([xl, xs[k][:, :, 0:TX], xr], axis=2)))
        chh.append(np.ascontiguousarray(np.concatenate([cl, xs[k][:, :, TX:NTOK], cr], axis=2)))
        m = np.zeros((128, 4), np.float32)
        m[:, 0] = m[:, 2] = 1.0 if k > 0 else 0.0
        m[:, 1] = m[:, 3] = 1.0 if k < NCORES - 1 else 0.0
        hmask.append(m)
    return xh, chh, hmask


_RWA_NC = {}


def run_rwa(xs, inp, mods, layer, j, vf):
    vres = j > 0
    if vres not in _RWA_NC:
        _RWA_NC[vres] = build_rwa(vres)
    xh, chh, hmask = halo_cols(xs, 1)
    modp = np.ascontiguousarray(mods[layer][:, 3:6])
    gn = np.ascontiguousarray(np.stack([fm_vec(inp['norm_pre'][layer, 1]), fm_vec(inp['norm_post'][layer, 1])], axis=1))
    v0 = inp['rw_v0'][j - 1] if vres else np.zeros(D, np.float32)
    vl = [inp['rw_mu'][j][m] for m in range(6)] + [inp['rw_w0'][j][0], inp['rw_w0'][j][1], inp['rw_a0'][j][0], inp['rw_a0'][j][1],
                                                    v0, inp['rw_k_k'][j], inp['rw_k_a'][j], inp['rw_r_k'][j].reshape(D)]
    vecs = np.ascontiguousarray(np.stack([fm_vec(v) for v in vl], axis=1))
    base = {"modp": modp, "gn": gn, "vecs": vecs, "bones": block_ones(),
            "w_r": inp['rw_w_r'][j], "w_k": inp['rw_w_k'][j], "w_v": inp['rw_w_v'][j],
            "w1": inp['rw_w1'][j], "a1": inp['rw_a1'][j], "w2": inp['rw_w2'][j], "a2": inp['rw_a2'][j],
            "g1": inp['rw_g1'][j], "g2": inp['rw_g2'][j]}
    if vres:
        base.update({"v1": inp['rw_v1'][j - 1], "v2": inp['rw_v2'][j - 1]})
    maps = []
    for k in range(NCORES):
        m = dict(base)
        m.update({"xh": xh[k], "chh": chh[k], "hmask": hmask[k]})
        if vres:
            m["vf"] = vf[k]
        maps.append(m)
    res = _run(_RWA_NC[vres], maps)
    return [{nm: r[f"o_{nm}"] for nm in SCAN_NAMES} for r in res]


NT_T = 256
GN_EPS = 64e-5
_TAIL_NC = {}


def build_tail(front):
    c = Ctx("tail")
    p = c.p
    xin = c.din("x", [128, DC, NTOK])
    xout = c.dout("xo", [128, DC, NTOK])
    wo_d = c.din("wo", [D, D])
    modp = c.din("modp", [128, 3, DC, 2])
    gn = c.din("gn", [128, 2, DC])
    if front == 'rwkv':
        yd = [c.din(f"y{d}", [128, DC, NTOK]) for d in range(2)]
        g_d = c.din("g", [128, DC, NTOK])
        bon_d = c.din("bon", [128, DC, NTOK])
        lnv_d = c.din("lnv", [128, 2, DC])
        b64_d = c.din("b64", [128, 128])
    else:
        z_d = c.din("z", [128, DC, NTOK])
    c.init_psum(split=1)
    load_consts(c)
    mod_prologue(c, modp, gn, 1.0)
    wo, kwo = load_w_bf16(c, "wo", wo_d, D, D)
    xt = [c.sb(f"xt{b}", [128, DC, NT_T]) for b in range(2)]
    zb = c.sb("zb", [128, DC, NT_T], BF16)
    ysb = c.sb("ysb", [128, DC, NT_T])
    sq = c.sb("sq", [128, DC, NT_T])
    rs = c.sb("rs", [128, NT_T])
    tmpa = c.sb("tmpa", [128, NT_T])
    if front == 'rwkv':
        yt = [c.sb(f"yt{d}", [128, DC, NT_T]) for d in range(2)]
        gt = c.sb("gt", [128, DC, NT_T])
        bt = c.sb("bt", [128, DC, NT_T])
        lnv = c.sb("lnv", [128, 2, DC])
        b64 = c.sb("b64", [128, 128])
        gne = c.sb("gne", [128, 1])
        c.dma(lnv[:], lnv_d[:, :, :], writes=['lnv'])
        c.dma(b64[:], b64_d[:, :], writes=['b64'])
        p.op('pool', lambda e: e.memset(gne[:], GN_EPS), writes=['gne'])
        R = Rot(c, "sc", [128, NT_T], F32, 12)
    else:
        zt = c.sb("zt", [128, DC, NT_T])
    tiles = [(i * NT_T, NT_T, 0) for i in range(TX // NT_T)] + [(TX, TCX, 1)]

    def do_tile(ti, col0, n, var):
        b = ti % 2
        xk = f'xt{b}'
        c.dma(xt[b][:, :, 0:n], xin[:, :, col0:col0 + n], writes=[xk])
        if front == 'rwkv':
            for d in range(2):
                c.dma(yt[d][:, :, 0:n], yd[d][:, :, col0:col0 + n], writes=[f'yt{d}'])
            c.dma(gt[:, :, 0:n], g_d[:, :, col0:col0 + n], writes=['gt'])
            c.dma(bt[:, :, 0:n], bon_d[:, :, col0:col0 + n], writes=['bt'])

            def do_oc(oc):
                acc, acc_k = R.get()

                def do_dir(d):
                    t, _, _, key = c.ps()
                    p.op('pe', lambda e: e.matmul(t[:, 0:n], lhsT=b64[:], rhs=yt[d][:, oc, 0:n], start=True, stop=True),
                         reads=['b64', f'yt{d}'], writes=[key])
                    yc, yc_k = R.get()
                    p.op('dve', lambda e: e.scalar_tensor_tensor(out=yc[:, 0:n], in0=t[:, 0:n], scalar=-1.0, in1=yt[d][:, oc, 0:n],
                                                                 op0=ALU.mult, op1=ALU.add),
                         reads=[f'yt{d}', key], writes=[yc_k])
                    s2, s2_k = R.get()
                    p.op('pool', lambda e: e.tensor_tensor(out=s2[:, 0:n], in0=yc[:, 0:n], in1=yc[:, 0:n], op=ALU.mult), reads=[yc_k], writes=[s2_k])
                    t2, _, _, key2 = c.ps()
                    p.op('pe', lambda e: e.matmul(t2[:, 0:n], lhsT=b64[:], rhs=s2[:, 0:n], start=True, stop=True),
                         reads=['b64', s2_k], writes=[key2])
                    p.op('act', lambda e: e.activation(out=s2[:, 0:n], in_=t2[:, 0:n], func=AF.Sqrt, bias=gne[:, 0:1]),
                         reads=[key2, 'gne'], writes=[s2_k])
                    p.op('dve', lambda e: e.reciprocal(out=s2[:, 0:n], in_=s2[:, 0:n]), reads=[s2_k], writes=[s2_k])
                    p.op('dve', lambda e: e.tensor_tensor(out=yc[:, 0:n], in0=yc[:, 0:n], in1=s2[:, 0:n], op=ALU.mult), reads=[yc_k, s2_k], writes=[yc_k])
                    if d == 0:
                        p.op('act', lambda e: e.activation(out=acc[:, 0:n], in_=yc[:, 0:n], func=AF.Identity,
                                                           scale=lnv[:, 0, oc:oc + 1], bias=lnv[:, 1, oc:oc + 1]),
                             reads=[yc_k, 'lnv'], writes=[acc_k])
                    else:
                        p.op('act', lambda e: e.activation(out=yc[:, 0:n], in_=yc[:, 0:n], func=AF.Identity,
                                                           scale=lnv[:, 0, oc:oc + 1], bias=lnv[:, 1, oc:oc + 1]),
                             reads=[yc_k, 'lnv'], writes=[yc_k])
                        p.op('pool', lambda e: e.tensor_tensor(out=acc[:, 0:n], in0=acc[:, 0:n], in1=yc[:, 0:n], op=ALU.add),
                             reads=[acc_k, yc_k], writes=[acc_k])
                for d in range(2):
                    do_dir(d)
                p.op('pool', lambda e: e.tensor_tensor(out=acc[:, 0:n], in0=acc[:, 0:n], in1=bt[:, oc, 0:n], op=ALU.add),
                     reads=[acc_k, 'bt'], writes=[acc_k])
                p.op('dve', lambda e: e.tensor_tensor(out=zb[:, oc, 0:n], in0=acc[:, 0:n], in1=gt[:, oc, 0:n], op=ALU.mult),
                     reads=[acc_k, 'gt'], writes=[f'zb{oc}'])
            for oc in range(DC):
                do_oc(oc)
        else:
            c.dma(zt[:, :, 0:n], z_d[:, :, col0:col0 + n], writes=['zt'])
            for oc in range(DC):
                if oc % 2:
                    p.op('act', lambda e, oc=oc: e.copy(out=zb[:, oc, 0:n], in_=zt[:, oc, 0:n]), reads=['zt'], writes=[f'zb{oc}'])
                else:
                    p.op('dve', lambda e, oc=oc: e.tensor_copy(out=zb[:, oc, 0:n], in_=zt[:, oc, 0:n]), reads=['zt'], writes=[f'zb{oc}'])
        for oc in range(DC):
            t, _, _, key = c.ps()
            for kc in range(DC):
                p.op('pe', lambda e, kc=kc, oc=oc, t=t: e.matmul(t[:, 0:n], lhsT=wo[:, kc, oc * 128:(oc + 1) * 128], rhs=zb[:, kc, 0:n],
                                                                 start=(kc == 0), stop=(kc == DC - 1)), reads=[kwo[kc], f'zb{kc}'], writes=[key])
            if oc % 2:
                p.op('act', lambda e, oc=oc, t=t: e.copy(out=ysb[:, oc, 0:n], in_=t[:, 0:n]), reads=[key], writes=[f'y{oc}'])
            else:
                p.op('dve', lambda e, oc=oc, t=t: e.tensor_copy(out=ysb[:, oc, 0:n], in_=t[:, 0:n]), reads=[key], writes=[f'y{oc}'])
        p.op('pool', lambda e: e.memset(tmpa[:, 0:1], 0.0), reads=[f'y{oc}' for oc in range(DC)], writes=['yall', 'tmpa'])
        emit_post(c, ysb, 'yall', xt[b], xk, n, var, sq, 'sq', rs, 'rs', tmpa, 'tmpa')
        c.dma(xout[:, :, col0:col0 + n], xt[b][:, :, 0:n], reads=[xk], writes=['xout'])

    for ti, (col0, n, var) in enumerate(tiles):
        do_tile(ti, col0, n, var)
    return c.finish()


def run_tail(front, xs, inp, mods, layer, wo, extra):
    if front not in _TAIL_NC:
        _TAIL_NC[front] = build_tail(front)
    modp = np.ascontiguousarray(mods[layer][:, 3:6])
    gn = np.ascontiguousarray(np.stack([fm_vec(inp['norm_pre'][layer, 1]), fm_vec(inp['norm_post'][layer, 1])], axis=1))
    maps = []
    for k in range(NCORES):
        m = {"x": xs[k], "wo": wo, "modp": modp, "gn": gn}
        for nm, v in extra.items():
            m[nm] = v[k] if isinstance(v, list) else v
        maps.append(m)
    res = _run(_TAIL_NC[front], maps)
    return [r["xo"] for r in res]


NT_B = 256
NH = 16
QR, KVR = 384, 256
_MLAP_NC = None


def rope_consts():
    def perm(base, size):
        m = np.zeros((size, size), np.float32)
        for r in range(32):
            i = r % 16
            if i < 8:
                m[base + r + 8, base + r] = -1.0
            else:
                m[base + r - 8, base + r] = 1.0
        return m
    return perm(64, 96), perm(0, 32)


def rope_tables():
    t = np.arange(SEQ)
    row = (t // 64).astype(np.float32)
    col = (t % 64).astype(np.float32)
    inv = (10000.0 ** (-np.arange(0, 16, 2, dtype=np.float32) / 16)).astype(np.float32)
    ang = np.zeros((32, SEQ), np.float32)
    for r in range(32):
        f = (r % 16) % 8
        ang[r] = (row if r < 16 else col) * inv[f]
    return np.cos(ang).astype(np.float32), np.sin(ang).astype(np.float32)


def build_mla_proj():
    c = Ctx("mlap")
    p = c.p
    xin = c.din("x", [128, DC, NTOK])
    modp = c.din("modp", [128, 3, DC, 2])
    gn = c.din("gn", [128, 2, DC])
    wdq_d = c.din("w_dq", [D, QR])
    wuq_d = c.din("w_uq", [QR, NH * 96])
    wdkv_d = c.din("w_dkv", [D, KVR + 32])
    wukv_d = c.din("w_ukv", [KVR, NH * 128])
    qn_d = c.din("qn", [128, 3])
    kvn_d = c.din("kvn", [128, 2])
    cos96_d, sin96_d = c.din("cos96", [96, NTOK]), c.din("sin96", [96, NTOK])
    cos32_d, sin32_d = c.din("cos32", [32, NTOK]), c.din("sin32", [32, NTOK])
    p96_d, p32_d = c.din("p96", [96, 96]), c.din("p32", [32, 32])
    q_o = c.dout("q_o", [NH, 96, NTOK], BF16)
    kr_o = c.dout("kr_o", [32, NTOK], BF16)
    kv_o = c.dout("kv_o", [NH, 128, NTOK], BF16)
    c.init_psum(split=1)
    load_consts(c)
    mod_prologue(c, modp, gn, 1.0)
    wdq, kwdq = load_w_bf16(c, "wdq", wdq_d, D, QR)
    wuq, kwuq = load_w_bf16(c, "wuq", wuq_d, QR, NH * 96)
    wdkv, kwdkv = load_w_bf16(c, "wdkv", wdkv_d, D, KVR + 32)
    wukv, kwukv = load_w_bf16(c, "wukv", wukv_d, KVR, NH * 128)
    qn = c.sb("qn", [128, 3])
    kvn = c.sb("kvn", [128, 2])
    p96 = c.sb("p96", [96, 96])
    p32 = c.sb("p32", [32, 32])
    c.dma(qn[:], qn_d[:, :], writes=['qn'])
    c.dma(kvn[:], kvn_d[:, :], writes=['kvn'])
    c.dma(p96[:], p96_d[:, :], writes=['p96'])
    c.dma(p32[:], p32_d[:, :], writes=['p32'])
    xt = [c.sb(f"xt{b}", [128, DC, NT_B]) for b in range(2)]
    sq = c.sb("sq", [128, DC, NT_B])
    hb = c.sb("hb", [128, DC, NT_B], BF16)
    rs = c.sb("rs", [128, NT_B])
    tmpa = c.sb("tmpa", [128, NT_B])
    cq = c.sb("cq", [128, 3, NT_B])
    cqn = c.sb("cqn", [128, 3, NT_B], BF16)
    lat = c.sb("lat", [128, 2, NT_B])
    latn = c.sb("latn", [128, 2, NT_B], BF16)
    kr32 = c.sb("kr32", [32, NT_B])
    cs96 = [c.sb(f"cs96_{i}", [96, NT_B]) for i in range(2)]
    cs32 = [c.sb(f"cs32_{i}", [32, NT_B]) for i in range(2)]
    Rq = Rot(c, "q32", [96, NT_B], F32, 3)
    Rt = Rot(c, "qt", [96, NT_B], F32, 4)
    Rqo = Rot(c, "qo", [96, NT_B], BF16, 3)
    Rkv = Rot(c, "kvo", [128, NT_B], BF16, 3)
    kro = c.sb("kro", [32, NT_B], BF16)
    tiles = [(i * NT_B, NT_B, 0) for i in range(TX // NT_B)] + [(TX, TCX, 1)]

    def proj(wt, kw, kcn, c0, mcols, rhs, rkey, n):
        t, _, _, key = c.ps()
        for kc in range(kcn):
            p.op('pe', lambda e, kc=kc: e.matmul(t[0:mcols, 0:n], lhsT=wt[:, kc, c0:c0 + mcols], rhs=rhs[:, kc, 0:n],
                                                 start=(kc == 0), stop=(kc == kcn - 1)), reads=[kw[kc], rkey], writes=[key])
        return t, key

    def rope(src, skey, perm, pkey, rows, cos, sin, ckey, dst, dkey, n):
        t, _, _, key = c.ps()
        p.op('pe', lambda e: e.matmul(t[0:rows, 0:n], lhsT=perm[:], rhs=src[0:rows, 0:n], start=True, stop=True), reads=[pkey, skey], writes=[key])
        t1, t1k = Rt.get()
        t2, t2k = Rt.get()
        p.op('pool', lambda e: e.tensor_tensor(out=t1[0:rows, 0:n], in0=src[0:rows, 0:n], in1=cos[0:rows, 0:n], op=ALU.mult), reads=[skey, ckey], writes=[t1k])
        p.op('dve', lambda e: e.tensor_tensor(out=t2[0:rows, 0:n], in0=t[0:rows, 0:n], in1=sin[0:rows, 0:n], op=ALU.mult), reads=[key, ckey], writes=[t2k])
        p.op('dve', lambda e: e.tensor_tensor(out=dst[0:rows, 0:n], in0=t1[0:rows, 0:n], in1=t2[0:rows, 0:n], op=ALU.add), reads=[t1k, t2k], writes=[dkey])

    def do_tile(ti, col0, n, var):
        b = ti % 2
        xk = f'xt{b}'
        c.dma(xt[b][:, :, 0:n], xin[:, :, col0:col0 + n], writes=[xk])
        c.dma(cs96[0][:, 0:n], cos96_d[:, col0:col0 + n], writes=['cs96'])
        c.dma(cs96[1][:, 0:n], sin96_d[:, col0:col0 + n], writes=['cs96'])
        c.dma(cs32[0][:, 0:n], cos32_d[:, col0:col0 + n], writes=['cs32'])
        c.dma(cs32[1][:, 0:n], sin32_d[:, col0:col0 + n], writes=['cs32'])
        emit_pre(c, xt[b], xk, n, var, sq, 'sq', rs, 'rs', tmpa, 'tmpa', hb, 'hb')
        for qc in range(3):
            t, key = proj(wdq, kwdq, DC, qc * 128, 128, hb, 'hb', n)
            p.op('act', lambda e, qc=qc, t=t: e.copy(out=cq[:, qc, 0:n], in_=t[:, 0:n]), reads=[key], writes=['cq'])
        rms_rstd(c, cq[:, :, 0:n], 'cq', 3, n, QR, sq, 'sq', rs, 'rs', tmpa, 'tmpa')
        for qc in range(3):
            p.op('dve', lambda e, qc=qc: e.scalar_tensor_tensor(out=cqn[:, qc, 0:n], in0=cq[:, qc, 0:n], scalar=qn[:, qc:qc + 1], in1=rs[:, 0:n],
                                                                op0=ALU.mult, op1=ALU.mult), reads=['cq', 'qn', 'rs'], writes=['cqn'])
        for lc in range(2):
            t, key = proj(wdkv, kwdkv, DC, lc * 128, 128, hb, 'hb', n)
            p.op('act', lambda e, lc=lc, t=t: e.copy(out=lat[:, lc, 0:n], in_=t[:, 0:n]), reads=[key], writes=['lat'])
        t, key = proj(wdkv, kwdkv, DC, 256, 32, hb, 'hb', n)
        p.op('act', lambda e, t=t: e.copy(out=kr32[:, 0:n], in_=t[0:32, 0:n]), reads=[key], writes=['kr32'])
        rms_rstd(c, lat[:, :, 0:n], 'lat', 2, n, KVR, sq, 'sq', rs, 'rs', tmpa, 'tmpa')
        for lc in range(2):
            p.op('dve', lambda e, lc=lc: e.scalar_tensor_tensor(out=latn[:, lc, 0:n], in0=lat[:, lc, 0:n], scalar=kvn[:, lc:lc + 1], in1=rs[:, 0:n],
                                                                op0=ALU.mult, op1=ALU.mult), reads=['lat', 'kvn', 'rs'], writes=['latn'])
        rope(kr32, 'kr32', p32, 'p32', 32, cs32[0], cs32[1], 'cs32', kro, 'kro', n)
        c.dma(kr_o[:, col0:col0 + n], kro[:, 0:n], reads=['kro'], writes=['o_kr'])

        def do_head(hd):
            t, key = proj(wuq, kwuq, 3, hd * 96, 96, cqn, 'cqn', n)
            q32, q32k = Rq.get()
            p.op('act', lambda e: e.copy(out=q32[:, 0:n], in_=t[0:96, 0:n]), reads=[key], writes=[q32k])
            qo, qok = Rqo.get()
            rope(q32, q32k, p96, 'p96', 96, cs96[0], cs96[1], 'cs96', qo, qok, n)
            c.dma(q_o[hd, :, col0:col0 + n], qo[:, 0:n], reads=[qok], writes=['o_q'])
            t2, key2 = proj(wukv, kwukv, 2, hd * 128, 128, latn, 'latn', n)
            kvo, kvok = Rkv.get()
            p.op('act', lambda e: e.copy(out=kvo[:, 0:n], in_=t2[:, 0:n]), reads=[key2], writes=[kvok])
            c.dma(kv_o[hd, :, col0:col0 + n], kvo[:, 0:n], reads=[kvok], writes=['o_kv'])
        for hd in range(NH):
            do_head(hd)

    for ti, (col0, n, var) in enumerate(tiles):
        do_tile(ti, col0, n, var)
    return c.finish()


def run_mla_proj(xs, inp, mods, layer):
    global _MLAP_NC
    if _MLAP_NC is None:
        _MLAP_NC = build_mla_proj()
    modp = np.ascontiguousarray(mods[layer][:, 3:6])
    gn = np.ascontiguousarray(np.stack([fm_vec(inp['norm_pre'][layer, 1]), fm_vec(inp['norm_post'][layer, 1])], axis=1))
    p96, p32 = rope_consts()
    cos, sin = rope_tables()
    qn = np.ascontiguousarray(inp['mla_q_norm'][0].reshape(3, 128).T)
    kvn = np.ascontiguousarray(inp['mla_kv_norm'][0].reshape(2, 128).T)
    base = {"modp": modp, "gn": gn, "w_dq": inp['mla_w_dq'][0], "w_uq": inp['mla_w_uq'][0], "w_dkv": inp['mla_w_dkv'][0],
            "w_ukv": inp['mla_w_ukv'][0], "qn": qn, "kvn": kvn, "p96": p96, "p32": p32}
    maps = []
    for k in range(NCORES):
        c32 = np.ones((32, NTOK), np.float32)
        s32 = np.zeros((32, NTOK), np.float32)
        c32[:, 0:TX] = cos[:, k * TX:(k + 1) * TX]
        s32[:, 0:TX] = sin[:, k * TX:(k + 1) * TX]
        c96 = np.ones((96, NTOK), np.float32)
        s96 = np.zeros((96, NTOK), np.float32)
        c96[64:] = c32
        s96[64:] = s32
        m = dict(base)
        m.update({"x": xs[k], "cos96": c96, "sin96": s96, "cos32": c32, "sin32": s32})
        maps.append(m)
    res = _run(_MLAP_NC, maps)
    return [(r["q_o"], r["kr_o"], r["kv_o"]) for r in res]


LK = CTX + SEQ
NKT = LK // 128
ATT_SCALE = 96 ** -0.5
_ATT_NC = None


def build_attn():
    c = Ctx("attn")
    p = c.p
    q_d = c.din("q", [NH, 96, NTOK], BF16)
    k_d = c.din("k", [NH, 96, LK], BF16)
    v_d = c.din("va", [NH, 128, NKT, 65], BF16)
    sel_d = c.din("sel", [65, 64])
    o_d = c.dout("o", [NH, 64, NTOK])
    c.init_psum(split=1)
    sel = c.sb("sel", [65, 64])
    c.dma(sel[:], sel_d[:, :], writes=['sel'])
    kT = [c.sb(f"kT{b}", [96, LK], BF16) for b in range(2)]
    vv = [c.sb(f"vv{b}", [128, NKT, 65], BF16) for b in range(2)]
    qq = [c.sb(f"qq{b}", [96, NTOK], BF16) for b in range(2)]
    NP = 4
    P = [c.sb(f"P{i}", [128, 512], BF16) for i in range(NP)]
    osb = [c.sb(f"osb{i}", [65, 512]) for i in range(2)]
    rd = c.sb("rd", [64, 512])
    outt = [c.sb(f"outt{i}", [64, 512]) for i in range(2)]
    obanks = c.ps_bufs[0:2]
    sbanks = c.ps_bufs[2:8]
    st = {'s': 0, 'p': 0, 'o': 0, 'e': 0}
    qtiles = [(i * 512, 512, NKT) for i in range(TX // 512)] + [(TX, TCX, CTX // 128)]

    def do_head(h):
        b = h % 2
        c.dma(kT[b][:], k_d[h, :, :], writes=[f'kT{b}'])
        c.dma(vv[b][:], v_d[h, :, :, :], writes=[f'vv{b}'])
        c.dma(qq[b][:], q_d[h, :, :], writes=[f'qq{b}'])

        def do_qt(col0, n, nk):
            ot, _, _, okey = obanks[st['o'] % 2]
            st['o'] += 1
            sl = {}

            def S(kt):
                t, _, _, key = sbanks[st['s'] % len(sbanks)]
                st['s'] += 1
                p.op('pe', lambda e: e.matmul(t[:, 0:n], lhsT=kT[b][:, kt * 128:(kt + 1) * 128], rhs=qq[b][:, col0:col0 + n], start=True, stop=True),
                     reads=[f'kT{b}', f'qq{b}'], writes=[key])
                sl[kt] = (t, key)

            def EXP_PV(kt):
                t, key = sl.pop(kt)
                pi = st['p'] % NP
                st['p'] += 1
                p.op('act', lambda e: e.activation(out=P[pi][:, 0:n], in_=t[:, 0:n], func=AF.Exp, scale=ATT_SCALE), reads=[key], writes=[f'P{pi}'])
                p.op('pe', lambda e: e.matmul(ot[0:65, 0:n], lhsT=vv[b][:, kt, :], rhs=P[pi][:, 0:n], start=(kt == 0), stop=(kt == nk - 1)),
                     reads=[f'vv{b}', f'P{pi}'], writes=[okey])
            S(0)
            for kt in range(nk):
                if kt + 1 < nk:
                    S(kt + 1)
                EXP_PV(kt)
            ei = st['e'] % 2
            st['e'] += 1
            p.op('act', lambda e: e.copy(out=osb[ei][:, 0:n], in_=ot[0:65, 0:n]), reads=[okey], writes=[f'osb{ei}'])
            t, _, _, key = sbanks[st['s'] % len(sbanks)]
            st['s'] += 1
            p.op('pe', lambda e: e.matmul(t[0:64, 0:n], lhsT=sel[:], rhs=osb[ei][:, 0:n], start=True, stop=True), reads=['sel', f'osb{ei}'], writes=[key])
            p.op('dve', lambda e: e.reciprocal(out=rd[:, 0:n], in_=t[0:64, 0:n]), reads=[key], writes=['rd'])
            p.op('dve', lambda e: e.tensor_tensor(out=outt[ei][:, 0:n], in0=osb[ei][0:64, 0:n], in1=rd[:, 0:n], op=ALU.mult),
                 reads=[f'osb{ei}', 'rd'], writes=[f'outt{ei}'])
            c.dma(o_d[h, :, col0:col0 + n], outt[ei][:, 0:n], reads=[f'outt{ei}'], writes=['o_out'])
        for (col0, n, nk) in qtiles:
            do_qt(col0, n, nk)
    for h in range(NH):
        do_head(h)
    return c.finish()


def run_attn(proj_outs):
    global _ATT_NC
    if _ATT_NC is None:
        _ATT_NC = build_attn()
    bf = proj_outs[0][0].dtype
    kn = np.concatenate([po[2][:, 0:64, TX:NTOK] for po in proj_outs] + [po[2][:, 0:64, 0:TX] for po in proj_outs], axis=2)
    vv = np.concatenate([po[2][:, 64:128, TX:NTOK] for po in proj_outs] + [po[2][:, 64:128, 0:TX] for po in proj_outs], axis=2)
    kr = np.concatenate([po[1][:, TX:NTOK] for po in proj_outs] + [po[1][:, 0:TX] for po in proj_outs], axis=1)
    kT = np.ascontiguousarray(np.concatenate([kn, np.broadcast_to(kr[None], (NH, 32, LK))], axis=1))
    va = np.ones((NH, 128, NKT, 65), dtype=bf)
    va[:, :, :, 0:64] = vv.reshape(NH, 64, NKT, 128).transpose(0, 3, 2, 1)
    sel = np.zeros((65, 64), np.float32)
    sel[64, :] = 1.0
    maps = [{"q": np.ascontiguousarray(proj_outs[k][0]), "k": kT, "va": va, "sel": sel} for k in range(NCORES)]
    res = _run(_ATT_NC, maps)
    outs = []
    for r in res:
        o = r["o"].reshape(DC, 2, 64, NTOK).reshape(DC, 128, NTOK).transpose(1, 0, 2)
        outs.append(np.ascontiguousarray(o))
    return outs


NT_P = 256
HW = 8
_POOL_NC = None


def build_pool():
    c = Ctx("pool")
    p = c.p
    xh = c.din("xh", [128, DC, TX + 2 * HW])
    chh = c.din("chh", [128, DC, TCX + 2 * HW])
    xout = c.dout("xo", [128, DC, NTOK])
    hmask = c.din("hmask", [128, 4])
    modp = c.din("modp", [128, 3, DC, 2])
    gn = c.din("gn", [128, 2, DC])
    rc_d = c.din("rcnt", [128, 4, NTOK])
    pw_d = c.din("pw", [4, 256, 256])
    pv_d = c.din("pvec", [128, 2, DC])
    c.init_psum(split=1)
    load_consts(c)
    mod_prologue(c, modp, gn, 1.0)
    hm = c.sb("hm", [128, 4])
    c.dma(hm[:], hmask[:, :], writes=['hm'])
    pv = c.sb("pv", [128, 2, DC])
    c.dma(pv[:], pv_d[:, :, :], writes=['pv'])
    bsc = c.sb("bsc", [128, DC])
    p.op('dve', lambda e: e.tensor_tensor(out=bsc[:], in0=pv[:, 0, :], in1=pv[:, 1, :], op=ALU.mult), reads=['pv'], writes=['bsc'])
    pw = [load_w_bf16(c, f"pw{g}", pw_d[g], 256, 256) for g in range(4)]
    NW = NT_P + 2 * HW
    xt = [c.sb(f"xt{b}", [128, DC, NW]) for b in range(2)]
    sq = c.sb("sq", [128, DC, NW])
    hh = c.sb("hh", [128, DC, NW])
    s1 = c.sb("s1", [128, DC, NW])
    s4 = c.sb("s4", [128, DC, NW])
    s8 = c.sb("s8", [128, DC, NW])
    s16 = c.sb("s16", [128, DC, NW])
    rs = c.sb("rs", [128, NW])
    tmpa = c.sb("tmpa", [128, NW])
    rc = c.sb("rc", [128, 4, NT_P])
    mm = c.sb("mm", [128, DC, NT_P])
    df = c.sb("df", [128, DC, NT_P], BF16)
    ysb = c.sb("ysb", [128, DC, NT_P])
    tiles = [(xh, i * NT_P, NT_P, 0, i * NT_P, (0 if i == 0 else None, 1 if i == TX // NT_P - 1 else None)) for i in range(TX // NT_P)]
    tiles.append((chh, 0, TCX, 1, TX, (2, 3)))

    def do_tile(ti, src, col0, n, var, ocol, ml_, mr_):
        b = ti % 2
        xk = f'xt{b}'
        w = n + 2 * HW
        c.dma(xt[b][:, :, 0:w], src[:, :, col0:col0 + w], writes=[xk])
        c.dma(rc[:, :, 0:n], rc_d[:, :, ocol:ocol + n], writes=['rc'])
        emit_pre(c, xt[b], xk, w, var, sq, 'sq', rs, 'rs', tmpa, 'tmpa', hh, 'hh')
        if ml_ is not None:
            p.op('dve', lambda e: e.tensor_scalar(out=hh[:, :, 0:HW], in0=hh[:, :, 0:HW], scalar1=hm[:, ml_:ml_ + 1], scalar2=None, op0=ALU.mult),
                 reads=['hh', 'hm'], writes=['hh'])
        if mr_ is not None:
            p.op('dve', lambda e: e.tensor_scalar(out=hh[:, :, n + HW:w], in0=hh[:, :, n + HW:w], scalar1=hm[:, mr_:mr_ + 1], scalar2=None, op0=ALU.mult),
                 reads=['hh', 'hm'], writes=['hh'])
        p.op('pool', lambda e: e.tensor_tensor(out=s1[:, :, 1:w], in0=hh[:, :, 0:w - 1], in1=hh[:, :, 1:w], op=ALU.add), reads=['hh'], writes=['s1'])
        p.op('dve', lambda e: e.tensor_tensor(out=s4[:, 2:8, 2:w - 1], in0=s1[:, 2:8, 1:w - 2], in1=s1[:, 2:8, 3:w], op=ALU.add), reads=['s1'], writes=['s4'])
        p.op('pool', lambda e: e.tensor_tensor(out=s8[:, 4:8, 4:w - 3], in0=s4[:, 4:8, 2:w - 5], in1=s4[:, 4:8, 6:w - 1], op=ALU.add), reads=['s4'], writes=['s8'])
        p.op('dve', lambda e: e.tensor_tensor(out=s16[:, 6:8, 8:w - 7], in0=s8[:, 6:8, 4:w - 11], in1=s8[:, 6:8, 12:w - 3], op=ALU.add), reads=['s8'], writes=['s16'])
        for g, (sb_, sk) in enumerate(((s1, 's1'), (s4, 's4'), (s8, 's8'), (s16, 's16'))):
            cs = slice(2 * g, 2 * g + 2)
            p.op('dve', lambda e, g=g, sb_=sb_, cs=cs: e.tensor_tensor(out=mm[:, cs, 0:n], in0=sb_[:, cs, HW:HW + n],
                                                                      in1=rc[:, g, 0:n].unsqueeze(1).to_broadcast([128, 2, n]), op=ALU.mult),
                 reads=[sk, 'rc'], writes=[f'mm{g}'])
            p.op('pool', lambda e, g=g, cs=cs: e.tensor_tensor(out=df[:, cs, 0:n], in0=mm[:, cs, 0:n], in1=hh[:, cs, HW:HW + n], op=ALU.subtract),
                 reads=[f'mm{g}', 'hh'], writes=[f'df{g}'])
            for oo in range(2):
                oc = 2 * g + oo
                t, _, _, key = c.ps()
                for kk in range(2):
                    p.op('pe', lambda e, g=g, oo=oo, kk=kk, t=t: e.matmul(t[:, 0:n], lhsT=pw[g][0][:, kk, oo * 128:(oo + 1) * 128], rhs=df[:, 2 * g + kk, 0:n],
                                                                          start=(kk == 0), stop=(kk == 1)), reads=[pw[g][1][kk], f'df{g}'], writes=[key])
                p.op('act', lambda e, oc=oc, t=t: e.activation(out=ysb[:, oc, 0:n], in_=t[:, 0:n], func=AF.Identity,
                                                               scale=pv[:, 1, oc:oc + 1], bias=bsc[:, oc:oc + 1]),
                     reads=[key, 'pv', 'bsc'], writes=[f'y{oc}'])
        p.op('pool', lambda e: e.memset(tmpa[:, 0:1], 0.0), reads=[f'y{oc}' for oc in range(DC)], writes=['yall', 'tmpa'])
        emit_post(c, ysb, 'yall', xt[b], xk, n, var, sq, 'sq', rs, 'rs', tmpa, 'tmpa', xoff=HW)
        c.dma(xout[:, :, ocol:ocol + n], xt[b][:, :, HW:HW + n], reads=[xk], writes=['xout'])

    for ti, (src, col0, n, var, ocol, (ml_, mr_)) in enumerate(tiles):
        do_tile(ti, src, col0, n, var, ocol, ml_, mr_)
    return c.finish()


def pool_rcnt():
    out = []
    for T in (SEQ, CTX):
        t = np.arange(T)
        rows = []
        for win in (2, 4, 8, 16):
            lo = np.clip(t - win // 2, 0, T)
            hi = np.clip(t + win // 2, 0, T)
            rows.append((1.0 / (hi - lo).astype(np.float32)).astype(np.float32))
        out.append(np.stack(rows, axis=0))
    return out


def run_pool(xs, inp, mods, layer):
    global _POOL_NC
    if _POOL_NC is None:
        _POOL_NC = build_pool()
    xh, chh, hmask = halo_cols(xs, HW)
    modp = np.ascontiguousarray(mods[layer][:, 3:6])
    gn = np.ascontiguousarray(np.stack([fm_vec(inp['norm_pre'][layer, 1]), fm_vec(inp['norm_post'][layer, 1])], axis=1))
    rx, rcx = pool_rcnt()
    pvec = np.ascontiguousarray(np.stack([fm_vec(inp['pool_b'][0].reshape(D)), fm_vec(inp['pool_scale'][0])], axis=1))
    maps = []
    for k in range(NCORES):
        rc = np.concatenate([rx[:, k * TX:(k + 1) * TX], rcx[:, k * TCX:(k + 1) * TCX]], axis=1)
        rc = np.ascontiguousarray(np.broadcast_to(rc[None], (128, 4, NTOK)))
        maps.append({"xh": xh[k], "chh": chh[k], "hmask": hmask[k], "modp": modp, "gn": gn, "rcnt": rc,
                     "pw": inp['pool_w'][0], "pvec": pvec})
    res = _run(_POOL_NC, maps)
    return [r["xo"] for r in res]


_SCAN_NC = None


def _rev(a):
    return np.concatenate([a[..., 0:CTX][..., ::-1], a[..., CTX:][..., ::-1]], axis=-1)


def run_scan(pr):
    global _SCAN_NC
    if _SCAN_NC is None:
        _SCAN_NC = build_scan()
    nch = LSCAN // CH
    consts = scan_consts()

    def chan(nm, c):
        return np.concatenate([pr[k][nm][:, c, TX:NTOK] for k in range(NCORES)] + [pr[k][nm][:, c, 0:TX] for k in range(NCORES)], axis=1)
    maps = []
    for c in range(NCORES):
        kk, r, v = chan('kk', c), chan('r', c), chan('v', c)
        m = dict(consts)
        for d in range(2):
            arrs = [chan(f'ld{d}', c), chan(f'kd{d}', c), chan(f'b{d}', c), kk, r]
            vv = v
            if d == 1:
                arrs = [_rev(a) for a in arrs]
                vv = _rev(v)
            m[f"ch5_{d}"] = np.ascontiguousarray(np.stack(arrs, axis=1))
            m[f"vst_{d}"] = np.ascontiguousarray(vv.reshape(2, CH, nch, CH).transpose(2, 0, 3, 1).reshape(nch, 128, CH))
        maps.append(m)
    res = _run(_SCAN_NC, maps)
    ys = [[np.empty((128, DC, NTOK), np.float32) for _ in range(NCORES)] for _ in range(2)]
    for c in range(NCORES):
        for d in range(2):
            y = res[c][f"yo_{d}"].reshape(CH, nch, 2, CH).transpose(2, 0, 1, 3).reshape(128, LSCAN)
            if d == 1:
                y = _rev(y)
            for k in range(NCORES):
                ys[d][k][:, c, 0:TX] = y[:, CTX + k * TX:CTX + (k + 1) * TX]
                ys[d][k][:, c, TX:NTOK] = y[:, k * TCX:(k + 1) * TCX]
    return ys


def kernel(**inputs):
    inp = {k: np.asarray(v) for k, v in inputs.items()}
    mods = run_mod(inp)
    x = inp['x'][0]
    ctx = inp['ctx'][0]
    xs = [to_fm(np.concatenate([x[k * TX:(k + 1) * TX], ctx[k * TCX:(k + 1) * TCX]], axis=0)) for k in range(NCORES)]
    vf = None
    for i in range(4):
        kind, j = i % 3, i // 3
        xs = run_ffn(xs, inp, mods, i, 0)
        if kind == 0:
            pr = run_rwa(xs, inp, mods, i, j, vf)
            if j == 0:
                vf = [np.ascontiguousarray(q['v']) for q in pr]
            ys = run_scan(pr)
            lnv = np.ascontiguousarray(np.stack([fm_vec(inp['rw_ln_w'][j]), fm_vec(inp['rw_ln_b'][j])], axis=1))
            extra = {"y0": ys[0], "y1": ys[1], "g": [q['g'] for q in pr], "bon": [q['bon'] for q in pr],
                     "lnv": lnv, "b64": block_ones() / np.float32(64.0)}
            xs = run_tail('rwkv', xs, inp, mods, i, inp['rw_w_o'][j], extra)
        elif kind == 1:
            po = run_mla_proj(xs, inp, mods, i)
            o = run_attn(po)
            xs = run_tail('plain', xs, inp, mods, i, inp['mla_w_o'][0], {"z": o})
        else:
            xs = run_pool(xs, inp, mods, i)
        xs = run_ffn(xs, inp, mods, i, 1)
    out = np.concatenate([from_fm(xs[k][:, :, 0:TX]) for k in range(NCORES)], axis=0)
    return np.ascontiguousarray(out[None].astype(np.float32))
```

```python
import numpy as np
from contextlib import ExitStack
import concourse.bass as bass
import concourse.mybir as mybir
from concourse.bass_utils import run_bass_kernel_spmd

F32 = mybir.dt.float32
BF16 = mybir.dt.bfloat16
AF = mybir.ActivationFunctionType
ALU = mybir.AluOpType
AX = mybir.AxisListType

NCORES = 8
D = 1024
DC = 8
SEQ = 16384
CTX = 256
DFF = 2816
FC = 22
TX = SEQ // NCORES
TCX = CTX // NCORES
NTOK = TX + TCX
EPS = 1e-6
SEM_LIMIT = 30000
NDSLOT = 6


class Prog:
    ENG = ('pe', 'dve', 'act', 'pool', 'sp')

    def __init__(self, nc):
        self.nc = nc
        self.ops = {e: [] for e in self.ENG}
        self.cnt = {}
        self.epoch = {}
        self.seen = {e: {} for e in self.ENG}
        self.last_w = {}
        self.readers = {}
        self.semnames = []
        self.n_ops = 0
        self.dslot = {}
        self.dlast = {}

    def _tok(self, kind, eng, inc):
        base = kind + eng
        ep = self.epoch.get(base, 0)
        name = f"{base}{ep}"
        c = self.cnt.get(name, 0) + inc
        if c > SEM_LIMIT:
            ep += 1
            self.epoch[base] = ep
            name = f"{base}{ep}"
            c = inc
        self.cnt[name] = c
        if name not in self.semnames:
            self.semnames.append(name)
        return (name, c)

    def op(self, eng, fn, reads=(), writes=(), dma=False):
        need = {}

        def add(tok):
            if tok is None:
                return
            s, v = tok
            if eng == 'pe' and s.startswith('cpe') and not dma:
                return
            if need.get(s, 0) < v:
                need[s] = v
        for k in reads:
            add(self.last_w.get(k))
        for k in writes:
            add(self.last_w.get(k))
            for s, v in self.readers.get(k, {}).items():
                add((s, v))
        waits = []
        for s, v in need.items():
            if self.seen[eng].get(s, 0) < v:
                waits.append((s, v))
                self.seen[eng][s] = v
        if dma:
            slot = self.dslot.get(eng, 0)
            self.dslot[eng] = (slot + 1) % NDSLOT
            kind = f'd{slot}'
            prev = self.dlast.get((eng, slot))
            if prev is not None and self.seen[eng].get(prev[0], 0) < prev[1]:
                waits.append(prev)
                self.seen[eng][prev[0]] = prev[1]
            tok = self._tok(kind, eng, 16)
            self.dlast[(eng, slot)] = tok
        else:
            tok = self._tok('c', eng, 1)
        for k in writes:
            self.last_w[k] = tok
            self.readers[k] = {}
        for k in reads:
            r = self.readers.setdefault(k, {})
            if r.get(tok[0], 0) < tok[1]:
                r[tok[0]] = tok[1]
        self.ops[eng].append((waits, fn, tok, dma))
        self.n_ops += 1
        return tok

    def emit(self):
        nc = self.nc
        with ExitStack() as st:
            sems = {n: st.enter_context(nc.semaphore(n)) for n in self.semnames}
            block = st.enter_context(nc.Block())
            finals = [(n, self.cnt[n]) for n in self.semnames]

            def mk(engname):
                def body(e):
                    for waits, fn, tok, dma in self.ops[engname]:
                        for s, v in waits:
                            e.wait_ge(sems[s], v)
                        ins = fn(e)
                        ins.then_inc(sems[tok[0]], 16 if dma else 1)
                    if engname == 'sp':
                        for n, v in finals:
                            e.wait_ge(sems[n], v)
                return body
            block.tensor(mk('pe'))
            block.vector(mk('dve'))
            block.scalar(mk('act'))
            block.gpsimd(mk('pool'))
            block.sync(mk('sp'))


class Ctx:
    def __init__(self, name):
        self.nc = bass.Bass("TRN2", target_bir_lowering=False)
        self.st = ExitStack()
        self.p = Prog(self.nc)
        self.uid = 0
        self.psum_banks = []
        self.ps_rr = 0
        self.dq = 0

    def din(self, name, shape, dt=F32):
        return self.nc.dram_tensor(name, list(shape), dt, kind="ExternalInput").ap()

    def dout(self, name, shape, dt=F32):
        return self.nc.dram_tensor(name, list(shape), dt, kind="ExternalOutput").ap()

    def sb(self, name, shape, dt=F32):
        return self.st.enter_context(self.nc.sbuf_tensor('s_' + name, list(shape), dt))

    def init_psum(self, split=2):
        self.ps_bufs = []
        for b in range(8):
            t = self.st.enter_context(self.nc.psum_tensor(f"psb{b}", [128, 512], F32))
            w = 512 // split
            for h in range(split):
                self.ps_bufs.append((t, h * w, w, f"ps{b}_{h}"))

    def ps(self):
        t, off, w, key = self.ps_bufs[self.ps_rr % len(self.ps_bufs)]
        self.ps_rr += 1
        return t, off, w, key

    def dma(self, out, in_, reads=(), writes=(), q=None):
        if q is None:
            q = 'sp'
        self.p.op(q, lambda e: e.dma_start(out=out, in_=in_), reads=reads, writes=writes, dma=True)

    def finish(self):
        self.p.emit()
        self.st.close()
        return self.nc


def _run(nc, in_maps):
    res = run_bass_kernel_spmd(nc, in_maps, core_ids=list(range(NCORES)))
    return res.results


_MOD_NC = None


def build_mod():
    c = Ctx("mod")
    p = c.p
    cin = c.din("cin", [128, 8, 2])
    mw = c.din("mw", [4, 1024, 1152])
    mb = c.din("mb", [128, 4, 9])
    out = c.dout("mo", [128, 4, 9, 2])
    c.init_psum(split=1)
    cs = c.sb("cs", [128, 8, 2])
    ss = c.sb("ss", [128, 8, 2])
    sg = c.sb("sg", [128, 8, 2])
    bs = c.sb("bs", [128, 4, 9])
    os_ = c.sb("os", [128, 4, 9, 2])
    ws = [c.sb(f"w{l}", [128, 8, 1152]) for l in range(4)]
    c.dma(cs[:], cin[:, :, :], writes=['cs'])
    c.dma(bs[:], mb[:, :, :], writes=['bs'])
    for l in range(4):
        for kc in range(8):
            q = ('sp', 'pool')[kc % 2]
            c.dma(ws[l][:, kc, :], mw[l, kc * 128:(kc + 1) * 128, :], writes=[f'w{l}_{kc}'], q=q)
    p.op('act', lambda e: e.activation(out=sg[:], in_=cs[:], func=AF.Sigmoid), reads=['cs'], writes=['sg'])
    p.op('dve', lambda e: e.tensor_tensor(out=ss[:], in0=cs[:], in1=sg[:], op=ALU.mult), reads=['cs', 'sg'], writes=['ss'])
    for l in range(4):
        for oc in range(9):
            t, off, w, key = c.ps()
            for kc in range(8):
                p.op('pe', lambda e, l=l, oc=oc, kc=kc, t=t, off=off: e.matmul(
                    t[:, off:off + 2], lhsT=ws[l][:, kc, oc * 128:(oc + 1) * 128], rhs=ss[:, kc, :],
                    start=(kc == 0), stop=(kc == 7)),
                    reads=['ss', f'w{l}_{kc}'], writes=[key])
            p.op('dve', lambda e, l=l, oc=oc, t=t, off=off: e.tensor_scalar(
                out=os_[:, l, oc, :], in0=t[:, off:off + 2], scalar1=bs[:, l, oc:oc + 1], scalar2=None, op0=ALU.add),
                reads=[key, 'bs'], writes=['os'])
    c.dma(out[:, :, :, :], os_[:], reads=['os'])
    return c.finish()


def run_mod(inp):
    global _MOD_NC
    if _MOD_NC is None:
        _MOD_NC = build_mod()
    cin = np.stack([inp['c'].reshape(D), inp['c_ctx'].reshape(D)], axis=-1)
    cin = np.ascontiguousarray(cin.reshape(8, 128, 2).transpose(1, 0, 2))
    maps = []
    for core in range(NCORES):
        mw = np.ascontiguousarray(inp['mod_w'][:, :, core * 1152:(core + 1) * 1152])
        mb = inp['mod_b'][:, core * 1152:(core + 1) * 1152].reshape(4, 9, 128)
        mb = np.ascontiguousarray(mb.transpose(2, 0, 1))
        maps.append({"cin": cin, "mw": mw, "mb": mb})
    res = _run(_MOD_NC, maps)
    mo = np.concatenate([r["mo"] for r in res], axis=2)
    return [np.ascontiguousarray(mo[:, l].reshape(128, 9, 8, 2)) for l in range(4)]


def load_consts(c):
    ones = c.sb("ones", [128, 128])
    epsc = c.sb("epsc", [128, 1])
    c.p.op('pool', lambda e: e.memset(ones[:], 1.0), writes=['ones'])
    c.p.op('pool', lambda e: e.memset(epsc[:], EPS), writes=['epsc'])
    c.ones, c.epsc = ones, epsc


def rms_rstd(c, src, skey, nch, n, dn, sq, sqkey, rstd, rkey, tmp, tkey, eps_ap=None, sq_eng='act'):
    p = c.p
    if sq_eng == 'act':
        p.op('act', lambda e: e.activation(out=sq[:, 0:nch, 0:n], in_=src, func=AF.Square), reads=[skey], writes=[sqkey])
    else:
        p.op('pool', lambda e: e.tensor_tensor(out=sq[:, 0:nch, 0:n], in0=src, in1=src, op=ALU.mult), reads=[skey], writes=[sqkey])
    t, off, w, key = c.ps()
    for kc in range(nch):
        p.op('pe', lambda e, kc=kc: e.matmul(t[:, off:off + n], lhsT=c.ones[:], rhs=sq[:, kc, 0:n],
                                             start=(kc == 0), stop=(kc == nch - 1)),
             reads=[sqkey, 'ones'], writes=[key])
    ea = c.epsc if eps_ap is None else eps_ap
    p.op('act', lambda e: e.activation(out=tmp[:, 0:n], in_=t[:, off:off + n], func=AF.Sqrt, scale=1.0 / dn, bias=ea[:, 0:1]),
         reads=[key, 'epsc'], writes=[tkey])
    p.op('dve', lambda e: e.reciprocal(out=rstd[:, 0:n], in_=tmp[:, 0:n]), reads=[tkey], writes=[rkey])


NT_F = 256
_FFN_NC = None


def build_ffn():
    c = Ctx("ffn")
    p = c.p
    xin = c.din("x", [128, DC, NTOK])
    xout = c.dout("xo", [128, DC, NTOK])
    wg = c.din("wg", [D, DFF])
    wu = c.din("wu", [D, DFF])
    wd = c.din("wd", [DFF, D])
    modp = c.din("modp", [128, 3, DC, 2])
    gn = c.din("gn", [128, 2, DC])
    c.init_psum(split=1)
    load_consts(c)
    wgs = c.sb("wgs", [128, DC, DFF], BF16)
    wus = c.sb("wus", [128, DC, DFF], BF16)
    wds = c.sb("wds", [128, FC, D], BF16)
    mps = c.sb("mps", [128, 3, DC, 2])
    gns = c.sb("gns", [128, 2, DC])
    gs = c.sb("gs", [128, DC, 2])
    gp = c.sb("gp", [128, DC, 2])
    xt = [c.sb(f"xt{b}", [128, DC, NT_F]) for b in range(2)]
    sqa = c.sb("sqa", [128, DC, NT_F])
    sqb = c.sb("sqb", [128, DC, NT_F])
    hb = [c.sb(f"h{b}", [128, DC, NT_F], BF16) for b in range(2)]
    hact = c.sb("hact", [128, FC, NT_F], BF16)
    sg = [c.sb(f"sg{b}", [128, NT_F], BF16) for b in range(3)]
    ysb = c.sb("ysb", [128, DC, NT_F])
    rs = [c.sb(f"rs{b}", [128, NT_F]) for b in range(2)]
    rs2 = c.sb("rs2", [128, NT_F])
    tmpa = c.sb("tmpa", [128, NT_F])
    tmpb = c.sb("tmpb", [128, NT_F])

    c.dma(mps[:], modp[:, :, :, :], writes=['mps'])
    c.dma(gns[:], gn[:, :, :], writes=['gns'])
    for kc in range(DC):
        c.dma(wgs[:, kc, :], wg[kc * 128:(kc + 1) * 128, :], writes=[f'wg{kc}'], q='pool')
        c.dma(wus[:, kc, :], wu[kc * 128:(kc + 1) * 128, :], writes=[f'wu{kc}'], q='pool')
    for fc in range(FC):
        c.dma(wds[:, fc, :], wd[fc * 128:(fc + 1) * 128, :], writes=[f'wd{fc}'], q='pool')
    p.op('dve', lambda e: e.tensor_scalar(out=gs[:], in0=mps[:, 1, :, :], scalar1=1.0, scalar2=None, op0=ALU.add),
         reads=['mps'], writes=['gs'])
    p.op('dve', lambda e: e.tensor_tensor(out=gs[:], in0=gs[:], in1=gns[:, 0, :].unsqueeze(2).to_broadcast([128, DC, 2]), op=ALU.mult),
         reads=['gs', 'gns'], writes=['gs'])
    p.op('dve', lambda e: e.tensor_scalar(out=gp[:], in0=mps[:, 2, :, :], scalar1=0.5, scalar2=None, op0=ALU.mult),
         reads=['mps'], writes=['gp'])
    p.op('dve', lambda e: e.tensor_tensor(out=gp[:], in0=gp[:], in1=gns[:, 1, :].unsqueeze(2).to_broadcast([128, DC, 2]), op=ALU.mult),
         reads=['gp', 'gns'], writes=['gp'])

    tiles = [(i * NT_F, NT_F, 0) for i in range(TX // NT_F)] + [(TX, TCX, 1)]

    def pre(i):
        col0, n, var = tiles[i]
        b = i % 2
        c.dma(xt[b][:, :, 0:n], xin[:, :, col0:col0 + n], writes=[f'xt{b}'])
        rms_rstd(c, xt[b][:, :, 0:n], f'xt{b}', DC, n, D, sqa, 'sqa', rs[b], f'rs{b}', tmpa, 'tmpa')
        p.op('dve', lambda e: e.tensor_tensor(out=sqa[:, :, 0:n], in0=xt[b][:, :, 0:n],
                                              in1=rs[b][:, 0:n].unsqueeze(1).to_broadcast([128, DC, n]), op=ALU.mult),
             reads=[f'xt{b}', f'rs{b}'], writes=['sqa'])
        for kc in range(DC):
            p.op('act', lambda e, kc=kc: e.activation(out=hb[b][:, kc, 0:n], in_=sqa[:, kc, 0:n], func=AF.Identity,
                                                      scale=gs[:, kc, var:var + 1], bias=mps[:, 0, kc, var:var + 1]),
                 reads=['sqa', 'gs', 'mps'], writes=[f'h{b}'])

    def phase1(i):
        col0, n, var = tiles[i]
        b = i % 2
        for fc in range(FC):
            tg, og, _, kg = c.ps()
            tu, ou, _, ku = c.ps()
            for kc in range(DC):
                p.op('pe', lambda e, kc=kc, fc=fc, tg=tg, og=og: e.matmul(
                    tg[:, og:og + n], lhsT=wgs[:, kc, fc * 128:(fc + 1) * 128], rhs=hb[b][:, kc, 0:n],
                    start=(kc == 0), stop=(kc == DC - 1)), reads=[f'h{b}', f'wg{kc}'], writes=[kg])
            for kc in range(DC):
                p.op('pe', lambda e, kc=kc, fc=fc, tu=tu, ou=ou: e.matmul(
                    tu[:, ou:ou + n], lhsT=wus[:, kc, fc * 128:(fc + 1) * 128], rhs=hb[b][:, kc, 0:n],
                    start=(kc == 0), stop=(kc == DC - 1)), reads=[f'h{b}', f'wu{kc}'], writes=[ku])
            s = sg[fc % 3]
            sk = f'sg{fc % 3}'
            p.op('act', lambda e, tg=tg, og=og, s=s: e.activation(out=s[:, 0:n], in_=tg[:, og:og + n], func=AF.Silu),
                 reads=[kg], writes=[sk])
            p.op('dve', lambda e, tu=tu, ou=ou, s=s, fc=fc: e.tensor_tensor(out=hact[:, fc, 0:n], in0=tu[:, ou:ou + n], in1=s[:, 0:n], op=ALU.mult),
                 reads=[sk, ku], writes=[f'hact{fc}'])

    def phase2(i):
        col0, n, var = tiles[i]
        for dc in range(DC):
            t, o, _, k = c.ps()
            for fc in range(FC):
                p.op('pe', lambda e, fc=fc, dc=dc, t=t, o=o: e.matmul(
                    t[:, o:o + n], lhsT=wds[:, fc, dc * 128:(dc + 1) * 128], rhs=hact[:, fc, 0:n],
                    start=(fc == 0), stop=(fc == FC - 1)), reads=[f'hact{fc}', f'wd{fc}'], writes=[k])
            if dc % 2 == 0:
                p.op('act', lambda e, dc=dc, t=t, o=o: e.copy(out=ysb[:, dc, 0:n], in_=t[:, o:o + n]), reads=[k], writes=[f'y{dc}'])
            else:
                p.op('dve', lambda e, dc=dc, t=t, o=o: e.tensor_copy(out=ysb[:, dc, 0:n], in_=t[:, o:o + n]), reads=[k], writes=[f'y{dc}'])

    def post(i):
        col0, n, var = tiles[i]
        b = i % 2
        ykeys = [f'y{dc}' for dc in range(DC)]
        p.op('pool', lambda e: e.memset(tmpb[:, 0:1], 0.0), reads=ykeys, writes=['yall'])
        rms_rstd(c, ysb[:, :, 0:n], 'yall', DC, n, D, sqb, 'sqb', rs2, 'rs2', tmpb, 'tmpb2')
        p.op('dve', lambda e: e.tensor_tensor(out=sqb[:, :, 0:n], in0=ysb[:, :, 0:n],
                                              in1=rs2[:, 0:n].unsqueeze(1).to_broadcast([128, DC, n]), op=ALU.mult),
             reads=['yall', 'rs2'], writes=['sqb', 'yall'])
        for kc in range(DC):
            p.op('dve', lambda e, kc=kc: e.scalar_tensor_tensor(
                out=xt[b][:, kc, 0:n], in0=sqb[:, kc, 0:n], scalar=gp[:, kc, var:var + 1], in1=xt[b][:, kc, 0:n],
                op0=ALU.mult, op1=ALU.add), reads=['sqb', 'gp', f'xt{b}'], writes=[f'xt{b}'])
        c.dma(xout[:, :, col0:col0 + n], xt[b][:, :, 0:n], reads=[f'xt{b}'], writes=['xout'])

    nt = len(tiles)
    pre(0)
    for i in range(nt):
        phase1(i)
        if i + 1 < nt:
            pre(i + 1)
        phase2(i)
        post(i)
    return c.finish()


def to_fm(a):
    T = a.shape[0]
    return np.ascontiguousarray(a.reshape(T, DC, 128).transpose(2, 1, 0))


def from_fm(a):
    T = a.shape[2]
    return np.ascontiguousarray(a.transpose(2, 1, 0).reshape(T, D))


def run_ffn(xs, inp, mods, layer, which):
    global _FFN_NC
    if _FFN_NC is None:
        _FFN_NC = build_ffn()
    slot = 0 if which == 0 else 2
    modp = np.ascontiguousarray(mods[layer][:, 3 * slot:3 * slot + 3])
    gn = np.stack([inp['norm_pre'][layer, slot].reshape(DC, 128).T,
                   inp['norm_post'][layer, slot].reshape(DC, 128).T], axis=1)
    gn = np.ascontiguousarray(gn)
    wg = inp['ffn_w_gate'][layer, which]
    wu = inp['ffn_w_up'][layer, which]
    wd = inp['ffn_w_down'][layer, which]
    maps = [{"x": xs[k], "wg": wg, "wu": wu, "wd": wd, "modp": modp, "gn": gn} for k in range(NCORES)]
    res = _run(_FFN_NC, maps)
    return [r["xo"] for r in res]


CH = 64
NBLK = 10
LSCAN = CTX + SEQ


def scan_consts():
    su = np.triu(np.ones((CH, CH), np.float32), 1)
    ui = np.triu(np.ones((CH, CH), np.float32), 0)
    z = np.zeros((CH, CH), np.float32)
    bd = lambda m: np.block([[m, z], [z, m]])
    m12 = np.concatenate([bd(su), bd(ui)], axis=1)
    ml = bd(su.T.copy())
    idn = np.eye(128, dtype=np.float32)
    sm = np.ones((128, NBLK, CH), np.float32)
    sm[:, :, 0] = 0.0
    return {"m12": m12, "ml": ml, "idn": idn, "smask": sm.reshape(128, NBLK * CH)}


def build_scan(L=LSCAN, dbg=9):
    nch = L // CH
    assert nch % NBLK == 0
    nblk = nch // NBLK
    W = NBLK * CH
    c = Ctx("scan")
    p = c.p
    ch5 = [c.din(f"ch5_{d}", [128, 5, L]) for d in range(2)]
    vst = [c.din(f"vst_{d}", [nch, 128, CH]) for d in range(2)]
    yo = [c.dout(f"yo_{d}", [CH, nch, 128]) for d in range(2)]
    m12d = c.din("m12", [128, 256])
    mld = c.din("ml", [128, 128])
    idnd = c.din("idn", [128, 128])
    smd = c.din("smask", [128, W])
    c.init_psum(split=1)
    m12 = c.sb("m12s", [128, 256])
    ml = c.sb("mls", [128, 128])
    idn = c.sb("idns", [128, 128])
    sm = c.sb("sms", [128, W])
    c.dma(m12[:], m12d[:, :], writes=['m12'])
    c.dma(ml[:], mld[:, :], writes=['ml'])
    c.dma(idn[:], idnd[:, :], writes=['idn'])
    c.dma(sm[:], smd[:, :], writes=['sm'])
    NB2 = 2
    T = {}
    for d in range(2):
        for b in range(NB2):
            T[f'ch5{d}{b}'] = c.sb(f"ch5s{d}{b}", [128, 5, W])
            T[f'v{d}{b}'] = c.sb(f"vs{d}{b}", [128, NBLK, CH])
            T[f'g{d}{b}'] = c.sb(f"g{d}{b}", [128, W])
            T[f'kr{d}{b}'] = c.sb(f"kr{d}{b}", [128, NBLK, 2, 128])
            T[f'kig{d}{b}'] = c.sb(f"kig{d}{b}", [128, NBLK, 128])
            T[f'big{d}{b}'] = c.sb(f"big{d}{b}", [128, NBLK, 128])
            for nm in ('kr', 'kig', 'big'):
                t = T[f'{nm}{d}{b}']
                p.op('pool', lambda e, t=t: e.memset(t[:], 0.0), writes=[f'{nm}{d}{b}'])
        T[f'cl{d}'] = c.sb(f"cl{d}", [128, W])
        T[f'ig{d}'] = c.sb(f"ig{d}", [128, W])
        T[f'gm{d}'] = c.sb(f"gm{d}", [128, W])
        T[f'st{d}'] = c.sb(f"st{d}", [128, CH])
        p.op('pool', lambda e, d=d: e.memset(T[f'st{d}'][:], 0.0), writes=[f'st{d}'])
        for b in range(2):
            T[f'kigB{d}{b}'] = c.sb(f"kigB{d}{b}", [128, 128])
            T[f'bigB{d}{b}'] = c.sb(f"bigB{d}{b}", [128, 128])
            T[f'a1{d}{b}'] = c.sb(f"a1_{d}{b}", [128, 256])
            T[f'a2{d}{b}'] = c.sb(f"a2_{d}{b}", [128, 256])
            T[f'P{d}{b}'] = c.sb(f"P{d}{b}", [128, 128])
            T[f'X{d}{b}'] = c.sb(f"X{d}{b}", [128, 128])
            T[f'XT{d}{b}'] = c.sb(f"XT{d}{b}", [128, 128])
            T[f'yo{d}{b}'] = c.sb(f"yos{d}{b}", [CH, 128])
            T[f'Pf{d}{b}'] = c.sb(f"Pf{d}{b}", [128, 128])
        T[f'rhs{d}'] = c.sb(f"rhs{d}", [128, CH])
        T[f'nu{d}'] = c.sb(f"nu{d}", [128, CH])
    cnt = {'ev': 0}

    def evac(out, in_, reads, writes):
        cnt['ev'] += 1
        if cnt['ev'] % 2:
            p.op('act', lambda e: e.copy(out=out, in_=in_), reads=reads, writes=writes)
        else:
            p.op('dve', lambda e: e.tensor_copy(out=out, in_=in_), reads=reads, writes=writes)

    def prep(d, blk):
        b = blk % NB2
        c0 = blk * W
        t5 = T[f'ch5{d}{b}']
        k5 = f'ch5{d}{b}'
        c.dma(t5[:], ch5[d][:, :, c0:c0 + W], writes=[k5])
        c.dma(T[f'v{d}{b}'][:], vst[d][blk * NBLK:(blk + 1) * NBLK].rearrange("c p v -> p c v"), writes=[f'v{d}{b}'])
        cl, g, ig, gm = T[f'cl{d}'], T[f'g{d}{b}'], T[f'ig{d}'], T[f'gm{d}']
        p.op('dve', lambda e: e.tensor_tensor_scan(out=cl[:], data0=sm[:], data1=t5[:, 0, :], initial=0.0, op0=ALU.mult, op1=ALU.add),
             reads=[k5, 'sm'], writes=[f'cl{d}'])
        p.op('act', lambda e: e.activation(out=g[:], in_=cl[:], func=AF.Exp), reads=[f'cl{d}'], writes=[f'g{d}{b}'])
        p.op('act', lambda e: e.activation(out=ig[:], in_=cl[:], func=AF.Exp, scale=-1.0), reads=[f'cl{d}'], writes=[f'ig{d}'])
        p.op('dve', lambda e: e.tensor_tensor(out=gm[:], in0=cl[:], in1=t5[:, 0, :], op=ALU.subtract), reads=[f'cl{d}', k5], writes=[f'gm{d}'])
        p.op('act', lambda e: e.activation(out=gm[:], in_=gm[:], func=AF.Exp), reads=[f'gm{d}'], writes=[f'gm{d}'])
        kr, kig, big = T[f'kr{d}{b}'], T[f'kig{d}{b}'], T[f'big{d}{b}']
        i = 0
        for h in range(2):
            ps_ = slice(64 * h, 64 * h + 64)
            v3 = lambda ap: ap.rearrange("p (c t) -> p c t", t=CH)
            jobs = [
                (kr[ps_, :, 0, 64 * h:64 * h + 64], v3(t5[ps_, 3, :]), v3(gm[ps_, :]), [k5, f'gm{d}'], f'kr{d}{b}'),
                (kr[ps_, :, 1, 64 * h:64 * h + 64], v3(t5[ps_, 4, :]), v3(g[ps_, :]), [k5, f'g{d}{b}'], f'kr{d}{b}'),
                (kig[ps_, :, 64 * h:64 * h + 64], v3(t5[ps_, 1, :]), v3(ig[ps_, :]), [k5, f'ig{d}'], f'kig{d}{b}'),
                (big[ps_, :, 64 * h:64 * h + 64], v3(t5[ps_, 2, :]), v3(ig[ps_, :]), [k5, f'ig{d}'], f'big{d}{b}'),
            ]
            for (o, a, bb, rd, wk) in jobs:
                eng = 'pool' if i % 2 == 0 else 'dve'
                i += 1
                p.op(eng, lambda e, o=o, a=a, bb=bb: e.tensor_tensor(out=o, in0=a, in1=bb, op=ALU.mult), reads=rd, writes=[wk])

    def chunk_pre_stages(d, blk, ci):
        b = blk % NB2
        q = ci % 2
        kr, kig, big = T[f'kr{d}{b}'], T[f'kig{d}{b}'], T[f'big{d}{b}']
        kkr, kkig, kbig = f'kr{d}{b}', f'kig{d}{b}', f'big{d}{b}'
        kigB, bigB, a1, a2 = T[f'kigB{d}{q}'], T[f'bigB{d}{q}'], T[f'a1{d}{q}'], T[f'a2{d}{q}']
        nk = lambda s: f'{s}{d}{q}'
        P = [T[f'P{d}0'], T[f'P{d}1']]
        X = [T[f'X{d}0'], T[f'X{d}1']]
        XT = [T[f'XT{d}0'], T[f'XT{d}1']]
        stages = []

        def s0():
            for (src, ksrc, dst, kd) in ((kig, kkig, kigB, nk('kigB')), (big, kbig, bigB, nk('bigB'))):
                t, _, _, key = c.ps()
                p.op('pe', lambda e, t=t, src=src: e.transpose(t[:, 0:128], src[:, ci, :], idn[:]), reads=[ksrc, 'idn'], writes=[key])
                evac(dst[:], t[:, 0:128], [key], [kd])
            for (src, ksrc, dst, kd) in ((kig, kkig, a1, nk('a1')), (big, kbig, a2, nk('a2'))):
                t, _, _, key = c.ps()
                p.op('pe', lambda e, t=t, src=src: e.matmul(t[:, 0:256], lhsT=src[:, ci, :], rhs=kr[:, ci, :, :].rearrange("p a b -> p (a b)"),
                                                            start=True, stop=True), reads=[ksrc, kkr], writes=[key])
                p.op('dve', lambda e, t=t, dst=dst: e.tensor_tensor(out=dst[:], in0=t[:, 0:256], in1=m12[:], op=ALU.mult),
                     reads=[key, 'm12'], writes=[kd])
            t, _, _, key = c.ps()
            p.op('pe', lambda e, t=t: e.matmul(t[:, 0:128], lhsT=kr[:, ci, 0, :], rhs=big[:, ci, :], start=True, stop=True),
                 reads=[kkr, kbig], writes=[key])
            p.op('dve', lambda e, t=t: e.tensor_tensor(out=XT[0][:], in0=t[:, 0:128], in1=ml[:], op=ALU.mult),
                 reads=[key, 'ml'], writes=[f'XT{d}0'])
            p.op('pool', lambda e: e.tensor_copy(out=X[0][:], in_=a2[:, 0:128]), reads=[nk('a2')], writes=[f'X{d}0'])
            p.op('pool', lambda e: e.tensor_tensor(out=P[0][:], in0=idn[:], in1=a2[:, 0:128], op=ALU.subtract),
                 reads=[nk('a2'), 'idn'], writes=[f'P{d}0'])
        stages.append(s0)

        def mkstep(k):
            def s():
                a, bn = k % 2, (k + 1) % 2
                if k < 4:
                    t, _, _, key = c.ps()
                    p.op('pe', lambda e, t=t: e.matmul(t[:, 0:128], lhsT=XT[a][:], rhs=X[a][:], start=True, stop=True),
                         reads=[f'XT{d}{a}', f'X{d}{a}'], writes=[key])
                    evac(X[bn][:], t[:, 0:128], [key], [f'X{d}{bn}'])
                t2, _, _, key2 = c.ps()
                p.op('pe', lambda e, t2=t2: e.matmul(t2[:, 0:128], lhsT=X[a][:], rhs=XT[a][:], start=True, stop=True),
                     reads=[f'XT{d}{a}', f'X{d}{a}'], writes=[key2])
                evac(XT[bn][:], t2[:, 0:128], [key2], [f'XT{d}{bn}'])
                t3, _, _, key3 = c.ps()
                p.op('pe', lambda e, t3=t3: e.matmul(t3[:, 0:128], lhsT=XT[bn][:], rhs=P[a][:], start=True, stop=True),
                     reads=[f'XT{d}{bn}', f'P{d}{a}'], writes=[key3])
                pdst, pkey = (T[f'Pf{d}{q}'], f'Pf{d}{q}') if k == 4 else (P[bn], f'P{d}{bn}')
                p.op('dve', lambda e, t3=t3: e.tensor_tensor(out=pdst[:], in0=t3[:, 0:128], in1=P[a][:], op=ALU.add),
                     reads=[key3, f'P{d}{a}'], writes=[pkey])
            return s
        for k in range(5):
            stages.append(mkstep(k))
        return stages

    def chunk_seq_stages(d, blk, ci):
        b = blk % NB2
        q = ci % 2
        gci = blk * NBLK + ci
        kr, g = T[f'kr{d}{b}'], T[f'g{d}{b}']
        kkr = f'kr{d}{b}'
        v = T[f'v{d}{b}']
        kv = f'v{d}{b}'
        kigB, bigB, a1, a2 = T[f'kigB{d}{q}'], T[f'bigB{d}{q}'], T[f'a1{d}{q}'], T[f'a2{d}{q}']
        nk = lambda s: f'{s}{d}{q}'
        Pf = T[f'Pf{d}{q}']
        st, rhs, nu, yos = T[f'st{d}'], T[f'rhs{d}'], T[f'nu{d}'], T[f'yo{d}{q}']
        kst, krhs, knu, kyo = f'st{d}', f'rhs{d}', f'nu{d}', f'yo{d}{q}'

        def s_rhs():
            t, _, _, key = c.ps()
            p.op('pe', lambda e: e.matmul(t[:, 0:CH], lhsT=a1[:, 0:128], rhs=v[:, ci, :], start=True, stop=False),
                 reads=[nk('a1'), kv], writes=[key])
            p.op('pe', lambda e: e.matmul(t[:, 0:CH], lhsT=kr[:, ci, 0, :], rhs=st[:], start=False, stop=True),
                 reads=[kkr, kst], writes=[key])
            p.op('dve', lambda e: e.tensor_copy(out=rhs[:], in_=t[:, 0:CH]), reads=[key], writes=[krhs])

        def s_u():
            t, _, _, key = c.ps()
            p.op('pe', lambda e: e.matmul(t[:, 0:CH], lhsT=Pf[:], rhs=rhs[:], start=True, stop=True),
                 reads=[f'Pf{d}{q}', krhs], writes=[key])
            p.op('act', lambda e: e.mul(out=nu[:], in_=t[:, 0:CH], mul=-1.0), reads=[key], writes=[knu])

        def s_y():
            t, _, _, key = c.ps()
            p.op('pe', lambda e: e.matmul(t[0:CH, 0:128], lhsT=st[:], rhs=kr[:, ci, 1, :], start=True, stop=False),
                 reads=[kst, kkr], writes=[key])
            p.op('pe', lambda e: e.matmul(t[0:CH, 0:128], lhsT=v[:, ci, :], rhs=a1[:, 128:256], start=False, stop=False),
                 reads=[kv, nk('a1')], writes=[key])
            p.op('pe', lambda e: e.matmul(t[0:CH, 0:128], lhsT=nu[:], rhs=a2[:, 128:256], start=False, stop=True),
                 reads=[knu, nk('a2')], writes=[key])
            p.op('dve', lambda e: e.tensor_copy(out=yos[:], in_=t[0:CH, 0:128]), reads=[key], writes=[kyo])
            c.dma(yo[d][:, gci, :], yos[:], reads=[kyo], writes=[f'yout{d}'])

        def s_s():
            t, _, _, key = c.ps()
            p.op('pe', lambda e: e.matmul(t[:, 0:CH], lhsT=idn[:], rhs=st[:], start=True, stop=False),
                 reads=['idn', kst], writes=[key])
            p.op('pe', lambda e: e.matmul(t[:, 0:CH], lhsT=kigB[:], rhs=v[:, ci, :], start=False, stop=False),
                 reads=[nk('kigB'), kv], writes=[key])
            p.op('pe', lambda e: e.matmul(t[:, 0:CH], lhsT=bigB[:], rhs=nu[:], start=False, stop=True),
                 reads=[nk('bigB'), knu], writes=[key])
            col = ci * CH + CH - 1
            p.op('act', lambda e: e.activation(out=st[:], in_=t[:, 0:CH], func=AF.Identity, scale=g[:, col:col + 1]),
                 reads=[key, f'g{d}{b}'], writes=[kst])
        return [s_rhs, s_u, s_y, s_s]

    chunks = [(blk, ci) for blk in range(nblk) for ci in range(NBLK)]
    for d in range(2):
        prep(d, 0)
    first = [chunk_pre_stages(d, 0, 0) for d in range(2)]
    for k in range(len(first[0])):
        for d in range(2):
            first[d][k]()
    for idx, (blk, ci) in enumerate(chunks):
        lists = [chunk_seq_stages(d, blk, ci) for d in range(2)]
        if idx + 1 < len(chunks):
            nb, nci = chunks[idx + 1]
            if nci == 0:
                for d in range(2):
                    prep(d, nb)
            lists += [chunk_pre_stages(d, nb, nci) for d in range(2)]
        for k in range(max(len(l) for l in lists)):
            for l in lists:
                if k < len(l):
                    l[k]()
    return c.finish()


def mod_prologue(c, modp_d, gn_d, res_w):
    p = c.p
    mps = c.sb("mps", [128, 3, DC, 2])
    gns = c.sb("gns", [128, 2, DC])
    gs = c.sb("gs", [128, DC, 2])
    gp = c.sb("gp", [128, DC, 2])
    c.dma(mps[:], modp_d[:, :, :, :], writes=['mps'])
    c.dma(gns[:], gn_d[:, :, :], writes=['gns'])
    p.op('dve', lambda e: e.tensor_scalar(out=gs[:], in0=mps[:, 1, :, :], scalar1=1.0, scalar2=None, op0=ALU.add),
         reads=['mps'], writes=['gs'])
    p.op('dve', lambda e: e.tensor_tensor(out=gs[:], in0=gs[:], in1=gns[:, 0, :].unsqueeze(2).to_broadcast([128, DC, 2]), op=ALU.mult),
         reads=['gs', 'gns'], writes=['gs'])
    p.op('dve', lambda e: e.tensor_scalar(out=gp[:], in0=mps[:, 2, :, :], scalar1=float(res_w), scalar2=None, op0=ALU.mult),
         reads=['mps'], writes=['gp'])
    p.op('dve', lambda e: e.tensor_tensor(out=gp[:], in0=gp[:], in1=gns[:, 1, :].unsqueeze(2).to_broadcast([128, DC, 2]), op=ALU.mult),
         reads=['gp', 'gns'], writes=['gp'])
    c.mps, c.gs, c.gp = mps, gs, gp


def emit_pre(c, xt, xkey, n, var, sq, sqkey, rstd, rkey, tmp, tkey, hout, hkey):
    p = c.p
    rms_rstd(c, xt[:, :, 0:n], xkey, DC, n, D, sq, sqkey, rstd, rkey, tmp, tkey)
    p.op('dve', lambda e: e.tensor_tensor(out=sq[:, :, 0:n], in0=xt[:, :, 0:n],
                                          in1=rstd[:, 0:n].unsqueeze(1).to_broadcast([128, DC, n]), op=ALU.mult),
         reads=[xkey, rkey], writes=[sqkey])
    for kc in range(DC):
        p.op('act', lambda e, kc=kc: e.activation(out=hout[:, kc, 0:n], in_=sq[:, kc, 0:n], func=AF.Identity,
                                                  scale=c.gs[:, kc, var:var + 1], bias=c.mps[:, 0, kc, var:var + 1]),
             reads=[sqkey, 'gs', 'mps'], writes=[hkey])


def emit_post(c, ysb, ykey, xt, xkey, n, var, sq, sqkey, rstd, rkey, tmp, tkey, xoff=0):
    p = c.p
    rms_rstd(c, ysb[:, :, 0:n], ykey, DC, n, D, sq, sqkey, rstd, rkey, tmp, tkey)
    p.op('dve', lambda e: e.tensor_tensor(out=sq[:, :, 0:n], in0=ysb[:, :, 0:n],
                                          in1=rstd[:, 0:n].unsqueeze(1).to_broadcast([128, DC, n]), op=ALU.mult),
         reads=[ykey, rkey], writes=[sqkey])
    for kc in range(DC):
        p.op('dve', lambda e, kc=kc: e.scalar_tensor_tensor(
            out=xt[:, kc, xoff:xoff + n], in0=sq[:, kc, 0:n], scalar=c.gp[:, kc, var:var + 1], in1=xt[:, kc, xoff:xoff + n],
            op0=ALU.mult, op1=ALU.add), reads=[sqkey, 'gp', xkey], writes=[xkey])


def load_w_bf16(c, name, dram2d, rows, cols):
    kc_n = (rows + 127) // 128
    pr = min(rows, 128)
    t = c.sb(name, [pr, kc_n, cols], BF16)
    keys = []
    for kc in range(kc_n):
        r0 = kc * 128
        r1 = min(rows, r0 + 128)
        c.dma(t[0:r1 - r0, kc, :], dram2d[r0:r1, :], writes=[f'{name}{kc}'], q='pool')
        keys.append(f'{name}{kc}')
    return t, keys


class Rot:
    def __init__(self, c, name, shape, dt, n):
        self.t = [c.sb(f"{name}{i}", shape, dt) for i in range(n)]
        self.k = [f"{name}{i}" for i in range(n)]
        self.i = 0

    def get(self):
        j = self.i % len(self.t)
        self.i += 1
        return self.t[j], self.k[j]


def block_ones():
    z = np.zeros((128, 128), np.float32)
    z[:64, :64] = 1.0
    z[64:, 64:] = 1.0
    return z


NT_A = 256
SCAN_NAMES = ['ld0', 'ld1', 'kd0', 'kd1', 'b0', 'b1', 'kk', 'v', 'r', 'g', 'bon']


def build_rwa(vres):
    c = Ctx("rwa")
    p = c.p
    xh = c.din("xh", [128, DC, TX + 2])
    chh = c.din("chh", [128, DC, TCX + 2])
    hmask = c.din("hmask", [128, 4])
    modp = c.din("modp", [128, 3, DC, 2])
    gn = c.din("gn", [128, 2, DC])
    vecs_d = c.din("vecs", [128, 14, DC])
    bones_d = c.din("bones", [128, 128])
    wr_d, wk_d, wv_d = c.din("w_r", [D, D]), c.din("w_k", [D, D]), c.din("w_v", [D, D])
    w1_d, a1_d = c.din("w1", [2, D, 64]), c.din("a1", [2, D, 64])
    w2_d, a2_d = c.din("w2", [2, 64, D]), c.din("a2", [2, 64, D])
    g1_d, g2_d = c.din("g1", [D, 160]), c.din("g2", [160, D])
    if vres:
        v1_d, v2_d = c.din("v1", [D, 32]), c.din("v2", [32, D])
        vf_d = c.din("vf", [128, DC, NTOK])
    outs = {nm: c.dout(f"o_{nm}", [128, DC, NTOK]) for nm in SCAN_NAMES}
    c.init_psum(split=1)
    load_consts(c)
    mod_prologue(c, modp, gn, 1.0)
    vecs = c.sb("vecs", [128, 14, DC])
    c.dma(vecs[:], vecs_d[:, :, :], writes=['vecs'])
    bones = c.sb("bones", [128, 128])
    c.dma(bones[:], bones_d[:, :], writes=['bones'])
    hm = c.sb("hm", [128, 4])
    c.dma(hm[:], hmask[:, :], writes=['hm'])
    omka = c.sb("omka", [128, DC])
    p.op('dve', lambda e: e.tensor_scalar(out=omka[:], in0=vecs[:, 12, :], scalar1=-1.0, scalar2=1.0, op0=ALU.mult, op1=ALU.add),
         reads=['vecs'], writes=['omka'])
    wr, kwr = load_w_bf16(c, "wr", wr_d, D, D)
    wk, kwk = load_w_bf16(c, "wk", wk_d, D, D)
    wv, kwv = load_w_bf16(c, "wv", wv_d, D, D)
    w1 = [load_w_bf16(c, f"w1_{d}", w1_d[d], D, 64) for d in range(2)]
    a1 = [load_w_bf16(c, f"a1_{d}", a1_d[d], D, 64) for d in range(2)]
    w2 = [load_w_bf16(c, f"w2_{d}", w2_d[d], 64, D) for d in range(2)]
    a2 = [load_w_bf16(c, f"a2_{d}", a2_d[d], 64, D) for d in range(2)]
    g1 = load_w_bf16(c, "g1", g1_d, D, 160)
    g2a = load_w_bf16(c, "g2a", g2_d[0:128, :], 128, D)
    g2b = load_w_bf16(c, "g2b", g2_d[128:160, :], 32, D)
    if vres:
        v1 = load_w_bf16(c, "v1", v1_d, D, 32)
        v2 = load_w_bf16(c, "v2", v2_d, 32, D)
    NW = NT_A + 2
    xt = [c.sb(f"xt{b}", [128, DC, NW]) for b in range(2)]
    sq = c.sb("sq", [128, DC, NW])
    hh = c.sb("hh", [128, DC, NW])
    xx = c.sb("xx", [128, DC, NT_A])
    mix = [c.sb(f"mix{m}", [128, DC, NT_A], BF16) for m in range(6)]
    k32 = c.sb("k32", [128, DC, NT_A])
    v32 = c.sb("v32", [128, DC, NT_A])
    r32 = c.sb("r32", [128, DC, NT_A])
    rs = c.sb("rs", [128, NW])
    tmpa = c.sb("tmpa", [128, NW])
    tw = [c.sb(f"tw{d}", [64, NT_A], BF16) for d in range(2)]
    ta = [c.sb(f"ta{d}", [64, NT_A], BF16) for d in range(2)]
    tga = c.sb("tga", [128, NT_A], BF16)
    tgb = c.sb("tgb", [32, NT_A], BF16)
    tv = c.sb("tv", [32, NT_A], BF16)
    if vres:
        vft = c.sb("vft", [128, DC, NT_A])
    R = Rot(c, "sc", [128, NT_A], F32, 24)
    cnt = {'i': 0}

    def evac(out, in_, reads, writes):
        cnt['i'] += 1
        if cnt['i'] % 2:
            p.op('act', lambda e: e.copy(out=out, in_=in_), reads=reads, writes=writes)
        else:
            p.op('dve', lambda e: e.tensor_copy(out=out, in_=in_), reads=reads, writes=writes)

    tiles = [(xh, i * NT_A, NT_A, 0, i * NT_A, (0 if i == 0 else None, 1 if i == TX // NT_A - 1 else None)) for i in range(TX // NT_A)]
    tiles.append((chh, 0, TCX, 1, TX, (2, 3)))
    def do_tile(ti, src, col0, n, var, ocol, ml_, mr_):
        b = ti % 2
        xk = f'xt{b}'
        c.dma(xt[b][:, :, 0:n + 2], src[:, :, col0:col0 + n + 2], writes=[xk])
        emit_pre(c, xt[b], xk, n + 2, var, sq, 'sq', rs, 'rs', tmpa, 'tmpa', hh, 'hh')
        if ml_ is not None:
            p.op('dve', lambda e, ml_=ml_: e.tensor_scalar(out=hh[:, :, 0:1], in0=hh[:, :, 0:1], scalar1=hm[:, ml_:ml_ + 1], scalar2=None, op0=ALU.mult),
                 reads=['hh', 'hm'], writes=['hh'])
        if mr_ is not None:
            p.op('dve', lambda e, mr_=mr_: e.tensor_scalar(out=hh[:, :, n + 1:n + 2], in0=hh[:, :, n + 1:n + 2], scalar1=hm[:, mr_:mr_ + 1], scalar2=None, op0=ALU.mult),
                 reads=['hh', 'hm'], writes=['hh'])
        p.op('pool', lambda e: e.tensor_tensor(out=xx[:, :, 0:n], in0=hh[:, :, 0:n], in1=hh[:, :, 2:n + 2], op=ALU.add),
             reads=['hh'], writes=['xx'])
        p.op('dve', lambda e: e.scalar_tensor_tensor(out=xx[:, :, 0:n], in0=xx[:, :, 0:n], scalar=0.5, in1=hh[:, :, 1:n + 1],
                                                     op0=ALU.mult, op1=ALU.subtract), reads=['xx', 'hh'], writes=['xx'])
        for m in range(6):
            for kc in range(DC):
                p.op('dve', lambda e, m=m, kc=kc: e.scalar_tensor_tensor(
                    out=mix[m][:, kc, 0:n], in0=xx[:, kc, 0:n], scalar=vecs[:, m, kc:kc + 1], in1=hh[:, kc, 1:n + 1],
                    op0=ALU.mult, op1=ALU.add), reads=['xx', 'hh', 'vecs'], writes=[f'mix{m}'])
        if vres:
            c.dma(vft[:, :, 0:n], vf_d[:, :, ocol:ocol + n], writes=['vft'])
        def down(wt, mi, mcols, c0, dst, dkey, func):
            t, _, _, key = c.ps()
            for kc in range(DC):
                p.op('pe', lambda e, kc=kc: e.matmul(t[0:mcols, 0:n], lhsT=wt[0][:, kc, c0:c0 + mcols], rhs=mix[mi][:, kc, 0:n],
                                                     start=(kc == 0), stop=(kc == DC - 1)), reads=[wt[1][kc], f'mix{mi}'], writes=[key])
            p.op('act', lambda e: e.activation(out=dst[0:mcols, 0:n], in_=t[0:mcols, 0:n], func=func), reads=[key], writes=[dkey])
        for d in range(2):
            down(w1[d], 1, 64, 0, tw[d], f'tw{d}', AF.Tanh)
            down(a1[d], 4, 64, 0, ta[d], f'ta{d}', AF.Identity)
        down(g1, 5, 128, 0, tga, 'tga', AF.Sigmoid)
        down(g1, 5, 32, 128, tgb, 'tgb', AF.Sigmoid)
        if vres:
            down(v1, 3, 32, 0, tv, 'tv', AF.Identity)
        for (wt, kw, mi, dst, dk) in ((wr, kwr, 0, r32, 'r32'), (wk, kwk, 2, k32, 'k32'), (wv, kwv, 3, v32, 'v32')):
            for oc in range(DC):
                t, _, _, key = c.ps()
                for kc in range(DC):
                    p.op('pe', lambda e, kc=kc, oc=oc, wt=wt, mi=mi, t=t: e.matmul(
                        t[:, 0:n], lhsT=wt[:, kc, oc * 128:(oc + 1) * 128], rhs=mix[mi][:, kc, 0:n],
                        start=(kc == 0), stop=(kc == DC - 1)), reads=[kw[kc], f'mix{mi}'], writes=[key])
                evac(dst[:, oc, 0:n], t[:, 0:n], [key], [f'{dk}{oc}'])
        def do_oc(oc):
            ocs = slice(oc * 128, (oc + 1) * 128)

            def emit_out(nm, tile, key):
                c.dma(outs[nm][:, oc, ocol:ocol + n], tile[:, 0:n], reads=[key], writes=[f'out_{nm}'])
            kkr, kkr_k = R.get()
            p.op('act', lambda e: e.activation(out=kkr[:, 0:n], in_=k32[:, oc, 0:n], func=AF.Identity, scale=vecs[:, 11, oc:oc + 1]),
                 reads=[f'k32{oc}', 'vecs'], writes=[kkr_k])
            sqk, sqk_k = R.get()
            p.op('pool', lambda e: e.tensor_tensor(out=sqk[:, 0:n], in0=kkr[:, 0:n], in1=kkr[:, 0:n], op=ALU.mult), reads=[kkr_k], writes=[sqk_k])
            t, _, _, key = c.ps()
            p.op('pe', lambda e, t=t: e.matmul(t[:, 0:n], lhsT=bones[:], rhs=sqk[:, 0:n], start=True, stop=True), reads=['bones', sqk_k], writes=[key])
            rn, rn_k = R.get()
            p.op('dve', lambda e, t=t: e.tensor_scalar(out=rn[:, 0:n], in0=t[:, 0:n], scalar1=1e-24, scalar2=None, op0=ALU.max), reads=[key], writes=[rn_k])
            p.op('act', lambda e: e.activation(out=rn[:, 0:n], in_=rn[:, 0:n], func=AF.Sqrt), reads=[rn_k], writes=[rn_k])
            p.op('dve', lambda e: e.reciprocal(out=rn[:, 0:n], in_=rn[:, 0:n]), reads=[rn_k], writes=[rn_k])
            kk, kk_k = R.get()
            p.op('dve', lambda e: e.tensor_tensor(out=kk[:, 0:n], in0=kkr[:, 0:n], in1=rn[:, 0:n], op=ALU.mult), reads=[kkr_k, rn_k], writes=[kk_k])
            emit_out('kk', kk, kk_k)
            if vres:
                t, _, _, key = c.ps()
                p.op('pe', lambda e, t=t: e.matmul(t[:, 0:n], lhsT=v2[0][:, 0, ocs], rhs=tv[:, 0:n], start=True, stop=True),
                     reads=[v2[1][0], 'tv'], writes=[key])
                sgv, sgv_k = R.get()
                p.op('act', lambda e, t=t: e.activation(out=sgv[:, 0:n], in_=t[:, 0:n], func=AF.Sigmoid, bias=vecs[:, 10, oc:oc + 1]),
                     reads=[key, 'vecs'], writes=[sgv_k])
                dv, dv_k = R.get()
                p.op('pool', lambda e: e.tensor_tensor(out=dv[:, 0:n], in0=vft[:, oc, 0:n], in1=v32[:, oc, 0:n], op=ALU.subtract),
                     reads=['vft', f'v32{oc}'], writes=[dv_k])
                p.op('pool', lambda e: e.tensor_tensor(out=dv[:, 0:n], in0=dv[:, 0:n], in1=sgv[:, 0:n], op=ALU.mult), reads=[dv_k, sgv_k], writes=[dv_k])
                p.op('pool', lambda e: e.tensor_tensor(out=v32[:, oc, 0:n], in0=v32[:, oc, 0:n], in1=dv[:, 0:n], op=ALU.add),
                     reads=[dv_k, f'v32{oc}'], writes=[f'v32{oc}'])
            c.dma(outs['v'][:, oc, ocol:ocol + n], v32[:, oc, 0:n], reads=[f'v32{oc}'], writes=['out_v'])
            c.dma(outs['r'][:, oc, ocol:ocol + n], r32[:, oc, 0:n], reads=[f'r32{oc}'], writes=['out_r'])
            t, _, _, key = c.ps()
            p.op('pe', lambda e, t=t: e.matmul(t[:, 0:n], lhsT=g2a[0][:, 0, ocs], rhs=tga[:, 0:n], start=True, stop=False),
                 reads=[g2a[1][0], 'tga'], writes=[key])
            p.op('pe', lambda e, t=t: e.matmul(t[:, 0:n], lhsT=g2b[0][:, 0, ocs], rhs=tgb[:, 0:n], start=False, stop=True),
                 reads=[g2b[1][0], 'tgb'], writes=[key])
            gt, gt_k = R.get()
            evac(gt[:, 0:n], t[:, 0:n], [key], [gt_k])
            emit_out('g', gt, gt_k)
            rrk, rrk_k = R.get()
            p.op('pool', lambda e: e.tensor_scalar(out=rrk[:, 0:n], in0=r32[:, oc, 0:n], scalar1=vecs[:, 13, oc:oc + 1], scalar2=None, op0=ALU.mult),
                 reads=[f'r32{oc}', 'vecs'], writes=[rrk_k])
            tb, _, _, keyb = c.ps()
            def do_dir(d):
                t, _, _, key = c.ps()
                p.op('pe', lambda e, t=t, d=d: e.matmul(t[:, 0:n], lhsT=w2[d][0][:, 0, ocs], rhs=tw[d][:, 0:n], start=True, stop=True),
                     reads=[w2[d][1][0], f'tw{d}'], writes=[key])
                ld, ld_k = R.get()
                p.op('act', lambda e, t=t, d=d, ld=ld: e.activation(out=ld[:, 0:n], in_=t[:, 0:n], func=AF.Sigmoid, bias=vecs[:, 6 + d, oc:oc + 1]),
                     reads=[key, 'vecs'], writes=[ld_k])
                p.op('pool', lambda e, ld=ld: e.tensor_scalar(out=ld[:, 0:n], in0=ld[:, 0:n], scalar1=-0.6065306597126334, scalar2=None, op0=ALU.mult),
                     reads=[ld_k], writes=[ld_k])
                emit_out(f'ld{d}', ld, ld_k)
                t2, _, _, key2 = c.ps()
                p.op('pe', lambda e, t2=t2, d=d: e.matmul(t2[:, 0:n], lhsT=a2[d][0][:, 0, ocs], rhs=ta[d][:, 0:n], start=True, stop=True),
                     reads=[a2[d][1][0], f'ta{d}'], writes=[key2])
                ad, ad_k = R.get()
                p.op('act', lambda e, t2=t2, d=d, ad=ad: e.activation(out=ad[:, 0:n], in_=t2[:, 0:n], func=AF.Sigmoid, bias=vecs[:, 8 + d, oc:oc + 1]),
                     reads=[key2, 'vecs'], writes=[ad_k])
                bd, bd_k = R.get()
                p.op('pool', lambda e, bd=bd, ad=ad: e.tensor_tensor(out=bd[:, 0:n], in0=kk[:, 0:n], in1=ad[:, 0:n], op=ALU.mult),
                     reads=[kk_k, ad_k], writes=[bd_k])
                emit_out(f'b{d}', bd, bd_k)
                kd, kd_k = R.get()
                p.op('dve', lambda e, kd=kd, ad=ad: e.tensor_scalar(out=kd[:, 0:n], in0=ad[:, 0:n], scalar1=vecs[:, 12, oc:oc + 1],
                                                                    scalar2=omka[:, oc:oc + 1], op0=ALU.mult, op1=ALU.add),
                     reads=[ad_k, 'vecs', 'omka'], writes=[kd_k])
                p.op('dve', lambda e, kd=kd: e.tensor_tensor(out=kd[:, 0:n], in0=kd[:, 0:n], in1=k32[:, oc, 0:n], op=ALU.mult),
                     reads=[kd_k, f'k32{oc}'], writes=[kd_k])
                emit_out(f'kd{d}', kd, kd_k)
                rk, rk_k = R.get()
                p.op('dve', lambda e, rk=rk, kd=kd: e.tensor_tensor(out=rk[:, 0:n], in0=rrk[:, 0:n], in1=kd[:, 0:n], op=ALU.mult),
                     reads=[rrk_k, kd_k], writes=[rk_k])
                p.op('pe', lambda e, rk=rk, d=d: e.matmul(tb[:, 0:n], lhsT=bones[:], rhs=rk[:, 0:n], start=(d == 0), stop=(d == 1)),
                     reads=['bones', rk_k], writes=[keyb])
            for d in range(2):
                do_dir(d)
            bon, bon_k = R.get()
            p.op('dve', lambda e: e.tensor_tensor(out=bon[:, 0:n], in0=tb[:, 0:n], in1=v32[:, oc, 0:n], op=ALU.mult),
                 reads=[keyb, f'v32{oc}'], writes=[bon_k])
            emit_out('bon', bon, bon_k)
        for oc in range(DC):
            do_oc(oc)

    for ti, (src, col0, n, var, ocol, (ml_, mr_)) in enumerate(tiles):
        do_tile(ti, src, col0, n, var, ocol, ml_, mr_)
    return c.finish()


def fm_vec(v):
    return np.ascontiguousarray(np.asarray(v).reshape(DC, 128).T)


def halo_cols(xs, width):
    xh, chh, hmask = [], [], []
    for k in range(NCORES):
        z = np.zeros((128, DC, width), np.float32)
        xl = xs[k - 1][:, :, TX - width:TX] if k > 0 else z
        xr = xs[k + 1][:, :, 0:width] if k < NCORES - 1 else z
        cl = xs[k - 1][:, :, NTOK - width:NTOK] if k > 0 else z
        cr = xs[k + 1][:, :, TX:TX + width] if k < NCORES - 1 else z
        xh.append(np.ascontiguousarray(np.concatenate([xl, xs[k][:, :, 0:TX], xr], axis=2)))
        chh.append(np.ascontiguousarray(np.concatenate([cl, xs[k][:, :, TX:NTOK], cr], axis=2)))
        m = np.zeros((128, 4), np.float32)
        m[:, 0] = m[:, 2] = 1.0 if k > 0 else 0.0
        m[:, 1] = m[:, 3] = 1.0 if k < NCORES - 1 else 0.0
        hmask.append(m)
    return xh, chh, hmask


_RWA_NC = {}


def run_rwa(xs, inp, mods, layer, j, vf):
    vres = j > 0
    if vres not in _RWA_NC:
        _RWA_NC[vres] = build_rwa(vres)
    xh, chh, hmask = halo_cols(xs, 1)
    modp = np.ascontiguousarray(mods[layer][:, 3:6])
    gn = np.ascontiguousarray(np.stack([fm_vec(inp['norm_pre'][layer, 1]), fm_vec(inp['norm_post'][layer, 1])], axis=1))
    v0 = inp['rw_v0'][j - 1] if vres else np.zeros(D, np.float32)
    vl = [inp['rw_mu'][j][m] for m in range(6)] + [inp['rw_w0'][j][0], inp['rw_w0'][j][1], inp['rw_a0'][j][0], inp['rw_a0'][j][1],
                                                    v0, inp['rw_k_k'][j], inp['rw_k_a'][j], inp['rw_r_k'][j].reshape(D)]
    vecs = np.ascontiguousarray(np.stack([fm_vec(v) for v in vl], axis=1))
    base = {"modp": modp, "gn": gn, "vecs": vecs, "bones": block_ones(),
            "w_r": inp['rw_w_r'][j], "w_k": inp['rw_w_k'][j], "w_v": inp['rw_w_v'][j],
            "w1": inp['rw_w1'][j], "a1": inp['rw_a1'][j], "w2": inp['rw_w2'][j], "a2": inp['rw_a2'][j],
            "g1": inp['rw_g1'][j], "g2": inp['rw_g2'][j]}
    if vres:
        base.update({"v1": inp['rw_v1'][j - 1], "v2": inp['rw_v2'][j - 1]})
    maps = []
    for k in range(NCORES):
        m = dict(base)
        m.update({"xh": xh[k], "chh": chh[k], "hmask": hmask[k]})
        if vres:
            m["vf"] = vf[k]
        maps.append(m)
    res = _run(_RWA_NC[vres], maps)
    return [{nm: r[f"o_{nm}"] for nm in SCAN_NAMES} for r in res]


NT_T = 256
GN_EPS = 64e-5
_TAIL_NC = {}


def build_tail(front):
    c = Ctx("tail")
    p = c.p
    xin = c.din("x", [128, DC, NTOK])
    xout = c.dout("xo", [128, DC, NTOK])
    wo_d = c.din("wo", [D, D])
    modp = c.din("modp", [128, 3, DC, 2])
    gn = c.din("gn", [128, 2, DC])
    if front == 'rwkv':
        yd = [c.din(f"y{d}", [128, DC, NTOK]) for d in range(2)]
        g_d = c.din("g", [128, DC, NTOK])
        bon_d = c.din("bon", [128, DC, NTOK])
        lnv_d = c.din("lnv", [128, 2, DC])
        b64_d = c.din("b64", [128, 128])
    else:
        z_d = c.din("z", [128, DC, NTOK])
    c.init_psum(split=1)
    load_consts(c)
    mod_prologue(c, modp, gn, 1.0)
    wo, kwo = load_w_bf16(c, "wo", wo_d, D, D)
    xt = [c.sb(f"xt{b}", [128, DC, NT_T]) for b in range(2)]
    zb = c.sb("zb", [128, DC, NT_T], BF16)
    ysb = c.sb("ysb", [128, DC, NT_T])
    sq = c.sb("sq", [128, DC, NT_T])
    rs = c.sb("rs", [128, NT_T])
    tmpa = c.sb("tmpa", [128, NT_T])
    if front == 'rwkv':
        yt = [c.sb(f"yt{d}", [128, DC, NT_T]) for d in range(2)]
        gt = c.sb("gt", [128, DC, NT_T])
        bt = c.sb("bt", [128, DC, NT_T])
        lnv = c.sb("lnv", [128, 2, DC])
        b64 = c.sb("b64", [128, 128])
        gne = c.sb("gne", [128, 1])
        c.dma(lnv[:], lnv_d[:, :, :], writes=['lnv'])
        c.dma(b64[:], b64_d[:, :], writes=['b64'])
        p.op('pool', lambda e: e.memset(gne[:], GN_EPS), writes=['gne'])
        R = Rot(c, "sc", [128, NT_T], F32, 12)
    else:
        zt = c.sb("zt", [128, DC, NT_T])
    tiles = [(i * NT_T, NT_T, 0) for i in range(TX // NT_T)] + [(TX, TCX, 1)]

    def do_tile(ti, col0, n, var):
        b = ti % 2
        xk = f'xt{b}'
        c.dma(xt[b][:, :, 0:n], xin[:, :, col0:col0 + n], writes=[xk])
        if front == 'rwkv':
            for d in range(2):
                c.dma(yt[d][:, :, 0:n], yd[d][:, :, col0:col0 + n], writes=[f'yt{d}'])
            c.dma(gt[:, :, 0:n], g_d[:, :, col0:col0 + n], writes=['gt'])
            c.dma(bt[:, :, 0:n], bon_d[:, :, col0:col0 + n], writes=['bt'])

            def do_oc(oc):
                acc, acc_k = R.get()

                def do_dir(d):
                    t, _, _, key = c.ps()
                    p.op('pe', lambda e: e.matmul(t[:, 0:n], lhsT=b64[:], rhs=yt[d][:, oc, 0:n], start=True, stop=True),
                         reads=['b64', f'yt{d}'], writes=[key])
                    yc, yc_k = R.get()
                    p.op('dve', lambda e: e.scalar_tensor_tensor(out=yc[:, 0:n], in0=t[:, 0:n], scalar=-1.0, in1=yt[d][:, oc, 0:n],
                                                                 op0=ALU.mult, op1=ALU.add),
                         reads=[f'yt{d}', key], writes=[yc_k])
                    s2, s2_k = R.get()
                    p.op('pool', lambda e: e.tensor_tensor(out=s2[:, 0:n], in0=yc[:, 0:n], in1=yc[:, 0:n], op=ALU.mult), reads=[yc_k], writes=[s2_k])
                    t2, _, _, key2 = c.ps()
                    p.op('pe', lambda e: e.matmul(t2[:, 0:n], lhsT=b64[:], rhs=s2[:, 0:n], start=True, stop=True),
                         reads=['b64', s2_k], writes=[key2])
                    p.op('act', lambda e: e.activation(out=s2[:, 0:n], in_=t2[:, 0:n], func=AF.Sqrt, bias=gne[:, 0:1]),
                         reads=[key2, 'gne'], writes=[s2_k])
                    p.op('dve', lambda e: e.reciprocal(out=s2[:, 0:n], in_=s2[:, 0:n]), reads=[s2_k], writes=[s2_k])
                    p.op('dve', lambda e: e.tensor_tensor(out=yc[:, 0:n], in0=yc[:, 0:n], in1=s2[:, 0:n], op=ALU.mult), reads=[yc_k, s2_k], writes=[yc_k])
                    if d == 0:
                        p.op('act', lambda e: e.activation(out=acc[:, 0:n], in_=yc[:, 0:n], func=AF.Identity,
                                                           scale=lnv[:, 0, oc:oc + 1], bias=lnv[:, 1, oc:oc + 1]),
                             reads=[yc_k, 'lnv'], writes=[acc_k])
                    else:
                        p.op('act', lambda e: e.activation(out=yc[:, 0:n], in_=yc[:, 0:n], func=AF.Identity,
                                                           scale=lnv[:, 0, oc:oc + 1], bias=lnv[:, 1, oc:oc + 1]),
                             reads=[yc_k, 'lnv'], writes=[yc_k])
                        p.op('pool', lambda e: e.tensor_tensor(out=acc[:, 0:n], in0=acc[:, 0:n], in1=yc[:, 0:n], op=ALU.add),
                             reads=[acc_k, yc_k], writes=[acc_k])
                for d in range(2):
                    do_dir(d)
                p.op('pool', lambda e: e.tensor_tensor(out=acc[:, 0:n], in0=acc[:, 0:n], in1=bt[:, oc, 0:n], op=ALU.add),
                     reads=[acc_k, 'bt'], writes=[acc_k])
                p.op('dve', lambda e: e.tensor_tensor(out=zb[:, oc, 0:n], in0=acc[:, 0:n], in1=gt[:, oc, 0:n], op=ALU.mult),
                     reads=[acc_k, 'gt'], writes=[f'zb{oc}'])
            for oc in range(DC):
                do_oc(oc)
        else:
            c.dma(zt[:, :, 0:n], z_d[:, :, col0:col0 + n], writes=['zt'])
            for oc in range(DC):
                if oc % 2:
                    p.op('act', lambda e, oc=oc: e.copy(out=zb[:, oc, 0:n], in_=zt[:, oc, 0:n]), reads=['zt'], writes=[f'zb{oc}'])
                else:
                    p.op('dve', lambda e, oc=oc: e.tensor_copy(out=zb[:, oc, 0:n], in_=zt[:, oc, 0:n]), reads=['zt'], writes=[f'zb{oc}'])
        for oc in range(DC):
            t, _, _, key = c.ps()
            for kc in range(DC):
                p.op('pe', lambda e, kc=kc, oc=oc, t=t: e.matmul(t[:, 0:n], lhsT=wo[:, kc, oc * 128:(oc + 1) * 128], rhs=zb[:, kc, 0:n],
                                                                 start=(kc == 0), stop=(kc == DC - 1)), reads=[kwo[kc], f'zb{kc}'], writes=[key])
            if oc % 2:
                p.op('act', lambda e, oc=oc, t=t: e.copy(out=ysb[:, oc, 0:n], in_=t[:, 0:n]), reads=[key], writes=[f'y{oc}'])
            else:
                p.op('dve', lambda e, oc=oc, t=t: e.tensor_copy(out=ysb[:, oc, 0:n], in_=t[:, 0:n]), reads=[key], writes=[f'y{oc}'])
        p.op('pool', lambda e: e.memset(tmpa[:, 0:1], 0.0), reads=[f'y{oc}' for oc in range(DC)], writes=['yall', 'tmpa'])
        emit_post(c, ysb, 'yall', xt[b], xk, n, var, sq, 'sq', rs, 'rs', tmpa, 'tmpa')
        c.dma(xout[:, :, col0:col0 + n], xt[b][:, :, 0:n], reads=[xk], writes=['xout'])

    for ti, (col0, n, var) in enumerate(tiles):
        do_tile(ti, col0, n, var)
    return c.finish()


def run_tail(front, xs, inp, mods, layer, wo, extra):
    if front not in _TAIL_NC:
        _TAIL_NC[front] = build_tail(front)
    modp = np.ascontiguousarray(mods[layer][:, 3:6])
    gn = np.ascontiguousarray(np.stack([fm_vec(inp['norm_pre'][layer, 1]), fm_vec(inp['norm_post'][layer, 1])], axis=1))
    maps = []
    for k in range(NCORES):
        m = {"x": xs[k], "wo": wo, "modp": modp, "gn": gn}
        for nm, v in extra.items():
            m[nm] = v[k] if isinstance(v, list) else v
        maps.append(m)
    res = _run(_TAIL_NC[front], maps)
    return [r["xo"] for r in res]


NT_B = 256
NH = 16
QR, KVR = 384, 256
_MLAP_NC = None


def rope_consts():
    def perm(base, size):
        m = np.zeros((size, size), np.float32)
        for r in range(32):
            i = r % 16
            if i < 8:
                m[base + r + 8, base + r] = -1.0
            else:
                m[base + r - 8, base + r] = 1.0
        return m
    return perm(64, 96), perm(0, 32)


def rope_tables():
    t = np.arange(SEQ)
    row = (t // 64).astype(np.float32)
    col = (t % 64).astype(np.float32)
    inv = (10000.0 ** (-np.arange(0, 16, 2, dtype=np.float32) / 16)).astype(np.float32)
    ang = np.zeros((32, SEQ), np.float32)
    for r in range(32):
        f = (r % 16) % 8
        ang[r] = (row if r < 16 else col) * inv[f]
    return np.cos(ang).astype(np.float32), np.sin(ang).astype(np.float32)


def build_mla_proj():
    c = Ctx("mlap")
    p = c.p
    xin = c.din("x", [128, DC, NTOK])
    modp = c.din("modp", [128, 3, DC, 2])
    gn = c.din("gn", [128, 2, DC])
    wdq_d = c.din("w_dq", [D, QR])
    wuq_d = c.din("w_uq", [QR, NH * 96])
    wdkv_d = c.din("w_dkv", [D, KVR + 32])
    wukv_d = c.din("w_ukv", [KVR, NH * 128])
    qn_d = c.din("qn", [128, 3])
    kvn_d = c.din("kvn", [128, 2])
    cos96_d, sin96_d = c.din("cos96", [96, NTOK]), c.din("sin96", [96, NTOK])
    cos32_d, sin32_d = c.din("cos32", [32, NTOK]), c.din("sin32", [32, NTOK])
    p96_d, p32_d = c.din("p96", [96, 96]), c.din("p32", [32, 32])
    q_o = c.dout("q_o", [NH, 96, NTOK], BF16)
    kr_o = c.dout("kr_o", [32, NTOK], BF16)
    kv_o = c.dout("kv_o", [NH, 128, NTOK], BF16)
    c.init_psum(split=1)
    load_consts(c)
    mod_prologue(c, modp, gn, 1.0)
    wdq, kwdq = load_w_bf16(c, "wdq", wdq_d, D, QR)
    wuq, kwuq = load_w_bf16(c, "wuq", wuq_d, QR, NH * 96)
    wdkv, kwdkv = load_w_bf16(c, "wdkv", wdkv_d, D, KVR + 32)
    wukv, kwukv = load_w_bf16(c, "wukv", wukv_d, KVR, NH * 128)
    qn = c.sb("qn", [128, 3])
    kvn = c.sb("kvn", [128, 2])
    p96 = c.sb("p96", [96, 96])
    p32 = c.sb("p32", [32, 32])
    c.dma(qn[:], qn_d[:, :], writes=['qn'])
    c.dma(kvn[:], kvn_d[:, :], writes=['kvn'])
    c.dma(p96[:], p96_d[:, :], writes=['p96'])
    c.dma(p32[:], p32_d[:, :], writes=['p32'])
    xt = [c.sb(f"xt{b}", [128, DC, NT_B]) for b in range(2)]
    sq = c.sb("sq", [128, DC, NT_B])
    hb = c.sb("hb", [128, DC, NT_B], BF16)
    rs = c.sb("rs", [128, NT_B])
    tmpa = c.sb("tmpa", [128, NT_B])
    cq = c.sb("cq", [128, 3, NT_B])
    cqn = c.sb("cqn", [128, 3, NT_B], BF16)
    lat = c.sb("lat", [128, 2, NT_B])
    latn = c.sb("latn", [128, 2, NT_B], BF16)
    kr32 = c.sb("kr32", [32, NT_B])
    cs96 = [c.sb(f"cs96_{i}", [96, NT_B]) for i in range(2)]
    cs32 = [c.sb(f"cs32_{i}", [32, NT_B]) for i in range(2)]
    Rq = Rot(c, "q32", [96, NT_B], F32, 3)
    Rt = Rot(c, "qt", [96, NT_B], F32, 4)
    Rqo = Rot(c, "qo", [96, NT_B], BF16, 3)
    Rkv = Rot(c, "kvo", [128, NT_B], BF16, 3)
    kro = c.sb("kro", [32, NT_B], BF16)
    tiles = [(i * NT_B, NT_B, 0) for i in range(TX // NT_B)] + [(TX, TCX, 1)]

    def proj(wt, kw, kcn, c0, mcols, rhs, rkey, n):
        t, _, _, key = c.ps()
        for kc in range(kcn):
            p.op('pe', lambda e, kc=kc: e.matmul(t[0:mcols, 0:n], lhsT=wt[:, kc, c0:c0 + mcols], rhs=rhs[:, kc, 0:n],
                                                 start=(kc == 0), stop=(kc == kcn - 1)), reads=[kw[kc], rkey], writes=[key])
        return t, key

    def rope(src, skey, perm, pkey, rows, cos, sin, ckey, dst, dkey, n):
        t, _, _, key = c.ps()
        p.op('pe', lambda e: e.matmul(t[0:rows, 0:n], lhsT=perm[:], rhs=src[0:rows, 0:n], start=True, stop=True), reads=[pkey, skey], writes=[key])
        t1, t1k = Rt.get()
        t2, t2k = Rt.get()
        p.op('pool', lambda e: e.tensor_tensor(out=t1[0:rows, 0:n], in0=src[0:rows, 0:n], in1=cos[0:rows, 0:n], op=ALU.mult), reads=[skey, ckey], writes=[t1k])
        p.op('dve', lambda e: e.tensor_tensor(out=t2[0:rows, 0:n], in0=t[0:rows, 0:n], in1=sin[0:rows, 0:n], op=ALU.mult), reads=[key, ckey], writes=[t2k])
        p.op('dve', lambda e: e.tensor_tensor(out=dst[0:rows, 0:n], in0=t1[0:rows, 0:n], in1=t2[0:rows, 0:n], op=ALU.add), reads=[t1k, t2k], writes=[dkey])

    def do_tile(ti, col0, n, var):
        b = ti % 2
        xk = f'xt{b}'
        c.dma(xt[b][:, :, 0:n], xin[:, :, col0:col0 + n], writes=[xk])
        c.dma(cs96[0][:, 0:n], cos96_d[:, col0:col0 + n], writes=['cs96'])
        c.dma(cs96[1][:, 0:n], sin96_d[:, col0:col0 + n], writes=['cs96'])
        c.dma(cs32[0][:, 0:n], cos32_d[:, col0:col0 + n], writes=['cs32'])
        c.dma(cs32[1][:, 0:n], sin32_d[:, col0:col0 + n], writes=['cs32'])
        emit_pre(c, xt[b], xk, n, var, sq, 'sq', rs, 'rs', tmpa, 'tmpa', hb, 'hb')
        for qc in range(3):
            t, key = proj(wdq, kwdq, DC, qc * 128, 128, hb, 'hb', n)
            p.op('act', lambda e, qc=qc, t=t: e.copy(out=cq[:, qc, 0:n], in_=t[:, 0:n]), reads=[key], writes=['cq'])
        rms_rstd(c, cq[:, :, 0:n], 'cq', 3, n, QR, sq, 'sq', rs, 'rs', tmpa, 'tmpa')
        for qc in range(3):
            p.op('dve', lambda e, qc=qc: e.scalar_tensor_tensor(out=cqn[:, qc, 0:n], in0=cq[:, qc, 0:n], scalar=qn[:, qc:qc + 1], in1=rs[:, 0:n],
                                                                op0=ALU.mult, op1=ALU.mult), reads=['cq', 'qn', 'rs'], writes=['cqn'])
        for lc in range(2):
            t, key = proj(wdkv, kwdkv, DC, lc * 128, 128, hb, 'hb', n)
            p.op('act', lambda e, lc=lc, t=t: e.copy(out=lat[:, lc, 0:n], in_=t[:, 0:n]), reads=[key], writes=['lat'])
        t, key = proj(wdkv, kwdkv, DC, 256, 32, hb, 'hb', n)
        p.op('act', lambda e, t=t: e.copy(out=kr32[:, 0:n], in_=t[0:32, 0:n]), reads=[key], writes=['kr32'])
        rms_rstd(c, lat[:, :, 0:n], 'lat', 2, n, KVR, sq, 'sq', rs, 'rs', tmpa, 'tmpa')
        for lc in range(2):
            p.op('dve', lambda e, lc=lc: e.scalar_tensor_tensor(out=latn[:, lc, 0:n], in0=lat[:, lc, 0:n], scalar=kvn[:, lc:lc + 1], in1=rs[:, 0:n],
                                                                op0=ALU.mult, op1=ALU.mult), reads=['lat', 'kvn', 'rs'], writes=['latn'])
        rope(kr32, 'kr32', p32, 'p32', 32, cs32[0], cs32[1], 'cs32', kro, 'kro', n)
        c.dma(kr_o[:, col0:col0 + n], kro[:, 0:n], reads=['kro'], writes=['o_kr'])

        def do_head(hd):
            t, key = proj(wuq, kwuq, 3, hd * 96, 96, cqn, 'cqn', n)
            q32, q32k = Rq.get()
            p.op('act', lambda e: e.copy(out=q32[:, 0:n], in_=t[0:96, 0:n]), reads=[key], writes=[q32k])
            qo, qok = Rqo.get()
            rope(q32, q32k, p96, 'p96', 96, cs96[0], cs96[1], 'cs96', qo, qok, n)
            c.dma(q_o[hd, :, col0:col0 + n], qo[:, 0:n], reads=[qok], writes=['o_q'])
            t2, key2 = proj(wukv, kwukv, 2, hd * 128, 128, latn, 'latn', n)
            kvo, kvok = Rkv.get()
            p.op('act', lambda e: e.copy(out=kvo[:, 0:n], in_=t2[:, 0:n]), reads=[key2], writes=[kvok])
            c.dma(kv_o[hd, :, col0:col0 + n], kvo[:, 0:n], reads=[kvok], writes=['o_kv'])
        for hd in range(NH):
            do_head(hd)

    for ti, (col0, n, var) in enumerate(tiles):
        do_tile(ti, col0, n, var)
    return c.finish()


def run_mla_proj(xs, inp, mods, layer):
    global _MLAP_NC
    if _MLAP_NC is None:
        _MLAP_NC = build_mla_proj()
    modp = np.ascontiguousarray(mods[layer][:, 3:6])
    gn = np.ascontiguousarray(np.stack([fm_vec(inp['norm_pre'][layer, 1]), fm_vec(inp['norm_post'][layer, 1])], axis=1))
    p96, p32 = rope_consts()
    cos, sin = rope_tables()
    qn = np.ascontiguousarray(inp['mla_q_norm'][0].reshape(3, 128).T)
    kvn = np.ascontiguousarray(inp['mla_kv_norm'][0].reshape(2, 128).T)
    base = {"modp": modp, "gn": gn, "w_dq": inp['mla_w_dq'][0], "w_uq": inp['mla_w_uq'][0], "w_dkv": inp['mla_w_dkv'][0],
            "w_ukv": inp['mla_w_ukv'][0], "qn": qn, "kvn": kvn, "p96": p96, "p32": p32}
    maps = []
    for k in range(NCORES):
        c32 = np.ones((32, NTOK), np.float32)
        s32 = np.zeros((32, NTOK), np.float32)
        c32[:, 0:TX] = cos[:, k * TX:(k + 1) * TX]
        s32[:, 0:TX] = sin[:, k * TX:(k + 1) * TX]
        c96 = np.ones((96, NTOK), np.float32)
        s96 = np.zeros((96, NTOK), np.float32)
        c96[64:] = c32
        s96[64:] = s32
        m = dict(base)
        m.update({"x": xs[k], "cos96": c96, "sin96": s96, "cos32": c32, "sin32": s32})
        maps.append(m)
    res = _run(_MLAP_NC, maps)
    return [(r["q_o"], r["kr_o"], r["kv_o"]) for r in res]


LK = CTX + SEQ
NKT = LK // 128
ATT_SCALE = 96 ** -0.5
_ATT_NC = None


def build_attn():
    c = Ctx("attn")
    p = c.p
    q_d = c.din("q", [NH, 96, NTOK], BF16)
    k_d = c.din("k", [NH, 96, LK], BF16)
    v_d = c.din("va", [NH, 128, NKT, 65], BF16)
    sel_d = c.din("sel", [65, 64])
    o_d = c.dout("o", [NH, 64, NTOK])
    c.init_psum(split=1)
    sel = c.sb("sel", [65, 64])
    c.dma(sel[:], sel_d[:, :], writes=['sel'])
    kT = [c.sb(f"kT{b}", [96, LK], BF16) for b in range(2)]
    vv = [c.sb(f"vv{b}", [128, NKT, 65], BF16) for b in range(2)]
    qq = [c.sb(f"qq{b}", [96, NTOK], BF16) for b in range(2)]
    NP = 4
    P = [c.sb(f"P{i}", [128, 512], BF16) for i in range(NP)]
    osb = [c.sb(f"osb{i}", [65, 512]) for i in range(2)]
    rd = c.sb("rd", [64, 512])
    outt = [c.sb(f"outt{i}", [64, 512]) for i in range(2)]
    obanks = c.ps_bufs[0:2]
    sbanks = c.ps_bufs[2:8]
    st = {'s': 0, 'p': 0, 'o': 0, 'e': 0}
    qtiles = [(i * 512, 512, NKT) for i in range(TX // 512)] + [(TX, TCX, CTX // 128)]

    def do_head(h):
        b = h % 2
        c.dma(kT[b][:], k_d[h, :, :], writes=[f'kT{b}'])
        c.dma(vv[b][:], v_d[h, :, :, :], writes=[f'vv{b}'])
        c.dma(qq[b][:], q_d[h, :, :], writes=[f'qq{b}'])

        def do_qt(col0, n, nk):
            ot, _, _, okey = obanks[st['o'] % 2]
            st['o'] += 1
            sl = {}

            def S(kt):
                t, _, _, key = sbanks[st['s'] % len(sbanks)]
                st['s'] += 1
                p.op('pe', lambda e: e.matmul(t[:, 0:n], lhsT=kT[b][:, kt * 128:(kt + 1) * 128], rhs=qq[b][:, col0:col0 + n], start=True, stop=True),
                     reads=[f'kT{b}', f'qq{b}'], writes=[key])
                sl[kt] = (t, key)

            def EXP_PV(kt):
                t, key = sl.pop(kt)
                pi = st['p'] % NP
                st['p'] += 1
                p.op('act', lambda e: e.activation(out=P[pi][:, 0:n], in_=t[:, 0:n], func=AF.Exp, scale=ATT_SCALE), reads=[key], writes=[f'P{pi}'])
                p.op('pe', lambda e: e.matmul(ot[0:65, 0:n], lhsT=vv[b][:, kt, :], rhs=P[pi][:, 0:n], start=(kt == 0), stop=(kt == nk - 1)),
                     reads=[f'vv{b}', f'P{pi}'], writes=[okey])
            S(0)
            for kt in range(nk):
                if kt + 1 < nk:
                    S(kt + 1)
                EXP_PV(kt)
            ei = st['e'] % 2
            st['e'] += 1
            p.op('act', lambda e: e.copy(out=osb[ei][:, 0:n], in_=ot[0:65, 0:n]), reads=[okey], writes=[f'osb{ei}'])
            t, _, _, key = sbanks[st['s'] % len(sbanks)]
            st['s'] += 1
            p.op('pe', lambda e: e.matmul(t[0:64, 0:n], lhsT=sel[:], rhs=osb[ei][:, 0:n], start=True, stop=True), reads=['sel', f'osb{ei}'], writes=[key])
            p.op('dve', lambda e: e.reciprocal(out=rd[:, 0:n], in_=t[0:64, 0:n]), reads=[key], writes=['rd'])
            p.op('dve', lambda e: e.tensor_tensor(out=outt[ei][:, 0:n], in0=osb[ei][0:64, 0:n], in1=rd[:, 0:n], op=ALU.mult),
                 reads=[f'osb{ei}', 'rd'], writes=[f'outt{ei}'])
            c.dma(o_d[h, :, col0:col0 + n], outt[ei][:, 0:n], reads=[f'outt{ei}'], writes=['o_out'])
        for (col0, n, nk) in qtiles:
            do_qt(col0, n, nk)
    for h in range(NH):
        do_head(h)
    return c.finish()


def run_attn(proj_outs):
    global _ATT_NC
    if _ATT_NC is None:
        _ATT_NC = build_attn()
    bf = proj_outs[0][0].dtype
    kn = np.concatenate([po[2][:, 0:64, TX:NTOK] for po in proj_outs] + [po[2][:, 0:64, 0:TX] for po in proj_outs], axis=2)
    vv = np.concatenate([po[2][:, 64:128, TX:NTOK] for po in proj_outs] + [po[2][:, 64:128, 0:TX] for po in proj_outs], axis=2)
    kr = np.concatenate([po[1][:, TX:NTOK] for po in proj_outs] + [po[1][:, 0:TX] for po in proj_outs], axis=1)
    kT = np.ascontiguousarray(np.concatenate([kn, np.broadcast_to(kr[None], (NH, 32, LK))], axis=1))
    va = np.ones((NH, 128, NKT, 65), dtype=bf)
    va[:, :, :, 0:64] = vv.reshape(NH, 64, NKT, 128).transpose(0, 3, 2, 1)
    sel = np.zeros((65, 64), np.float32)
    sel[64, :] = 1.0
    maps = [{"q": np.ascontiguousarray(proj_outs[k][0]), "k": kT, "va": va, "sel": sel} for k in range(NCORES)]
    res = _run(_ATT_NC, maps)
    outs = []
    for r in res:
        o = r["o"].reshape(DC, 2, 64, NTOK).reshape(DC, 128, NTOK).transpose(1, 0, 2)
        outs.append(np.ascontiguousarray(o))
    return outs


NT_P = 256
HW = 8
_POOL_NC = None


def build_pool():
    c = Ctx("pool")
    p = c.p
    xh = c.din("xh", [128, DC, TX + 2 * HW])
    chh = c.din("chh", [128, DC, TCX + 2 * HW])
    xout = c.dout("xo", [128, DC, NTOK])
    hmask = c.din("hmask", [128, 4])
    modp = c.din("modp", [128, 3, DC, 2])
    gn = c.din("gn", [128, 2, DC])
    rc_d = c.din("rcnt", [128, 4, NTOK])
    pw_d = c.din("pw", [4, 256, 256])
    pv_d = c.din("pvec", [128, 2, DC])
    c.init_psum(split=1)
    load_consts(c)
    mod_prologue(c, modp, gn, 1.0)
    hm = c.sb("hm", [128, 4])
    c.dma(hm[:], hmask[:, :], writes=['hm'])
    pv = c.sb("pv", [128, 2, DC])
    c.dma(pv[:], pv_d[:, :, :], writes=['pv'])
    bsc = c.sb("bsc", [128, DC])
    p.op('dve', lambda e: e.tensor_tensor(out=bsc[:], in0=pv[:, 0, :], in1=pv[:, 1, :], op=ALU.mult), reads=['pv'], writes=['bsc'])
    pw = [load_w_bf16(c, f"pw{g}", pw_d[g], 256, 256) for g in range(4)]
    NW = NT_P + 2 * HW
    xt = [c.sb(f"xt{b}", [128, DC, NW]) for b in range(2)]
    sq = c.sb("sq", [128, DC, NW])
    hh = c.sb("hh", [128, DC, NW])
    s1 = c.sb("s1", [128, DC, NW])
    s4 = c.sb("s4", [128, DC, NW])
    s8 = c.sb("s8", [128, DC, NW])
    s16 = c.sb("s16", [128, DC, NW])
    rs = c.sb("rs", [128, NW])
    tmpa = c.sb("tmpa", [128, NW])
    rc = c.sb("rc", [128, 4, NT_P])
    mm = c.sb("mm", [128, DC, NT_P])
    df = c.sb("df", [128, DC, NT_P], BF16)
    ysb = c.sb("ysb", [128, DC, NT_P])
    tiles = [(xh, i * NT_P, NT_P, 0, i * NT_P, (0 if i == 0 else None, 1 if i == TX // NT_P - 1 else None)) for i in range(TX // NT_P)]
    tiles.append((chh, 0, TCX, 1, TX, (2, 3)))

    def do_tile(ti, src, col0, n, var, ocol, ml_, mr_):
        b = ti % 2
        xk = f'xt{b}'
        w = n + 2 * HW
        c.dma(xt[b][:, :, 0:w], src[:, :, col0:col0 + w], writes=[xk])
        c.dma(rc[:, :, 0:n], rc_d[:, :, ocol:ocol + n], writes=['rc'])
        emit_pre(c, xt[b], xk, w, var, sq, 'sq', rs, 'rs', tmpa, 'tmpa', hh, 'hh')
        if ml_ is not None:
            p.op('dve', lambda e: e.tensor_scalar(out=hh[:, :, 0:HW], in0=hh[:, :, 0:HW], scalar1=hm[:, ml_:ml_ + 1], scalar2=None, op0=ALU.mult),
                 reads=['hh', 'hm'], writes=['hh'])
        if mr_ is not None:
            p.op('dve', lambda e: e.tensor_scalar(out=hh[:, :, n + HW:w], in0=hh[:, :, n + HW:w], scalar1=hm[:, mr_:mr_ + 1], scalar2=None, op0=ALU.mult),
                 reads=['hh', 'hm'], writes=['hh'])
        p.op('pool', lambda e: e.tensor_tensor(out=s1[:, :, 1:w], in0=hh[:, :, 0:w - 1], in1=hh[:, :, 1:w], op=ALU.add), reads=['hh'], writes=['s1'])
        p.op('dve', lambda e: e.tensor_tensor(out=s4[:, 2:8, 2:w - 1], in0=s1[:, 2:8, 1:w - 2], in1=s1[:, 2:8, 3:w], op=ALU.add), reads=['s1'], writes=['s4'])
        p.op('pool', lambda e: e.tensor_tensor(out=s8[:, 4:8, 4:w - 3], in0=s4[:, 4:8, 2:w - 5], in1=s4[:, 4:8, 6:w - 1], op=ALU.add), reads=['s4'], writes=['s8'])
        p.op('dve', lambda e: e.tensor_tensor(out=s16[:, 6:8, 8:w - 7], in0=s8[:, 6:8, 4:w - 11], in1=s8[:, 6:8, 12:w - 3], op=ALU.add), reads=['s8'], writes=['s16'])
        for g, (sb_, sk) in enumerate(((s1, 's1'), (s4, 's4'), (s8, 's8'), (s16, 's16'))):
            cs = slice(2 * g, 2 * g + 2)
            p.op('dve', lambda e, g=g, sb_=sb_, cs=cs: e.tensor_tensor(out=mm[:, cs, 0:n], in0=sb_[:, cs, HW:HW + n],
                                                                      in1=rc[:, g, 0:n].unsqueeze(1).to_broadcast([128, 2, n]), op=ALU.mult),
                 reads=[sk, 'rc'], writes=[f'mm{g}'])
            p.op('pool', lambda e, g=g, cs=cs: e.tensor_tensor(out=df[:, cs, 0:n], in0=mm[:, cs, 0:n], in1=hh[:, cs, HW:HW + n], op=ALU.subtract),
                 reads=[f'mm{g}', 'hh'], writes=[f'df{g}'])
            for oo in range(2):
                oc = 2 * g + oo
                t, _, _, key = c.ps()
                for kk in range(2):
                    p.op('pe', lambda e, g=g, oo=oo, kk=kk, t=t: e.matmul(t[:, 0:n], lhsT=pw[g][0][:, kk, oo * 128:(oo + 1) * 128], rhs=df[:, 2 * g + kk, 0:n],
                                                                          start=(kk == 0), stop=(kk == 1)), reads=[pw[g][1][kk], f'df{g}'], writes=[key])
                p.op('act', lambda e, oc=oc, t=t: e.activation(out=ysb[:, oc, 0:n], in_=t[:, 0:n], func=AF.Identity,
                                                               scale=pv[:, 1, oc:oc + 1], bias=bsc[:, oc:oc + 1]),
                     reads=[key, 'pv', 'bsc'], writes=[f'y{oc}'])
        p.op('pool', lambda e: e.memset(tmpa[:, 0:1], 0.0), reads=[f'y{oc}' for oc in range(DC)], writes=['yall', 'tmpa'])
        emit_post(c, ysb, 'yall', xt[b], xk, n, var, sq, 'sq', rs, 'rs', tmpa, 'tmpa', xoff=HW)
        c.dma(xout[:, :, ocol:ocol + n], xt[b][:, :, HW:HW + n], reads=[xk], writes=['xout'])

    for ti, (src, col0, n, var, ocol, (ml_, mr_)) in enumerate(tiles):
        do_tile(ti, src, col0, n, var, ocol, ml_, mr_)
    return c.finish()


def pool_rcnt():
    out = []
    for T in (SEQ, CTX):
        t = np.arange(T)
        rows = []
        for win in (2, 4, 8, 16):
            lo = np.clip(t - win // 2, 0, T)
            hi = np.clip(t + win // 2, 0, T)
            rows.append((1.0 / (hi - lo).astype(np.float32)).astype(np.float32))
        out.append(np.stack(rows, axis=0))
    return out


def run_pool(xs, inp, mods, layer):
    global _POOL_NC
    if _POOL_NC is None:
        _POOL_NC = build_pool()
    xh, chh, hmask = halo_cols(xs, HW)
    modp = np.ascontiguousarray(mods[layer][:, 3:6])
    gn = np.ascontiguousarray(np.stack([fm_vec(inp['norm_pre'][layer, 1]), fm_vec(inp['norm_post'][layer, 1])], axis=1))
    rx, rcx = pool_rcnt()
    pvec = np.ascontiguousarray(np.stack([fm_vec(inp['pool_b'][0].reshape(D)), fm_vec(inp['pool_scale'][0])], axis=1))
    maps = []
    for k in range(NCORES):
        rc = np.concatenate([rx[:, k * TX:(k + 1) * TX], rcx[:, k * TCX:(k + 1) * TCX]], axis=1)
        rc = np.ascontiguousarray(np.broadcast_to(rc[None], (128, 4, NTOK)))
        maps.append({"xh": xh[k], "chh": chh[k], "hmask": hmask[k], "modp": modp, "gn": gn, "rcnt": rc,
                     "pw": inp['pool_w'][0], "pvec": pvec})
    res = _run(_POOL_NC, maps)
    return [r["xo"] for r in res]


_SCAN_NC = None


def _rev(a):
    return np.concatenate([a[..., 0:CTX][..., ::-1], a[..., CTX:][..., ::-1]], axis=-1)


def run_scan(pr):
    global _SCAN_NC
    if _SCAN_NC is None:
        _SCAN_NC = build_scan()
    nch = LSCAN // CH
    consts = scan_consts()

    def chan(nm, c):
        return np.concatenate([pr[k][nm][:, c, TX:NTOK] for k in range(NCORES)] + [pr[k][nm][:, c, 0:TX] for k in range(NCORES)], axis=1)
    maps = []
    for c in range(NCORES):
        kk, r, v = chan('kk', c), chan('r', c), chan('v', c)
        m = dict(consts)
        for d in range(2):
            arrs = [chan(f'ld{d}', c), chan(f'kd{d}', c), chan(f'b{d}', c), kk, r]
            vv = v
            if d == 1:
                arrs = [_rev(a) for a in arrs]
                vv = _rev(v)
            m[f"ch5_{d}"] = np.ascontiguousarray(np.stack(arrs, axis=1))
            m[f"vst_{d}"] = np.ascontiguousarray(vv.reshape(2, CH, nch, CH).transpose(2, 0, 3, 1).reshape(nch, 128, CH))
        maps.append(m)
    res = _run(_SCAN_NC, maps)
    ys = [[np.empty((128, DC, NTOK), np.float32) for _ in range(NCORES)] for _ in range(2)]
    for c in range(NCORES):
        for d in range(2):
            y = res[c][f"yo_{d}"].reshape(CH, nch, 2, CH).transpose(2, 0, 1, 3).reshape(128, LSCAN)
            if d == 1:
                y = _rev(y)
            for k in range(NCORES):
                ys[d][k][:, c, 0:TX] = y[:, CTX + k * TX:CTX + (k + 1) * TX]
                ys[d][k][:, c, TX:NTOK] = y[:, k * TCX:(k + 1) * TCX]
    return ys


def kernel(**inputs):
    inp = {k: np.asarray(v) for k, v in inputs.items()}
    mods = run_mod(inp)
    x = inp['x'][0]
    ctx = inp['ctx'][0]
    xs = [to_fm(np.concatenate([x[k * TX:(k + 1) * TX], ctx[k * TCX:(k + 1) * TCX]], axis=0)) for k in range(NCORES)]
    vf = None
    for i in range(4):
        kind, j = i % 3, i // 3
        xs = run_ffn(xs, inp, mods, i, 0)
        if kind == 0:
            pr = run_rwa(xs, inp, mods, i, j, vf)
            if j == 0:
                vf = [np.ascontiguousarray(q['v']) for q in pr]
            ys = run_scan(pr)
            lnv = np.ascontiguousarray(np.stack([fm_vec(inp['rw_ln_w'][j]), fm_vec(inp['rw_ln_b'][j])], axis=1))
            extra = {"y0": ys[0], "y1": ys[1], "g": [q['g'] for q in pr], "bon": [q['bon'] for q in pr],
                     "lnv": lnv, "b64": block_ones() / np.float32(64.0)}
            xs = run_tail('rwkv', xs, inp, mods, i, inp['rw_w_o'][j], extra)
        elif kind == 1:
            po = run_mla_proj(xs, inp, mods, i)
            o = run_attn(po)
            xs = run_tail('plain', xs, inp, mods, i, inp['mla_w_o'][0], {"z": o})
        else:
            xs = run_pool(xs, inp, mods, i)
        xs = run_ffn(xs, inp, mods, i, 1)
    out = np.concatenate([from_fm(xs[k][:, :, 0:TX]) for k in range(NCORES)], axis=0)
    return np.ascontiguousarray(out[None].astype(np.float32))
```
